# Optimizing a Trainium2 kernel written in Bass

```python
import math
import jax, jax.numpy as jnp
from jax import lax
import numpy as np

D_MODEL = 1024
BATCH = 2
SEQ = 16384
DEPTH = 2

N_A_LAYERS = DEPTH // 2
N_B_LAYERS = DEPTH - N_A_LAYERS
GDN_HEADS = D_MODEL // 128
GDN_DK = 128
GDN_DV = 128
CONV_K = 4
CHUNK = 64
GDN_HK = GDN_HEADS * GDN_DK
GDN_HV = GDN_HEADS * GDN_DV
GDN_IN = 2 * GDN_HK + 2 * GDN_HV + 2 * GDN_HEADS
DIFF_HEADS = D_MODEL // 256
DIFF_DH = 128
DIFF_QW = DIFF_HEADS * 2 * DIFF_DH
DIFF_VW = DIFF_HEADS * 2 * DIFF_DH
ROT_DIM = DIFF_DH // 4
ROPE_THETA = 500000.0
Q_BLOCK = 128
D_FF = 4 * D_MODEL
EPS = 1e-6

kernel_name = "yoco_gdn_diffattn_adaln_trunk"


def rmsnorm(x, g):
    xf = x.astype(jnp.float32)
    y = xf * lax.rsqrt(jnp.mean(xf * xf, axis=-1, keepdims=True) + EPS)
    return (y * g.astype(jnp.float32)).astype(x.dtype)


def modulate(x, g, shift, scale):
    return rmsnorm(x, g) * (1 + scale[:, None, :]) + shift[:, None, :]


def l2norm(x):
    xf = x.astype(jnp.float32)
    return (xf * lax.rsqrt(jnp.sum(xf * xf, axis=-1, keepdims=True) + EPS)).astype(x.dtype)


def causal_conv_silu(x, w):
    K = w.shape[0]
    S = x.shape[1]
    xp = jnp.pad(x, ((0, 0), (K - 1, 0), (0, 0)))
    y = xp[:, 0:S] * w[0]
    for j in range(1, K):
        y = y + xp[:, j:j + S] * w[j]
    return jax.nn.silu(y)


def gated_delta_rule(q, k, v, g, beta):
    f32 = jnp.float32
    B, S, H, DK = q.shape
    DV = v.shape[-1]
    N = S // CHUNK

    def chunks(t):
        return t.astype(f32).reshape(B, N, CHUNK, H, -1).transpose(0, 3, 1, 2, 4)

    qc = chunks(q) * (DK ** -0.5)
    kc = chunks(k)
    vc = chunks(v)
    gc = g.astype(f32).reshape(B, N, CHUNK, H).transpose(0, 3, 1, 2)
    bc = beta.astype(f32).reshape(B, N, CHUNK, H).transpose(0, 3, 1, 2)
    gcum = jnp.cumsum(gc, axis=-1)
    idx = jnp.arange(CHUNK)
    incl = idx[:, None] >= idx[None, :]
    strict = idx[:, None] > idx[None, :]
    gdiff = gcum[..., :, None] - gcum[..., None, :]
    decay = jnp.where(incl, jnp.exp(jnp.where(incl, gdiff, 0.0)), 0.0)
    kb = kc * bc[..., None]
    a_mat = jnp.where(strict, jnp.einsum('bhnik,bhnjk->bhnij', kb, kc) * decay, 0.0)
    eye = jnp.eye(CHUNK, dtype=f32)
    rhs = jnp.concatenate([vc * bc[..., None], kb * jnp.exp(gcum)[..., None]], axis=-1)
    sol = lax.linalg.triangular_solve(eye + a_mat, rhs, left_side=True, lower=True, unit_diagonal=True)
    u = sol[..., :DV]
    w = sol[..., DV:]
    qk = jnp.einsum('bhnik,bhnjk->bhnij', qc, kc) * decay
    qg = qc * jnp.exp(gcum)[..., None]
    kdec = kc * jnp.exp(gcum[..., -1:] - gcum)[..., None]
    glast = jnp.exp(gcum[..., -1])
    xs = tuple(jnp.moveaxis(t, 2, 0) for t in (qg, qk, u, w, kdec, glast))

    def step(state, inp):
        qg_i, qk_i, u_i, w_i, kd_i, gl_i = inp
        v_new = u_i - jnp.einsum('bhck,bhkv->bhcv', w_i, state)
        o_i = jnp.einsum('bhck,bhkv->bhcv', qg_i, state) + jnp.einsum('bhij,bhjv->bhiv', qk_i, v_new)
        state = state * gl_i[..., None, None] + jnp.einsum('bhck,bhcv->bhkv', kd_i, v_new)
        return state, o_i

    s0 = jnp.zeros((B, H, DK, DV), f32)
    _, o = lax.scan(step, s0, xs)
    return o.transpose(1, 0, 3, 2, 4).reshape(B, S, H, DV)


def gated_deltanet(h, w_in, conv_w, a_log, dt_bias, out_g, w_out):
    B, S, _ = h.shape
    proj = h @ w_in
    qkv = causal_conv_silu(proj[..., :2 * GDN_HK + GDN_HV], conv_w)
    q = qkv[..., :GDN_HK].reshape(B, S, GDN_HEADS, GDN_DK)
    k = qkv[..., GDN_HK:2 * GDN_HK].reshape(B, S, GDN_HEADS, GDN_DK)
    v = qkv[..., 2 * GDN_HK:].reshape(B, S, GDN_HEADS, GDN_DV)
    off = 2 * GDN_HK + GDN_HV
    z = proj[..., off:off + GDN_HV].reshape(B, S, GDN_HEADS, GDN_DV)
    a = proj[..., off + GDN_HV:off + GDN_HV + GDN_HEADS]
    b = proj[..., off + GDN_HV + GDN_HEADS:]
    g = -jnp.exp(a_log.astype(jnp.float32)) * jax.nn.softplus(a.astype(jnp.float32) + dt_bias.astype(jnp.float32))
    beta = jax.nn.sigmoid(b.astype(jnp.float32))
    o = gated_delta_rule(l2norm(q), l2norm(k), v, g, beta).astype(h.dtype)
    o = rmsnorm(o, out_g) * jax.nn.silu(z)
    return o.reshape(B, S, GDN_HV) @ w_out


def rope_tables(S):
    pos = jnp.arange(S, dtype=jnp.float32)
    inv_freq = ROPE_THETA ** (-jnp.arange(0, ROT_DIM, 2, dtype=jnp.float32) / ROT_DIM)
    freqs = pos[:, None] * inv_freq[None, :]
    return jnp.cos(freqs), jnp.sin(freqs)


def partial_rope(x, cos, sin):
    half = ROT_DIM // 2
    xf = x.astype(jnp.float32)
    x1 = xf[..., :half]
    x2 = xf[..., half:ROT_DIM]
    c = cos[None, :, None, None, :]
    s = sin[None, :, None, None, :]
    out = jnp.concatenate([x1 * c - x2 * s, x2 * c + x1 * s, xf[..., ROT_DIM:]], axis=-1)
    return out.astype(x.dtype)


def diff_attention(q, k, v, lam):
    B, S, H, _, DH = q.shape
    nb = S // Q_BLOCK
    qb = q.reshape(B, nb, Q_BLOCK, H, 2, DH).transpose(1, 0, 3, 4, 2, 5)
    kt = k.transpose(0, 2, 3, 1, 4)
    vt = v.transpose(0, 2, 1, 3)
    kpos = jnp.arange(S)
    scale = DH ** -0.5

    def block(args):
        i, qi = args
        s = jnp.einsum('bhmqd,bhmkd->bhmqk', qi, kt).astype(jnp.float32) * scale
        qpos = i * Q_BLOCK + jnp.arange(Q_BLOCK)
        s = jnp.where(kpos[None, :] <= qpos[:, None], s, -jnp.inf)
        p = jax.nn.softmax(s, axis=-1)
        a = p[:, :, 0] - lam * p[:, :, 1]
        return jnp.einsum('bhqk,bhkv->bhqv', a.astype(v.dtype), vt)

    o = lax.map(block, (jnp.arange(nb), qb))
    return o.transpose(1, 0, 3, 2, 4).reshape(B, S, H, 2 * DH)


def diff_attn_layer(h, k, v, w_q, lam_params, subln_g, w_out, lam_init, cos, sin):
    B, S, _ = h.shape
    q = (h @ w_q).reshape(B, S, DIFF_HEADS, 2, DIFF_DH)
    q = partial_rope(q, cos, sin)
    lp = lam_params.astype(jnp.float32)
    lam = jnp.exp(jnp.sum(lp[0] * lp[1])) - jnp.exp(jnp.sum(lp[2] * lp[3])) + lam_init
    o = diff_attention(q, k, v, lam)
    o = rmsnorm(o, subln_g) * (1.0 - lam_init)
    return o.reshape(B, S, DIFF_VW) @ w_out


def sqrelu_mlp(h, w1, w2):
    return jnp.square(jax.nn.relu(h @ w1)) @ w2


def setup_inputs(seed: int = 0) -> dict:
    key = jax.random.key(seed)
    ks = jax.random.split(key, 32)
    D = D_MODEL

    def nrm(k, shape, scale):
        return jax.random.normal(k, shape, jnp.float32) * scale

    dt = jnp.exp(jax.random.uniform(ks[8], (N_A_LAYERS, GDN_HEADS), jnp.float32) * (math.log(0.1) - math.log(0.001)) + math.log(0.001))
    return {
        "x": nrm(ks[0], (BATCH, SEQ, D), 1.0),
        "c": nrm(ks[1], (BATCH, D), 1.0),
        "mod_w": nrm(ks[2], (DEPTH, D, 6 * D), 0.5 * D ** -0.5),
        "mod_b": nrm(ks[3], (DEPTH, 6 * D), 0.02),
        "norm_mix_g": 1.0 + nrm(ks[4], (DEPTH, D), 0.02),
        "norm_mlp_g": 1.0 + nrm(ks[5], (DEPTH, D), 0.02),
        "a_w_in": nrm(ks[6], (N_A_LAYERS, D, GDN_IN), D ** -0.5),
        "a_conv_w": nrm(ks[7], (N_A_LAYERS, CONV_K, 2 * GDN_HK + GDN_HV), CONV_K ** -0.5),
        "a_log": jnp.log(jax.random.uniform(ks[9], (N_A_LAYERS, GDN_HEADS), jnp.float32, 1.0, 16.0)),
        "a_dt_bias": dt + jnp.log(-jnp.expm1(-dt)),
        "a_out_norm_g": 1.0 + nrm(ks[10], (N_A_LAYERS, GDN_DV), 0.02),
        "a_w_out": nrm(ks[11], (N_A_LAYERS, GDN_HV, D), GDN_HV ** -0.5),
        "kv_mod_w": nrm(ks[12], (D, 2 * D), 0.5 * D ** -0.5),
        "kv_mod_b": nrm(ks[13], (2 * D,), 0.02),
        "kv_norm_g": 1.0 + nrm(ks[14], (D,), 0.02),
        "kv_w": nrm(ks[15], (D, DIFF_HEADS * 2 * DIFF_DH + DIFF_VW), D ** -0.5),
        "b_w_q": nrm(ks[16], (N_B_LAYERS, D, DIFF_QW), D ** -0.5),
        "b_lambda": nrm(ks[17], (N_B_LAYERS, 4, DIFF_DH), 0.1),
        "b_subln_g": 1.0 + nrm(ks[18], (N_B_LAYERS, 2 * DIFF_DH), 0.02),
        "b_w_out": nrm(ks[19], (N_B_LAYERS, DIFF_VW, D), DIFF_VW ** -0.5),
        "mlp_w1": nrm(ks[20], (DEPTH, D, D_FF), D ** -0.5),
        "mlp_w2": nrm(ks[21], (DEPTH, D_FF, D), D_FF ** -0.5),
        "final_g": 1.0 + nrm(ks[22], (D,), 0.02),
    }


def reference(x, c, mod_w, mod_b, norm_mix_g, norm_mlp_g, a_w_in, a_conv_w, a_log, a_dt_bias, a_out_norm_g, a_w_out,
              kv_mod_w, kv_mod_b, kv_norm_g, kv_w, b_w_q, b_lambda, b_subln_g, b_w_out, mlp_w1, mlp_w2, final_g):
    B, S, D = x.shape
    cs = jax.nn.silu(c)
    cos, sin = rope_tables(S)
    k_sh = None
    v_sh = None
    for l in range(DEPTH):
        mod = cs @ mod_w[l] + mod_b[l]
        sh1, sc1, gt1, sh2, sc2, gt2 = jnp.split(mod, 6, axis=-1)
        if l < N_A_LAYERS:
            h = modulate(x, norm_mix_g[l], sh1, sc1)
            y = gated_deltanet(h, a_w_in[l], a_conv_w[l], a_log[l], a_dt_bias[l], a_out_norm_g[l], a_w_out[l])
        else:
            if l == N_A_LAYERS:
                kv_sh_shift, kv_sh_scale = jnp.split(cs @ kv_mod_w + kv_mod_b, 2, axis=-1)
                kv = modulate(x, kv_norm_g, kv_sh_shift, kv_sh_scale) @ kv_w
                k_sh = partial_rope(kv[..., :DIFF_HEADS * 2 * DIFF_DH].reshape(B, S, DIFF_HEADS, 2, DIFF_DH), cos, sin)
                v_sh = kv[..., DIFF_HEADS * 2 * DIFF_DH:].reshape(B, S, DIFF_HEADS, 2 * DIFF_DH)
            j = l - N_A_LAYERS
            lam_init = 0.8 - 0.6 * math.exp(-0.3 * l)
            h = modulate(x, norm_mix_g[l], sh1, sc1)
            y = diff_attn_layer(h, k_sh, v_sh, b_w_q[j], b_lambda[j], b_subln_g[j], b_w_out[j], lam_init, cos, sin)
        x = x + gt1[:, None, :] * y
        h = modulate(x, norm_mlp_g[l], sh2, sc2)
        x = x + gt2[:, None, :] * sqrelu_mlp(h, mlp_w1[l], mlp_w2[l])
    return rmsnorm(x, final_g)
```

```python
import contextlib
from collections import defaultdict
import numpy as np
import concourse.bass as bass
import concourse.mybir as mybir

F32 = mybir.dt.float32
BF16 = mybir.dt.bfloat16
U32 = mybir.dt.uint32
AF = mybir.ActivationFunctionType
ALU = mybir.AluOpType
AX = mybir.AxisListType

ENGS = ['sp', 'act', 'dve', 'pool', 'pe']


class Buf:
    def __init__(self, h, name, tr=None, multi=False):
        self.h = h
        self.name = name
        self.multi = multi
        self.tr = tr if tr is not None else {'lw': None, 'rd': {}, 'ws': {}}

    def __getitem__(self, k):
        return self.h[k]

    def view(self, pattern, **kw):
        return Buf(self.h[:].rearrange(pattern, **kw), self.name, self.tr, self.multi)


class Sched:
    def __init__(self, nc):
        self.nc = nc
        self.es = contextlib.ExitStack()
        self.ops = {e: [] for e in ENGS}
        self.val = defaultdict(int)
        self.known = {e: defaultdict(int) for e in ENGS}
        self.sem = {}
        for e in ENGS:
            self.sem['c_' + e] = self.es.enter_context(nc.semaphore('c_' + e))
        self.NS = 8
        self.ndma = defaultdict(int)
        for e in ['sp', 'act', 'pool']:
            for k in range(self.NS):
                nm = f'd_{e}_{k}'
                self.sem[nm] = self.es.enter_context(nc.semaphore(nm))
        self.nbuf = 0
        self.pes = None
        self.sem['cc'] = self.es.enter_context(nc.semaphore('cc'))

    def begin_phase(self):
        self.pes = contextlib.ExitStack()
        if hasattr(self, '_stage'):
            del self._stage

    def end_phase(self, final=False):
        allc = [(c, v) for c, v in self.val.items() if v > 0]
        for eng in ENGS:
            waits = []
            for c, v in allc:
                if self.known[eng][c] < v:
                    waits.append((c, v))
                    self.known[eng][c] = v
            self.ops[eng].append((waits, None, None, 0))
        self._emit_block()
        if self.pes is not None:
            self.pes.close()
            self.pes = None
        if final:
            self.es.close()

    def _stack(self):
        return self.pes if self.pes is not None else self.es

    def sb(self, name, shape, dt):
        self.nbuf += 1
        return Buf(self._stack().enter_context(self.nc.sbuf_tensor(f"{name}_{self.nbuf}", list(shape), dt)), name)

    def ps(self, name, shape, dt=F32):
        self.nbuf += 1
        return Buf(self._stack().enter_context(self.nc.psum_tensor(f"{name}_{self.nbuf}", list(shape), dt)), name)

    def dram(self, h, name):
        return Buf(h, name, multi=True)

    def op(self, eng, fn, reads=(), writes=(), dma=False, cc=False):
        deps = {}
        if cc:
            ctr = 'cc'
        elif dma:
            k = self.ndma[eng] % self.NS
            self.ndma[eng] += 1
            ctr = f'd_{eng}_{k}'
            if self.val[ctr] > 0:
                deps[ctr] = self.val[ctr]
        else:
            ctr = 'c_' + eng
        for b in reads:
            lw = b.tr['lw']
            if lw is not None:
                deps[lw[0]] = max(deps.get(lw[0], 0), lw[1])
            if b.multi:
                for c, v in b.tr['ws'].items():
                    deps[c] = max(deps.get(c, 0), v)
        for b in writes:
            if b.multi:
                continue
            lw = b.tr['lw']
            if lw is not None:
                deps[lw[0]] = max(deps.get(lw[0], 0), lw[1])
            for c, v in b.tr['rd'].items():
                deps[c] = max(deps.get(c, 0), v)
        waits = []
        for c, v in deps.items():
            if eng == 'pe' and c == 'c_pe':
                continue
            if self.known[eng][c] < v:
                waits.append((c, v))
                self.known[eng][c] = v
        inc = 1 if cc else (16 if dma else 1)
        self.val[ctr] += inc
        v = self.val[ctr]
        self.ops[eng].append((waits, fn, ctr, inc))
        for b in reads:
            b.tr['rd'][ctr] = max(b.tr['rd'].get(ctr, 0), v)
        for b in writes:
            if b.multi:
                b.tr['ws'][ctr] = max(b.tr['ws'].get(ctr, 0), v)
                continue
            b.tr['lw'] = (ctr, v)
            b.tr['rd'] = {}

    def I(self, eng, method, reads, writes, *a, **kw):
        self.op(eng, lambda e: getattr(e, method)(*a, **kw), reads, writes)

    def dma(self, eng, out_ap, in_ap, reads=(), writes=(), **kw):
        self.op(eng, lambda e: e.dma_start(out=out_ap, in_=in_ap, **kw), reads, writes, dma=True)

    def allgather(self, in_ap, out_ap, in_buf, out_buf, groups):
        self.op('pool', lambda e: e.collective_compute("AllGather", ALU.bypass, replica_groups=groups,
                                                       ins=[in_ap.opt()], outs=[out_ap.opt()]),
                reads=[in_buf], writes=[out_buf], cc=True)

    def gather(self, out_ap, in_view, idx_ap, reads, writes):
        self.op('pool', lambda e: e.indirect_dma_start(out=out_ap, out_offset=None, in_=in_view,
                                                       in_offset=bass.IndirectOffsetOnAxis(ap=idx_ap, axis=0)),
                reads, writes, dma=True)

    def _emit_block(self):
        nc = self.nc
        with nc.Block() as block:
            decos = {'sp': block.sync, 'act': block.scalar, 'dve': block.vector,
                     'pool': block.gpsimd, 'pe': block.tensor}
            for eng in ENGS:
                ops = self.ops[eng]

                def body(e, ops=ops):
                    for waits, fn, ctr, inc in ops:
                        for c, v in waits:
                            e.wait_ge(self.sem[c], v)
                        if fn is not None:
                            ins = fn(e)
                            if inc == 1 and ctr == 'cc':
                                ins.then_inc(self.sem[ctr])
                            else:
                                ins.then_inc(self.sem[ctr], inc)
                decos[eng](body)
        self.nops_total = getattr(self, 'nops_total', 0) + sum(len(v) for v in self.ops.values())
        self.ops = {e: [] for e in ENGS}

    def emit(self):
        self.end_phase(final=True)

    def n_ops(self):
        return {e: len(self.ops[e]) for e in ENGS}


EPS = 1e-6


def consts(s):
    c = {}
    c['ones_bf'] = s.sb('ones_bf', [128, 128], BF16)
    s.op('pool', lambda e: e.memset(c['ones_bf'][:], 1.0), writes=[c['ones_bf']])
    return c


def phase_mod(s, C, cT_d, ws, modbT_d):
    NJ = sum(w.shape[1] for w in ws) // 128
    cs = s.sb('cs', [128, 8, 1], F32)
    s.dma('sp', cs[:], cT_d, writes=[cs])
    s.op('act', lambda e: e.activation(out=cs[:], in_=cs[:], func=AF.Silu), reads=[cs], writes=[cs])
    mb = s.sb('mb', [128, NJ], F32)
    s.dma('sp', mb[:], modbT_d, writes=[mb])
    ps = s.ps('modps', [128, 512])
    st = [s.sb('mst', [128, 8, 512], F32) for _ in range(2)]
    j = 0
    k = 0
    for w in ws:
        wv = w.rearrange("(kc p) n -> p kc n", p=128)
        for n0 in range(0, w.shape[1], 512):
            b = st[k % 2]
            s.dma('sp' if k % 2 == 0 else 'pool', b[:], wv[:, :, n0:n0 + 512], writes=[b])
            k += 1
            for jj in range(4):
                for kc in range(8):
                    s.I('pe', 'matmul', [b, cs], [ps], ps[:, j:j + 1], lhsT=b[:, kc, jj * 128:(jj + 1) * 128], rhs=cs[:, kc, :],
                        start=(kc == 0), stop=(kc == 7))
                j += 1
    modv = C['modv']
    s.I('dve', 'tensor_tensor', [mb], [ps, modv], out=modv[:], in0=ps[:, 0:NJ], in1=mb[:], op=ALU.add)


def get_stage(s):
    if not hasattr(s, '_stage'):
        s._stage = [s.sb('wst', [128, 2048], F32) for _ in range(2)]
        s._stage_i = 0
    return s._stage


def load_cast_w(s, Wb, w_d, piece=512, engs=('dve', 'pool')):
    K, N = w_d.shape
    KC = K // 128
    nk = 2048 // piece
    wv = w_d.rearrange("(kc p) n -> p kc n", p=128)
    st = get_stage(s)
    for k0 in range(0, KC, nk):
        kw = min(nk, KC - k0)
        for n0 in range(0, N, piece):
            nw = min(piece, N - n0)
            i = s._stage_i
            s._stage_i += 1
            b = st[i % 2]
            bv = b[:].rearrange("p (k n) -> p k n", n=piece)
            s.dma('sp' if i % 2 == 0 else 'pool', bv[:, 0:kw, 0:nw], wv[:, k0:k0 + kw, n0:n0 + nw], writes=[b])
            eng = engs[i % len(engs)]
            s.I(eng, 'tensor_copy', [b], [Wb], out=Wb[:, k0:k0 + kw, n0:n0 + nw], in_=bv[:, 0:kw, 0:nw])


def norm_mod(s, C, xs, G, SH, h, tmp, sq, ss_ps, rstd, D=1024, TB=512):
    KC = D // 128
    s.op('act', lambda e: e.activation(out=sq[:], in_=xs[:], func=AF.Square), reads=[xs], writes=[sq])
    for kc in range(KC):
        s.op('pe', lambda e, kc=kc: e.matmul(ss_ps[:, 0:TB], lhsT=C['ones_bf'][:], rhs=sq[:, kc, :],
                                              start=(kc == 0), stop=(kc == KC - 1)),
             reads=[sq, C['ones_bf']], writes=[ss_ps])
    s.op('dve', lambda e: e.tensor_scalar(out=rstd[:], in0=ss_ps[:, 0:TB], scalar1=1.0 / D, scalar2=EPS,
                                          op0=ALU.mult, op1=ALU.add), reads=[ss_ps], writes=[rstd])
    s.op('dve', lambda e: e.reciprocal(out=rstd[:], in_=rstd[:]), reads=[rstd], writes=[rstd])
    s.op('act', lambda e: e.activation(out=rstd[:], in_=rstd[:], func=AF.Sqrt), reads=[rstd], writes=[rstd])
    for kc in range(KC):
        t = tmp[kc % len(tmp)]
        s.op('dve', lambda e, kc=kc, t=t: e.scalar_tensor_tensor(
            out=t[:], in0=xs[:, kc, :], scalar=G[0][:, G[1] + kc:G[1] + kc + 1], in1=rstd[:],
            op0=ALU.mult, op1=ALU.mult), reads=[xs, G[0], rstd], writes=[t])
        s.op('act', lambda e, kc=kc, t=t: e.activation(
            out=h[:, kc, :], in_=t[:], func=AF.Identity, bias=SH[0][:, SH[1] + kc:SH[1] + kc + 1], scale=1.0),
             reads=[t, SH[0]], writes=[h])


def phase_l1(s, C, xT_d, gT_d, w_in_d, P1, Z1, after_tb=None, NT=4096, TB=512):
    NOUT = w_in_d.shape[1]
    modv = C['modv']
    gT = s.sb('gT', [128, 8], F32)
    s.dma('sp', gT[:], gT_d, writes=[gT])
    G = s.sb('G', [128, 8], F32)
    s.op('dve', lambda e: e.scalar_tensor_tensor(out=G[:], in0=modv[:, 8:16], scalar=1.0, in1=gT[:],
                                                 op0=ALU.add, op1=ALU.mult), reads=[modv, gT], writes=[G])
    Wb = s.sb('Wb', [128, 8, NOUT], BF16)
    load_cast_w(s, Wb, w_in_d)
    xs = [s.sb('xs', [128, 8, TB], F32) for _ in range(2)]
    sq = s.sb('sq', [128, 8, TB], BF16)
    hs = [s.sb('h', [128, 8, TB], BF16) for _ in range(2)]
    tmp = [s.sb('tmp', [128, TB], F32) for _ in range(2)]
    rstd = s.sb('rstd', [128, TB], F32)
    ss_ps = s.ps('ss', [128, 512])
    pps = [s.ps('pp', [128, 512]) for _ in range(4)]
    outs = [s.sb('ost', [128, TB], F32) for _ in range(4)]
    xv = xT_d.rearrange("(kc p) t -> p kc t", p=128)
    nchunks = [(n0, min(128, NOUT - n0)) for n0 in range(0, NOUT, 128)]
    it = 0
    for tb in range(NT // TB):
        x = xs[tb % 2]
        h = hs[tb % 2]
        s.dma('sp', x[:], xv[:, :, tb * TB:(tb + 1) * TB], writes=[x])
        norm_mod(s, C, x, (G, 0), (modv, 0), h, tmp, sq, ss_ps, rstd)
        for (n0, nw) in nchunks:
            pp = pps[it % 4]
            o = outs[it % 4]
            for kc in range(8):
                s.op('pe', lambda e, pp=pp, kc=kc, n0=n0, nw=nw, h=h: e.matmul(
                    pp[0:nw, 0:TB], lhsT=Wb[:, kc, n0:n0 + nw], rhs=h[:, kc, :], start=(kc == 0), stop=(kc == 7)),
                     reads=[Wb, h], writes=[pp])
            if it % 2 == 0:
                s.op('act', lambda e, pp=pp, o=o, nw=nw: e.copy(out=o[0:nw, :], in_=pp[0:nw, 0:TB]),
                     reads=[pp], writes=[o])
            else:
                s.op('dve', lambda e, pp=pp, o=o, nw=nw: e.tensor_copy(out=o[0:nw, :], in_=pp[0:nw, 0:TB]),
                     reads=[pp], writes=[o])
            if n0 < 3072:
                dst, db = P1.h[tb * 3136 + n0:tb * 3136 + n0 + nw, :], P1
            elif n0 < 4096:
                dst, db = Z1.h[n0 - 3072:n0 - 3072 + nw, tb * TB:(tb + 1) * TB], Z1
            else:
                dst, db = P1.h[tb * 3136 + 3072:tb * 3136 + 3072 + nw, :], P1
            s.dma('sp', dst, o[0:nw, :], reads=[o], writes=[db])
            it += 1
        if after_tb is not None:
            after_tb(tb)


L2EPS = 1e-6


def gdn_consts(s, C, ident_d, negTi_d, negTs_d):
    for nm, d in (('ident', ident_d), ('negTi', negTi_d), ('negTs', negTs_d)):
        C[nm] = s.sb(nm, [128, 128], F32)
        s.dma('sp', C[nm][:], d, writes=[C[nm]])
        C[nm + '_bf'] = s.sb(nm + '_bf', [128, 128], BF16)
        s.I('dve', 'tensor_copy', [C[nm]], [C[nm + '_bf']], out=C[nm + '_bf'][:], in_=C[nm][:])


class PsumPool:
    def __init__(self, s, nbanks, name):
        self.banks = [s.ps(f'{name}{i}', [128, 512]) for i in range(nbanks)]
        self.i = 0

    def get(self):
        b = self.banks[(self.i // 4) % len(self.banks)]
        sl = self.i % 4
        self.i += 1
        return b, b[:, sl * 128:(sl + 1) * 128]

    def get2(self):
        if self.i % 2:
            self.i += 1
        b = self.banks[(self.i // 4) % len(self.banks)]
        sl = self.i % 4
        self.i += 2
        return b, b[:, sl * 128:(sl + 2) * 128]

    def get_bank(self):
        self.i = ((self.i + 3) // 4) * 4
        b = self.banks[(self.i // 4) % len(self.banks)]
        self.i += 4
        return b


def phase_gdn(s, C, G1, cw_d, hp_d, O2, S=16384, NH=2, IDX_G1=0, IDX_AB=192, after_block=None):
    SB = 512
    NSB = S // SB
    scale = 128 ** -0.5
    ident = C['ident']
    idx = C['idx']
    g1v512 = G1.h[:, :]
    I = s.I
    cw = s.sb('cw', [128, NH * 3 * 4], F32)
    s.dma('sp', cw[:], cw_d, writes=[cw])
    hp = s.sb('hp', [128, NH * 2], F32)
    s.dma('sp', hp[:], hp_d, writes=[hp])
    pp = PsumPool(s, 6, 'lv')
    seqbank = [s.ps(f'seq{h}', [128, 512]) for h in range(NH)]
    bc127 = ident[:, 127:128].to_broadcast([128, 128])
    import math
    lnsc = s.sb('lnsc', [128, 1], F32)
    I('pool', 'memset', [], [lnsc], lnsc[:], math.log(scale))
    lnscale_c = lnsc[:, 0:1]

    tabs = []
    for h in range(NH):
        T = {}
        for nm in ('a', 'b', 'x', 't', 'gcum', 'rtab', 'gcumT', 'betaT', 'negg', 'be', 'kdf', 'glc', 'ones'):
            T[nm] = s.sb(f'{nm}{h}', [128, 128], F32)
        T['nA'] = s.sb(f'nA{h}', [128, 1], F32)
        a, b, x, t, nA = T['a'], T['b'], T['x'], T['t'], T['nA']
        g1v128 = G1.h[:, :].rearrange("r (b w) -> (r b) w", w=128)
        s.gather(a[:], g1v128, idx[:, IDX_AB + 2 * h:IDX_AB + 2 * h + 1], [G1, idx], [a])
        s.gather(b[:], g1v128, idx[:, IDX_AB + 2 * h + 1:IDX_AB + 2 * h + 2], [G1, idx], [b])
        I('pool', 'memset', [], [T['ones']], T['ones'][:], 1.0)
        I('dve', 'tensor_scalar', [a, hp], [x], out=x[:], in0=a[:], scalar1=hp[:, 2 * h + 1:2 * h + 2], scalar2=None, op0=ALU.add)
        I('act', 'activation', [x], [t], out=t[:], in_=x[:], func=AF.Abs)
        I('act', 'activation', [t], [t], out=t[:], in_=t[:], func=AF.Exp, scale=-1.0)
        I('act', 'activation', [t], [t], out=t[:], in_=t[:], func=AF.Ln, bias=1.0)
        I('dve', 'tensor_scalar_max', [x], [x], out=x[:], in0=x[:], scalar1=0.0)
        I('dve', 'tensor_tensor', [x, t], [x], out=x[:], in0=x[:], in1=t[:], op=ALU.add)
        I('act', 'activation', [hp], [nA], out=nA[:], in_=hp[:, 2 * h:2 * h + 1], func=AF.Exp)
        I('dve', 'tensor_scalar', [nA], [nA], out=nA[:], in0=nA[:], scalar1=-1.0, scalar2=None, op0=ALU.mult)
        I('dve', 'tensor_scalar', [x, nA], [x], out=x[:], in0=x[:], scalar1=nA[:, 0:1], scalar2=None, op0=ALU.mult)
        I('dve', 'tensor_tensor_scan', [T['ones'], x], [T['gcum']], out=T['gcum'][:], data0=T['ones'][:], data1=x[:],
          initial=0.0, op0=ALU.mult, op1=ALU.add)
        I('act', 'activation', [b], [b], out=b[:], in_=b[:], func=AF.Sigmoid)
        I('act', 'activation', [b], [t], out=t[:], in_=b[:], func=AF.Ln)
        I('dve', 'tensor_tensor', [T['gcum'], t], [T['rtab']], out=T['rtab'][:], in0=T['gcum'][:], in1=t[:], op=ALU.add)
        bk, p1 = pp.get()
        I('pe', 'transpose', [T['gcum'], ident], [bk], p1, T['gcum'][:], ident[:])
        I('act', 'copy', [], [bk, T['gcumT']], out=T['gcumT'][:], in_=p1)
        bk2, p2 = pp.get()
        I('pe', 'transpose', [b, ident], [bk2], p2, b[:], ident[:])
        I('act', 'copy', [], [bk2, T['betaT']], out=T['betaT'][:], in_=p2)
        bk3, p3 = pp.get()
        I('pe', 'matmul', [ident, T['gcumT']], [bk3], p3, lhsT=bc127, rhs=T['gcumT'][:], start=True, stop=True)
        I('act', 'activation', [], [bk3, T['glc']], out=T['glc'][:], in_=p3, func=AF.Exp)
        I('dve', 'tensor_tensor', [T['gcumT']], [bk3, T['kdf']], out=T['kdf'][:], in0=p3, in1=T['gcumT'][:], op=ALU.subtract)
        I('act', 'activation', [T['kdf']], [T['kdf']], out=T['kdf'][:], in_=T['kdf'][:], func=AF.Exp)
        I('dve', 'tensor_scalar', [T['gcumT']], [T['negg']], out=T['negg'][:], in0=T['gcumT'][:], scalar1=-1.0, scalar2=None, op0=ALU.mult)
        I('act', 'activation', [T['gcumT']], [T['be']], out=T['be'][:], in_=T['gcumT'][:], func=AF.Exp)
        I('dve', 'tensor_tensor', [T['be'], T['betaT']], [T['be']], out=T['be'][:], in0=T['be'][:], in1=T['betaT'][:], op=ALU.mult)
        T['cat'] = s.sb(f'cat{h}', [128, 384], F32)
        I('pool', 'tensor_copy', [T['gcum']], [T['cat']], out=T['cat'][:, 0:128], in_=T['gcum'][:])
        I('pool', 'tensor_copy', [T['rtab']], [T['cat']], out=T['cat'][:, 128:256], in_=T['rtab'][:])
        I('pool', 'tensor_copy', [T['gcum']], [T['cat']], out=T['cat'][:, 256:384], in_=T['gcum'][:])
        tabs.append(T)

    S32 = [s.sb(f'S32_{h}', [128, 128], F32) for h in range(NH)]
    Sbf = [s.sb(f'Sbf_{h}', [128, 128], BF16) for h in range(NH)]
    for h in range(NH):
        I('pool', 'memset', [], [S32[h]], S32[h][:], 0.0)
        I('pool', 'memset', [], [Sbf[h]], Sbf[h][:], 0.0)

    NSET = 3
    sets = []
    for u in range(NSET):
        U = {}
        U['xin'] = [s.sb(f'xin{u}{X}', [128, 3 + SB], F32) for X in range(3)]
        U['acc'] = [s.sb(f'acc{u}{X}', [128, SB], F32) for X in range(3)]
        U['cv'] = [s.sb(f'cv{u}{X}', [128, SB], F32) for X in range(3)]
        U['sq'] = s.sb(f'sq{u}', [128, SB], BF16)
        U['rn'] = s.sb(f'rn{u}', [128, SB], F32)
        U['xn'] = [s.sb(f'xn{u}{X}', [128, SB], F32) for X in range(2)]
        U['ost'] = s.sb(f'ost{u}', [128, 4, 128], F32)
        U['ch'] = []
        for c in range(4):
            D = {}
            for nm in ('kbg', 'vb', 'CT', 'DT', 'EGB', 'Q0', 'Q1', 'u'):
                D[nm] = s.sb(f'{nm}{u}{c}', [128, 128], F32)
            for nm in ('YP0', 'YP1'):
                D[nm] = s.sb(f'{nm}{u}{c}', [128, 256], F32)
            for nm in ('kdec', 'qkT', 'qg', 'wT', 'vnew'):
                D[nm] = s.sb(f'{nm}{u}{c}', [128, 128], BF16)
            U['ch'].append(D)
        sets.append(U)

    def prep_gen(h, sb, U, Uprev):
        T = tabs[h]
        t0 = sb * SB
        for X in range(3):
            xin = U['xin'][X]
            if sb == 0:
                I('pool', 'memset', [], [xin], xin[:, 0:3], 0.0)
            else:
                pxin = Uprev['xin'][X]
                I('pool', 'tensor_copy', [pxin], [xin], out=xin[:, 0:3], in_=pxin[:, SB:SB + 3])
            col = IDX_G1 + (h * 3 + X) * 32 + sb
            s.gather(xin[:, 3:3 + SB], g1v512, idx[:, col:col + 1], [G1, idx], [xin])
            eng = 'dve' if X != 1 else 'pool'
            acc = U['acc'][X]
            wc = (h * 3 + X) * 4
            I(eng, 'tensor_scalar', [xin, cw], [acc], out=acc[:], in0=xin[:, 0:SB], scalar1=cw[:, wc:wc + 1], scalar2=None, op0=ALU.mult)
            for j in range(1, 4):
                if eng == 'dve':
                    I(eng, 'scalar_tensor_tensor', [xin, cw, acc], [acc], out=acc[:], in0=xin[:, j:j + SB],
                      scalar=cw[:, wc + j:wc + j + 1], in1=acc[:], op0=ALU.mult, op1=ALU.add)
                else:
                    tq = U['rn']
                    I(eng, 'tensor_scalar', [xin, cw], [tq], out=tq[:], in0=xin[:, j:j + SB], scalar1=cw[:, wc + j:wc + j + 1],
                      scalar2=None, op0=ALU.mult)
                    I(eng, 'tensor_tensor', [tq, acc], [acc], out=acc[:], in0=acc[:], in1=tq[:], op=ALU.add)
            cv = U['cv'][X]
            I('act', 'activation', [acc], [cv], out=cv[:], in_=acc[:], func=AF.Silu)
            yield
        for X in range(2):
            cv = U['cv'][X]
            sq, rn, xn = U['sq'], U['rn'], U['xn'][X]
            I('act', 'activation', [cv], [sq], out=sq[:], in_=cv[:], func=AF.Square)
            bk = pp.get_bank()
            I('pe', 'matmul', [C['ones_bf'], sq], [bk], bk[:, 0:SB], lhsT=C['ones_bf'][:], rhs=sq[:], start=True, stop=True)
            I('dve', 'tensor_scalar', [], [bk, rn], out=rn[:], in0=bk[:, 0:SB], scalar1=L2EPS, scalar2=None, op0=ALU.add)
            I('dve', 'reciprocal', [rn], [rn], out=rn[:], in_=rn[:])
            I('act', 'activation', [rn], [rn], out=rn[:], in_=rn[:], func=AF.Sqrt)
            I('pool', 'tensor_tensor', [cv, rn], [xn], out=xn[:], in0=cv[:], in1=rn[:], op=ALU.mult)
            yield
        qn, kn, vv = U['xn'][0], U['xn'][1], U['cv'][2]
        for c in range(4):
            D = U['ch'][c]
            n = sb * 4 + c
            cs = slice(c * 128, (c + 1) * 128)
            icol = ident[:, n:n + 1].to_broadcast([128, 128])
            be_c, kdf_c, bet_c, ng_c = (T[k][:, n:n + 1] for k in ('be', 'kdf', 'betaT', 'negg'))
            b_trk, p_trk = pp.get()
            I('pe', 'transpose', [kn, ident], [b_trk], p_trk, kn[:, cs], ident[:])
            b_trv, p_trv = pp.get()
            I('pe', 'transpose', [vv, ident], [b_trv], p_trv, vv[:, cs], ident[:])
            b_kk, p_kk = pp.get()
            I('pe', 'matmul', [kn], [b_kk], p_kk, lhsT=kn[:, cs], rhs=kn[:, cs], start=True, stop=True)
            b_bc = pp.get_bank()
            p_dt, p_ct, p_gb = b_bc[:, 0:128], b_bc[:, 128:256], b_bc[:, 256:384]
            I('pe', 'matmul', [ident, T['cat']], [b_bc], b_bc[:, 0:384], lhsT=icol, rhs=T['cat'][:], start=True, stop=False, skip_group_check=True)
            I('pe', 'matmul', [C['ident_bf'], C['negTi_bf']], [b_bc], p_dt, lhsT=C['ident_bf'][:], rhs=C['negTi_bf'][:],
              start=False, stop=False, skip_group_check=True)
            I('pe', 'matmul', [C['ident_bf'], C['negTs_bf']], [b_bc], p_ct, lhsT=C['ident_bf'][:], rhs=C['negTs_bf'][:],
              start=False, stop=True, skip_group_check=True)
            I('act', 'mul', [T['be']], [b_trk, D['kbg']], out=D['kbg'][:], in_=p_trk, mul=be_c)
            I('dve', 'tensor_scalar', [T['kdf']], [b_trk, D['kdec']], out=D['kdec'][:], in0=p_trk, scalar1=kdf_c, scalar2=None, op0=ALU.mult)
            I('act', 'mul', [T['betaT']], [b_trv, D['vb']], out=D['vb'][:], in_=p_trv, mul=bet_c)
            I('act', 'activation', [T['negg']], [b_bc, D['CT']], out=D['CT'][:], in_=p_ct, func=AF.Exp, bias=ng_c, scale=1.0)
            YP0 = D['YP0']
            I('dve', 'scalar_tensor_tensor', [D['CT']], [b_kk, YP0], out=YP0[:, 128:256], in0=p_kk, scalar=-1.0, in1=D['CT'][:],
              op0=ALU.mult, op1=ALU.mult)
            I('pool', 'tensor_copy', [ident], [YP0], out=YP0[:, 0:128], in_=ident[:])
            b_qk, p_qk = pp.get()
            I('pe', 'matmul', [kn, qn], [b_qk], p_qk, lhsT=kn[:, cs], rhs=qn[:, cs], start=True, stop=True)
            b_q, p_q = pp.get()
            I('pe', 'transpose', [YP0, ident], [b_q], p_q, YP0[:, 128:256], ident[:])
            I('act', 'activation', [T['negg']], [b_bc, D['DT']], out=D['DT'][:], in_=p_dt, func=AF.Exp, bias=ng_c, scale=1.0)
            I('dve', 'scalar_tensor_tensor', [D['DT']], [b_qk, D['qkT']], out=D['qkT'][:], in0=p_qk, scalar=scale, in1=D['DT'][:],
              op0=ALU.mult, op1=ALU.mult)
            I('act', 'activation', [lnsc], [b_bc, D['EGB']], out=D['EGB'][:], in_=p_gb, func=AF.Exp, bias=lnscale_c, scale=1.0)
            I('pool', 'tensor_tensor', [qn, D['EGB']], [D['qg']], out=D['qg'][:], in0=qn[:, cs], in1=D['EGB'][:], op=ALU.mult)
            I('act', 'copy', [], [b_q, D['Q0']], out=D['Q0'][:], in_=p_q)
            yield
        m = 1
        cur = 0
        while m <= 64:
            nxt = 1 - cur
            YPc, YPn, Qc, Qn_ = f'YP{cur}', f'YP{nxt}', f'Q{cur}', f'Q{nxt}'
            for c in range(4):
                D = U['ch'][c]
                wa = 256 if m < 32 else 128
                if wa == 256:
                    ba, pa_ = pp.get2()
                else:
                    ba, pa_ = pp.get()
                I('pe', 'matmul', [D[Qc], D[YPc]], [ba], pa_, lhsT=D[Qc][:], rhs=D[YPc][:, 0:wa], start=True, stop=True)
                if m < 64:
                    b2, p2 = pp.get()
                    I('pe', 'matmul', [D[Qc], D[YPc]], [b2], p2, lhsT=D[YPc][:, 128:256], rhs=D[Qc][:], start=True, stop=True)
                I('dve', 'tensor_tensor', [D[YPc]], [ba, D[YPn]], out=D[YPn][:, 0:128], in0=pa_[:, 0:128], in1=D[YPc][:, 0:128], op=ALU.add)
                if wa == 256:
                    I('act', 'copy', [], [ba, D[YPn]], out=D[YPn][:, 128:256], in_=pa_[:, 128:256])
                if m < 64:
                    I('act', 'copy', [], [b2, D[Qn_]], out=D[Qn_][:], in_=p2)
            cur = nxt
            m *= 2
            yield
        Yf = f'YP{cur}'
        for c in range(4):
            D = U['ch'][c]
            bu, pu = pp.get()
            I('pe', 'matmul', [D[Yf], D['vb']], [bu], pu, lhsT=D[Yf][:, 0:128], rhs=D['vb'][:], start=True, stop=True)
            bw, pw = pp.get()
            I('pe', 'matmul', [D[Yf], D['kbg']], [bw], pw, lhsT=D['kbg'][:], rhs=D[Yf][:, 0:128], start=True, stop=True)
            I('act', 'copy', [], [bu, D['u']], out=D['u'][:], in_=pu)
            I('dve', 'tensor_copy', [], [bw, D['wT']], out=D['wT'][:], in_=pw)
        yield

    def seq_gen(h, sb, U):
        T = tabs[h]
        bk = seqbank[h]
        pvn, po, psu = bk[:, 0:128], bk[:, 128:256], bk[:, 256:384]
        Sb, Sf = Sbf[h], S32[h]
        for c in range(4):
            D = U['ch'][c]
            n = sb * 4 + c
            I('pe', 'matmul', [D['wT'], Sb], [bk], pvn, lhsT=D['wT'][:], rhs=Sb[:], start=True, stop=True)
            I('dve', 'tensor_tensor', [D['u']], [bk, D['vnew']], out=D['vnew'][:], in0=D['u'][:], in1=pvn, op=ALU.subtract)
            I('pe', 'matmul', [D['qg'], Sb], [bk], po, lhsT=Sb[:], rhs=D['qg'][:], start=True, stop=False)
            I('pe', 'matmul', [D['qkT'], D['vnew']], [bk], po, lhsT=D['vnew'][:], rhs=D['qkT'][:], start=False, stop=True)
            I('pe', 'matmul', [D['kdec'], D['vnew']], [bk], psu, lhsT=D['kdec'][:], rhs=D['vnew'][:], start=True, stop=True)
            I('dve', 'scalar_tensor_tensor', [Sf, T['glc']], [bk, Sf], out=Sf[:], in0=Sf[:], scalar=T['glc'][:, n:n + 1], in1=psu,
              op0=ALU.mult, op1=ALU.add)
            I('act', 'copy', [Sf], [Sb], out=Sb[:], in_=Sf[:])
            I('act', 'copy', [], [bk, U['ost']], out=U['ost'][:, c, :], in_=po)
            yield
        t0 = sb * SB
        r0 = (h * 4 + sb // 8) * 128
        c0 = (sb % 8) * SB
        s.dma('sp', O2.h[r0:r0 + 128, c0:c0 + SB], U['ost'][:].rearrange("p c e -> p (c e)"), reads=[U['ost']], writes=[O2])
        if after_block is not None and sb % 8 == 7:
            after_block(h * 4 + sb // 8)
        yield

    def run_rr(gens):
        gens = list(gens)
        while gens:
            for g in list(gens):
                try:
                    next(g)
                except StopIteration:
                    gens.remove(g)

    prev = None
    u = 0
    for sb in range(NSB):
        for h in range(NH):
            U = sets[u % NSET]
            gens = [prep_gen(h, sb, U, sets[(u - 2) % NSET])]
            if prev is not None:
                gens.append(seq_gen(*prev))
            run_rr(gens)
            prev = (h, sb, U)
            u += 1
    run_rr([seq_gen(*prev)])


def rstd_from_ps(s, ss_ps_bank, ss_ap, rstd, D):
    I = s.I
    I('dve', 'tensor_scalar', [], [ss_ps_bank, rstd], out=rstd[:], in0=ss_ap, scalar1=1.0 / D, scalar2=EPS, op0=ALU.mult, op1=ALU.add)
    I('dve', 'reciprocal', [rstd], [rstd], out=rstd[:], in_=rstd[:])
    I('act', 'activation', [rstd], [rstd], out=rstd[:], in_=rstd[:], func=AF.Sqrt)


def phase_post(s, C, mode, Gin, idx_base, Z1, xsrc, xsrc_buf, gt_off, og_d, w_out_d, XO, lam_init=0.0, NT=4096, TB=512):
    I = s.I
    modv = C['modv']
    idx = C['idx']
    gv = Gin.h[:, :].rearrange("r (b w) -> (r b) w", w=TB)
    ncol = 1 if mode == 'gdn' else 2
    og = s.sb('og', [128, ncol], F32)
    s.dma('sp', og[:], og_d, writes=[og])
    if mode == 'attn':
        I('dve', 'tensor_scalar', [og], [og], out=og[:], in0=og[:], scalar1=1.0 - lam_init, scalar2=None, op0=ALU.mult)
    Wb = s.sb('Wb', [128, 8, 1024], BF16)
    load_cast_w(s, Wb, w_out_d)
    os_ = [s.sb('o', [128, 8, TB], F32) for _ in range(2)]
    zs = [s.sb('z', [128, 8, TB], F32) for _ in range(2)] if mode == 'gdn' else None
    xs = [s.sb('x', [128, 8, TB], F32) for _ in range(2)]
    sq = s.sb('sq', [128, 8, TB], BF16)
    hb = [s.sb('hb', [128, 8, TB], BF16) for _ in range(2)]
    rstd = [s.sb('rstd', [128, TB], F32) for _ in range(2)]
    t1 = [s.sb('t1', [128, TB], F32) for _ in range(2)]
    t2 = [s.sb('t2', [128, TB], F32) for _ in range(2)]
    ssb = [s.ps('ss', [128, 512]) for _ in range(2)]
    pps = [s.ps('pp', [128, 512]) for _ in range(4)]
    outs = [s.sb('ost', [128, TB], F32) for _ in range(4)]
    xv = xsrc.rearrange("(kc p) t -> p kc t", p=128)
    zv = Z1.h[:, :].rearrange("(kc p) t -> p kc t", p=128) if mode == 'gdn' else None
    it = 0
    k2 = 0
    for tb in range(NT // TB):
        tsl = slice(tb * TB, (tb + 1) * TB)
        o = os_[tb % 2]
        x = xs[tb % 2]
        h = hb[tb % 2]
        for kc in range(8):
            col = idx_base + kc * 8 + tb
            s.gather(o[:, kc, :], gv, idx[:, col:col + 1], [Gin, idx], [o])
        s.dma('sp', x[:], xv[:, :, tsl], reads=[xsrc_buf] if xsrc_buf is not None else [], writes=[x])
        if mode == 'gdn':
            z = zs[tb % 2]
            s.dma('sp', z[:], zv[:, :, tsl], reads=[Z1], writes=[z])
        I('act', 'activation', [o], [sq], out=sq[:], in_=o[:], func=AF.Square)
        if mode == 'gdn':
            for kc in range(8):
                sb_ = ssb[k2 % 2]
                r = rstd[k2 % 2]
                a1 = t1[k2 % 2]
                a2 = t2[k2 % 2]
                k2 += 1
                I('pe', 'matmul', [C['ones_bf'], sq], [sb_], sb_[:, 0:TB], lhsT=C['ones_bf'][:], rhs=sq[:, kc, :], start=True, stop=True)
                rstd_from_ps(s, sb_, sb_[:, 0:TB], r, 128)
                I('dve', 'tensor_tensor', [o, r], [a1], out=a1[:], in0=o[:, kc, :], in1=r[:], op=ALU.mult)
                I('act', 'activation', [z], [a2], out=a2[:], in_=z[:, kc, :], func=AF.Silu)
                I('pool', 'tensor_tensor', [a1, a2], [a1], out=a1[:], in0=a1[:], in1=a2[:], op=ALU.mult)
                I('act', 'mul', [a1, og], [h], out=h[:, kc, :], in_=a1[:], mul=og[:, 0:1])
        else:
            for hd in range(4):
                sb_ = ssb[k2 % 2]
                r = rstd[k2 % 2]
                k2 += 1
                for j in range(2):
                    I('pe', 'matmul', [C['ones_bf'], sq], [sb_], sb_[:, 0:TB], lhsT=C['ones_bf'][:], rhs=sq[:, 2 * hd + j, :],
                      start=(j == 0), stop=(j == 1))
                rstd_from_ps(s, sb_, sb_[:, 0:TB], r, 256)
                for j in range(2):
                    I('dve', 'scalar_tensor_tensor', [o, og, r], [h], out=h[:, 2 * hd + j, :], in0=o[:, 2 * hd + j, :],
                      scalar=og[:, j:j + 1], in1=r[:], op0=ALU.mult, op1=ALU.mult)
        for n in range(8):
            pp = pps[it % 4]
            ot = outs[it % 4]
            for kc in range(8):
                I('pe', 'matmul', [Wb, h], [pp], pp[:, 0:TB], lhsT=Wb[:, kc, n * 128:(n + 1) * 128], rhs=h[:, kc, :],
                  start=(kc == 0), stop=(kc == 7))
            I('dve', 'scalar_tensor_tensor', [modv, x], [pp, ot], out=ot[:], in0=pp[:, 0:TB],
              scalar=modv[:, gt_off + n:gt_off + n + 1], in1=x[:, n, :], op0=ALU.mult, op1=ALU.add)
            s.dma('sp', XO.h[n * 128:(n + 1) * 128, tsl], ot[:], reads=[ot], writes=[XO])
            it += 1


def phase_mlp(s, C, XI, moff, gT_d, w1_d, w2_d, xoT_d, XO=None, final_g_d=None, NT=4096, TB=256):
    I = s.I
    modv = C['modv']
    gT = s.sb('gT', [128, 8], F32)
    s.dma('sp', gT[:], gT_d, writes=[gT])
    G = s.sb('G', [128, 8], F32)
    I('dve', 'scalar_tensor_tensor', [modv, gT], [G], out=G[:], in0=modv[:, moff + 8:moff + 16], scalar=1.0, in1=gT[:],
      op0=ALU.add, op1=ALU.mult)
    if final_g_d is not None:
        fg = s.sb('fg', [128, 8], F32)
        s.dma('sp', fg[:], final_g_d, writes=[fg])
    W1 = s.sb('W1', [128, 8, 4096], BF16)
    W2 = s.sb('W2', [128, 32, 1024], BF16)
    load_cast_w(s, W1, w1_d)
    load_cast_w(s, W2, w2_d)
    xs = [b.view("p (k t) -> p k t", t=TB) for b in get_stage(s)]
    assert TB == 256
    sq = s.sb('sq', [128, 8, TB], BF16)
    hs = [s.sb('h', [128, 8, TB], BF16) for _ in range(2)]
    tmp = [s.sb('tmp', [128, TB], F32) for _ in range(2)]
    rstd = s.sb('rstd', [128, TB], F32)
    hid = s.sb('hid', [128, 32, TB], BF16)
    rl = [s.sb('rl', [128, TB], F32) for _ in range(2)]
    xo = [s.sb('xo', [128, 8, TB], F32) for _ in range(1)]
    ss_ps = s.ps('ss', [128, 512])
    pps = [s.ps('pp', [128, 512]) for _ in range(6)]
    xv = XI.h[:, :].rearrange("(kc p) t -> p kc t", p=128)
    ov = xoT_d.rearrange("(kc p) t -> p kc t", p=128)
    owr = [XO] if XO is not None else []
    it = 0
    for tb in range(NT // TB):
        tsl = slice(tb * TB, (tb + 1) * TB)
        x = xs[tb % 2]
        h = hs[tb % 2]
        xout = xo[0]
        s.dma('sp', x[:], xv[:, :, tsl], reads=[XI], writes=[x])
        norm_mod(s, C, x, (G, 0), (modv, moff), h, tmp, sq, ss_ps, rstd, TB=TB)
        for f in range(32):
            pp = pps[it % 6]
            r = rl[it % 2]
            it += 1
            for kc in range(8):
                I('pe', 'matmul', [W1, h], [pp], pp[:, 0:TB], lhsT=W1[:, kc, f * 128:(f + 1) * 128], rhs=h[:, kc, :],
                  start=(kc == 0), stop=(kc == 7))
            I('act', 'activation', [], [pp, r], out=r[:], in_=pp[:, 0:TB], func=AF.Relu)
            I('pool', 'tensor_tensor', [r], [hid], out=hid[:, f, :], in0=r[:], in1=r[:], op=ALU.mult)
        for n in range(8):
            pp = pps[it % 6]
            it += 1
            for f in range(32):
                I('pe', 'matmul', [W2, hid], [pp], pp[:, 0:TB], lhsT=W2[:, f, n * 128:(n + 1) * 128], rhs=hid[:, f, :],
                  start=(f == 0), stop=(f == 31))
            I('dve', 'scalar_tensor_tensor', [modv, x], [pp, xout], out=xout[:, n, :], in0=pp[:, 0:TB],
              scalar=modv[:, moff + 16 + n:moff + 17 + n], in1=x[:, n, :], op0=ALU.mult, op1=ALU.add)
        if final_g_d is None:
            s.dma('sp', ov[:, :, tsl], xout[:], reads=[xout], writes=owr)
        else:
            I('act', 'activation', [xout], [sq], out=sq[:], in_=xout[:], func=AF.Square)
            for kc in range(8):
                I('pe', 'matmul', [C['ones_bf'], sq], [ss_ps], ss_ps[:, 0:TB], lhsT=C['ones_bf'][:], rhs=sq[:, kc, :],
                  start=(kc == 0), stop=(kc == 7))
            rstd_from_ps(s, ss_ps, ss_ps[:, 0:TB], rstd, 1024)
            for kc in range(8):
                I('dve', 'scalar_tensor_tensor', [xout, fg, rstd], [xout], out=xout[:, kc, :], in0=xout[:, kc, :],
                  scalar=fg[:, kc:kc + 1], in1=rstd[:], op0=ALU.mult, op1=ALU.mult)
            s.dma('sp', ov[:, :, tsl], xout[:], reads=[xout], writes=owr)


def phase_kvq(s, C, XI, gkvT_d, gqT_d, kv_w_d, kv_wp_d, wq_d, wqp_d, ropeC_d, ropeS_d, Q3, K3, V3, after_tb=None, NT=4096, TB=512):
    I = s.I
    modv = C['modv']
    gkv = s.sb('gkv', [128, 8], F32)
    s.dma('sp', gkv[:], gkvT_d, writes=[gkv])
    gq = s.sb('gq', [128, 8], F32)
    s.dma('sp', gq[:], gqT_d, writes=[gq])
    Gkv = s.sb('Gkv', [128, 8], F32)
    Gq = s.sb('Gq', [128, 8], F32)
    I('dve', 'scalar_tensor_tensor', [modv, gkv], [Gkv], out=Gkv[:], in0=modv[:, 104:112], scalar=1.0, in1=gkv[:], op0=ALU.add, op1=ALU.mult)
    I('dve', 'scalar_tensor_tensor', [modv, gq], [Gq], out=Gq[:], in0=modv[:, 56:64], scalar=1.0, in1=gq[:], op0=ALU.add, op1=ALU.mult)
    Wkv = s.sb('Wkv', [128, 8, 2048], BF16)
    Wkp = s.sb('Wkp', [128, 8, 1024], BF16)
    Wq = s.sb('Wq', [128, 8, 1024], BF16)
    Wqp = s.sb('Wqp', [128, 8, 1024], BF16)
    load_cast_w(s, Wkv, kv_w_d)
    load_cast_w(s, Wkp, kv_wp_d)
    load_cast_w(s, Wq, wq_d)
    load_cast_w(s, Wqp, wqp_d)
    xs = [s.sb('xs', [128, 8, TB], F32) for _ in range(2)]
    sq = s.sb('sq', [128, 8, TB], BF16)
    hkv = s.sb('hkv', [128, 8, TB], BF16)
    hq = s.sb('hq', [128, 8, TB], BF16)
    tmp = [s.sb('tmp', [128, TB], F32) for _ in range(2)]
    rstd = s.sb('rstd', [128, TB], F32)
    rc = [s.sb('rc', [128, TB], F32) for _ in range(2)]
    rs = [s.sb('rs', [128, TB], F32) for _ in range(2)]
    ta = [s.sb('ta', [128, TB], F32) for _ in range(3)]
    tb_ = [s.sb('tb', [128, TB], F32) for _ in range(3)]
    outs = [s.sb('ost', [128, TB], BF16) for _ in range(4)]
    vouts = [s.sb('vost', [128, 256], BF16) for _ in range(4)]
    ss_ps = s.ps('ss', [128, 512])
    pps = [s.ps('pp', [128, 512]) for _ in range(6)]
    xv = XI.h[:, :].rearrange("(kc p) t -> p kc t", p=128)
    it = 0
    io = 0
    for tb in range(NT // TB):
        tsl = slice(tb * TB, (tb + 1) * TB)
        x = xs[tb % 2]
        cc = rc[tb % 2]
        sn = rs[tb % 2]
        s.dma('sp', x[:], xv[:, :, tsl], reads=[XI], writes=[x])
        s.dma('sp', cc[:], ropeC_d[:, tsl], writes=[cc])
        s.dma('sp', sn[:], ropeS_d[:, tsl], writes=[sn])
        norm_mod(s, C, x, (Gkv, 0), (modv, 96), hkv, tmp, sq, ss_ps, rstd, TB=TB)
        norm_mod(s, C, x, (Gq, 0), (modv, 48), hq, tmp, sq, ss_ps, rstd, TB=TB)
        for (W, Wp, hh, dst) in ((Wkv, Wkp, hkv, K3), (Wq, Wqp, hq, Q3)):
            for n in range(8):
                pa = pps[it % 6]
                it += 1
                for kc in range(8):
                    I('pe', 'matmul', [W, hh], [pa], pa[:, 0:TB], lhsT=W[:, kc, n * 128:(n + 1) * 128], rhs=hh[:, kc, :],
                      start=(kc == 0), stop=(kc == 7))
                pb = pps[it % 6]
                it += 1
                for kc in range(8):
                    I('pe', 'matmul', [Wp, hh], [pb], pb[0:32, 0:TB], lhsT=Wp[:, kc, n * 128:n * 128 + 32], rhs=hh[:, kc, :],
                      start=(kc == 0), stop=(kc == 7))
                ot = outs[io % 4]
                a = ta[io % 3]
                b = tb_[io % 3]
                I('dve', 'tensor_tensor', [cc], [pa, a], out=a[:], in0=pa[:, 0:TB], in1=cc[:], op=ALU.mult)
                I('dve', 'tensor_tensor', [sn], [pb, b], out=b[0:32, :], in0=pb[0:32, 0:TB], in1=sn[0:32, :], op=ALU.mult)
                I('pool', 'tensor_tensor', [a, b], [a], out=a[0:32, :], in0=a[0:32, :], in1=b[0:32, :], op=ALU.add)
                I('act', 'copy', [a], [ot], out=ot[:], in_=a[:])
                s.dma('sp', dst.h[tb * 1024 + n * 128:tb * 1024 + (n + 1) * 128, :], ot[:], reads=[ot], writes=[dst])
                io += 1
        for tsb in range(TB // 128):
            for hd in range(4):
                pa = pps[it % 6]
                it += 1
                for kc in range(8):
                    I('pe', 'matmul', [Wkv, hkv], [pa], pa[:, 0:256], lhsT=hkv[:, kc, tsb * 128:(tsb + 1) * 128],
                      rhs=Wkv[:, kc, 1024 + hd * 256:1024 + (hd + 1) * 256], start=(kc == 0), stop=(kc == 7))
                vo = vouts[io % 4]
                io += 1
                I('act', 'copy', [], [pa, vo], out=vo[:], in_=pa[:, 0:256])
                r0 = (tb * 4 + hd) * TB + tsb * 128
                s.dma('sp', V3.h[r0:r0 + 128, :], vo[:], reads=[vo], writes=[V3])
        if after_tb is not None:
            after_tb(tb)


def phase_attn(s, C, G3q, G3k, G3v, lamT_d, O4, lam_init, S=16384, IDX_QK=260, IDX_V=324, after_block=None):
    I = s.I
    scale = 128 ** -0.5
    NKB = S // 128
    NQB = S // 512
    ones_bf = C['ones_bf']
    ones_f = s.sb('ones_f', [128, 128], F32)
    I('pool', 'memset', [], [ones_f], ones_f[:], 1.0)
    tri = C['tri']
    idx = C['idx']
    ident = C['ident']
    gq = G3q.h[:, :]
    gk = G3k.h[:, :]
    misc = s.ps('misc', [128, 512])
    lp = s.sb('lp', [128, 4], F32)
    s.dma('sp', lp[:], lamT_d, writes=[lp])
    pr = s.sb('pr', [128, 2], F32)
    I('dve', 'tensor_tensor', [lp], [pr], out=pr[:, 0:1], in0=lp[:, 0:1], in1=lp[:, 1:2], op=ALU.mult)
    I('dve', 'tensor_tensor', [lp], [pr], out=pr[:, 1:2], in0=lp[:, 2:3], in1=lp[:, 3:4], op=ALU.mult)
    I('pe', 'matmul', [ones_f, pr], [misc], misc[:, 0:2], lhsT=ones_f[:], rhs=pr[:], start=True, stop=True)
    ex = s.sb('ex', [128, 2], F32)
    I('act', 'activation', [], [misc, ex], out=ex[:], in_=misc[:, 0:2], func=AF.Exp)
    nlam = s.sb('nlam', [128, 1], F32)
    I('dve', 'tensor_tensor', [ex], [nlam], out=nlam[:], in0=ex[:, 1:2], in1=ex[:, 0:1], op=ALU.subtract)
    I('dve', 'tensor_scalar', [nlam], [nlam], out=nlam[:], in0=nlam[:], scalar1=-lam_init, scalar2=None, op0=ALU.add)
    KbB = [[s.sb(f'Kb{m}_{j}', [128, 512], BF16) for j in range(NQB)] for m in range(2)]
    VbB = [s.sb(f'Vb{j}', [128, 4, 257], BF16) for j in range(NQB)]
    sqs = [s.sb('sqs', [128, 512], BF16) for _ in range(2)]
    kmx = [s.sb(f'kmx{m}', [128, NQB], F32) for m in range(2)]
    krun = [s.sb(f'krun{m}', [128, NQB], F32) for m in range(2)]
    lb_i = [0]

    def load_block(j):
        for m in range(2):
            sq = sqs[lb_i[0] % 2]
            lb_i[0] += 1
            col = IDX_QK + m * 32 + j
            kb_ = KbB[m][j]
            s.gather(kb_[:], gk, idx[:, col:col + 1], [G3k, idx], [kb_])
            I('act', 'activation', [kb_], [sq], out=sq[:], in_=kb_[:], func=AF.Square)
            I('pe', 'matmul', [ones_bf, sq], [misc], misc[:, 0:512], lhsT=ones_bf[:], rhs=sq[:], start=True, stop=True)
            I('dve', 'reduce_max', [], [misc, kmx[m]], out=kmx[m][:, j:j + 1], in_=misc[:, 0:512], axis=AX.X)
            if j == 0:
                I('dve', 'tensor_copy', [kmx[m]], [krun[m]], out=krun[m][:, 0:1], in_=kmx[m][:, 0:1])
            else:
                I('dve', 'tensor_tensor', [kmx[m], krun[m]], [krun[m]], out=krun[m][:, j:j + 1], in0=krun[m][:, j - 1:j],
                  in1=kmx[m][:, j:j + 1], op=ALU.max)
        vb_ = VbB[j]
        I('pool', 'memset', [], [vb_], vb_[:, :, 256:257], 1.0)
        for q4 in range(4):
            kb = 4 * j + q4
            s.gather(vb_[:, q4, 0:256], G3v.h[:, :], idx[:, IDX_V + kb:IDX_V + kb + 1], [G3v, idx], [vb_])

    load_block(0)
    acc = [s.ps(f'acc{j}', [128, 512]) for j in range(4)]
    stp = [s.ps(f'st{j}', [128, 512]) for j in range(3)]
    qbf = [s.sb('qbf', [128, 512], BF16) for _ in range(2)]
    PT = [s.sb('PT', [128, 512], BF16) for _ in range(3)]
    mq = [s.sb('mq', [128, 1], F32) for _ in range(2)]
    bias = [s.sb('bias', [128, 1], F32) for _ in range(2)]
    o1 = s.sb('o1', [128, 4, 256], F32)
    rl = [s.sb('rl', [128, 1], F32) for _ in range(4)]
    ot = [s.sb('ot', [128, 256], F32) for _ in range(4)]
    oT_st = [s.sb('oTs', [128, 256], F32) for _ in range(2)]
    it = 0
    qi = 0
    for qb in range(NQB):
        if qb + 1 < NQB:
            load_block(qb + 1)
        for m in range(2):
            qb_ = qbf[qi % 2]
            sq = sqs[qi % 2]
            mq_ = mq[qi % 2]
            bi = bias[qi % 2]
            qi += 1
            col = IDX_QK + m * 32 + qb
            s.gather(qb_[:], gq, idx[:, col:col + 1], [G3q, idx], [qb_])
            I('act', 'activation', [qb_], [sq], out=sq[:], in_=qb_[:], func=AF.Square)
            I('pe', 'matmul', [ones_bf, sq], [misc], misc[:, 0:512], lhsT=ones_bf[:], rhs=sq[:], start=True, stop=True)
            I('dve', 'reduce_max', [], [misc, mq_], out=mq_[:], in_=misc[:, 0:512], axis=AX.X)
            I('dve', 'tensor_tensor', [mq_, krun[m]], [mq_], out=mq_[:], in0=mq_[:], in1=krun[m][:, qb:qb + 1], op=ALU.mult)
            I('act', 'activation', [mq_], [mq_], out=mq_[:], in_=mq_[:], func=AF.Sqrt)
            I('dve', 'tensor_scalar', [mq_], [bi], out=bi[:], in0=mq_[:], scalar1=-scale, scalar2=None, op0=ALU.mult)
            nkb = 4 * qb + 4
            LOOK = 2
            slots = {}

            def issue_st(kb):
                nonlocal it
                r = kb - 4 * qb
                c0 = 128 * max(r, 0)
                sp_ = stp[it % 3]
                pt = PT[it % 3]
                it += 1
                slots[kb] = (sp_, pt, r, c0)
                kbuf = KbB[m][kb // 4]
                I('pe', 'matmul', [kbuf, qb_], [sp_], sp_[:, c0:512], lhsT=kbuf[:, (kb % 4) * 128:(kb % 4 + 1) * 128], rhs=qb_[:, c0:512],
                  start=True, stop=True)

            for kb in range(min(LOOK, nkb)):
                issue_st(kb)
            for kb in range(nkb):
                sp_, pt, r, c0 = slots.pop(kb)
                I('act', 'activation', [bi], [sp_, pt], out=pt[:, c0:512], in_=sp_[:, c0:512], func=AF.Exp, bias=bi[:, 0:1], scale=scale)
                if r >= 0:
                    I('pool', 'tensor_tensor', [pt, tri], [pt], out=pt[:, c0:c0 + 128], in0=pt[:, c0:c0 + 128], in1=tri[:], op=ALU.mult)
                if kb + LOOK < nkb:
                    issue_st(kb + LOOK)
                for j in range(max(r, 0), 4):
                    vbuf = VbB[kb // 4]
                    I('pe', 'matmul', [pt, vbuf], [acc[j]], acc[j][:, 0:257], lhsT=pt[:, 128 * j:128 * (j + 1)], rhs=vbuf[:, kb % 4, :],
                      start=(kb == 0), stop=(kb == 4 * qb + j))
            for j in range(4):
                a = acc[j]
                I('dve', 'reciprocal', [], [a, rl[j]], out=rl[j][:], in_=a[:, 256:257])
                if m == 0:
                    I('act', 'mul', [rl[j]], [a, o1], out=o1[:, j, :], in_=a[:, 0:256], mul=rl[j][:, 0:1])
                else:
                    I('dve', 'tensor_tensor', [rl[j], nlam], [rl[j]], out=rl[j][:], in0=rl[j][:], in1=nlam[:], op=ALU.mult)
                    I('dve', 'scalar_tensor_tensor', [rl[j], o1], [a, ot[j]], out=ot[j][:], in0=a[:, 0:256], scalar=rl[j][:, 0:1],
                      in1=o1[:, j, :], op0=ALU.mult, op1=ALU.add)
                    for cch in range(2):
                        I('pe', 'transpose', [ot[j], ident], [misc], misc[:, cch * 128:(cch + 1) * 128],
                          ot[j][:, cch * 128:(cch + 1) * 128], ident[:])
                    oTs = oT_st[(qb * 4 + j) % 2]
                    I('act', 'copy', [], [misc, oTs], out=oTs[:], in_=misc[:, 0:256])
                    tq = qb // 8
                    c0 = (qb % 8) * 512 + j * 128
                    for cch in range(2):
                        r0 = tq * 256 + cch * 128
                        s.dma('sp', O4.h[r0:r0 + 128, c0:c0 + 128], oTs[:, cch * 128:(cch + 1) * 128], reads=[oTs], writes=[O4])
                    if after_block is not None and qb % 8 == 7 and j == 3:
                        after_block(qb // 8)

import math
from concourse.bass_utils import run_bass_kernel_spmd

NCORES = 8
SEQ = 16384
TPC = 4096
LAM_INIT1 = 0.8 - 0.6 * math.exp(-0.3 * 1)
ROT_DIM = 32
ROPE_THETA = 500000.0
NIDX = 516
IDX_G1, IDX_AB, IDX_G2, IDX_QK, IDX_V, IDX_G4 = 0, 192, 196, 260, 324, 452
GROUPS = [[0, 1, 2, 3], [4, 5, 6, 7]]

IN_SPECS = {
    "xT": ((1024, TPC), F32), "cT": ((128, 8, 1), F32), "w0": ((1024, 6144), F32), "w1": ((1024, 6144), F32),
    "w2": ((1024, 2048), F32), "mb": ((128, 112), F32), "gT0": ((128, 8), F32), "w_in": ((1024, 4112), F32),
    "cw": ((128, 24), F32), "hp": ((128, 4), F32), "ident": ((128, 128), F32), "negTi": ((128, 128), F32),
    "negTs": ((128, 128), F32), "tri": ((128, 128), F32), "idx": ((128, NIDX), U32), "og_a": ((128, 1), F32),
    "a_w_out": ((1024, 1024), F32), "gm0": ((128, 8), F32), "m1_0": ((1024, 4096), F32), "m2_0": ((4096, 1024), F32),
    "gkv": ((128, 8), F32), "gq": ((128, 8), F32), "kv_w": ((1024, 2048), F32), "kv_wp": ((1024, 1024), F32),
    "wq": ((1024, 1024), F32), "wqp": ((1024, 1024), F32), "ropeC": ((128, TPC), F32), "ropeS": ((128, TPC), F32),
    "lamT": ((128, 4), F32), "og_b": ((128, 2), F32), "b_w_out": ((1024, 1024), F32), "gm1": ((128, 8), F32),
    "m1_1": ((1024, 4096), F32), "m2_1": ((4096, 1024), F32), "fg": ((128, 8), F32),
}


def build_program(stop_after=99):
    nc = bass.Bass("TRN2", target_bir_lowering=False)
    d = {k: nc.dram_tensor(k, list(shp), dt, kind="ExternalInput").ap() for k, (shp, dt) in IN_SPECS.items()}
    outT = nc.dram_tensor("outT", [1024, TPC], F32, kind="ExternalOutput").ap()
    s = Sched(nc)

    def scratch(name, shape, dt=F32):
        return s.dram(nc.dram_tensor(name, list(shape), dt), name)

    P1 = scratch("P1", (8 * 3136, 512))
    Z1 = scratch("Z1", (1024, TPC))
    G1 = scratch("G1", (4 * 8 * 3136, 512))
    O2 = scratch("O2", (1024, TPC))
    G2 = scratch("G2", (4096, TPC))
    XM = scratch("XM", (1024, TPC))
    X1 = scratch("X1", (1024, TPC))
    Q3 = scratch("Q3", (8 * 1024, 512), BF16)
    K3 = scratch("K3", (8 * 1024, 512), BF16)
    V3 = scratch("V3", (4 * TPC, 256), BF16)
    G3q = scratch("G3q", (4 * 8 * 1024, 512), BF16)
    G3k = scratch("G3k", (4 * 8 * 1024, 512), BF16)
    G3v = scratch("G3v", (16 * TPC, 256), BF16)
    O4 = scratch("O4", (1024, TPC))
    G4 = scratch("G4", (4096, TPC))
    XM2 = scratch("XM2", (1024, TPC))

    C = consts(s)
    gdn_consts(s, C, d["ident"], d["negTi"], d["negTs"])
    trif = s.sb('trif', [128, 128], F32)
    s.dma('sp', trif[:], d["tri"], writes=[trif])
    C['tri'] = s.sb('tri', [128, 128], BF16)
    s.I('dve', 'tensor_copy', [trif], [C['tri']], out=C['tri'][:], in_=trif[:])
    C['idx'] = s.sb('idx', [128, NIDX], U32)
    s.dma('sp', C['idx'][:], d["idx"], writes=[C['idx']])
    C['modv'] = s.sb('modvg', [128, 112], F32)

    def ag(src, dst):
        R, Fc = src.h.shape
        esz = 2 if src.h.dtype == BF16 else 4
        RS = (1 << 20) // (Fc * esz)
        assert R % RS == 0
        for k in range(R // RS):
            ag_slice(src, dst, k, RS)

    def ag_slice(src, dst, k, RS):
        s.allgather(src.h[k * RS:(k + 1) * RS, :], dst.h[k * 4 * RS:(k + 1) * 4 * RS, :], src, dst, GROUPS)

    def p1_after_tb(tb):
        lo = ((tb * 3136 + 511) // 512) if tb > 0 else 0
        for k in range(49):
            if (512 * k + 511) // 3136 == tb:
                ag_slice(P1, G1, k, 512)

    def o2_after_block(blk):
        for k in (2 * blk, 2 * blk + 1):
            ag_slice(O2, G2, k, 64)

    def kvq_after_tb(tb):
        ag_slice(Q3, G3q, tb, 1024)
        ag_slice(K3, G3k, tb, 1024)
        ag_slice(V3, G3v, tb, 2048)

    def o4_after_block(tq):
        for k in range(4 * tq, 4 * tq + 4):
            ag_slice(O4, G4, k, 64)

    s.begin_phase()
    phase_mod(s, C, d["cT"], [d["w0"], d["w1"], d["w2"]], d["mb"])
    s.end_phase()
    if stop_after == 1:
        s.begin_phase()
        s.end_phase(final=True)
        return nc
    s.begin_phase()
    phase_l1(s, C, d["xT"], d["gT0"], d["w_in"], P1, Z1, after_tb=p1_after_tb)
    s.end_phase()
    if stop_after == 2:
        s.begin_phase()
        s.end_phase(final=True)
        return nc
    s.begin_phase()
    phase_gdn(s, C, G1, d["cw"], d["hp"], O2, S=SEQ, IDX_G1=IDX_G1, IDX_AB=IDX_AB, after_block=o2_after_block)
    s.end_phase()
    if stop_after == 3:
        s.begin_phase()
        s.end_phase(final=True)
        return nc
    s.begin_phase()
    phase_post(s, C, 'gdn', G2, IDX_G2, Z1, d["xT"], None, 16, d["og_a"], d["a_w_out"], XM)
    s.end_phase()
    if stop_after == 4:
        s.begin_phase()
        s.end_phase(final=True)
        return nc
    s.begin_phase()
    phase_mlp(s, C, XM, 24, d["gm0"], d["m1_0"], d["m2_0"], X1.h[:, :], XO=X1)
    s.end_phase()
    if stop_after == 5:
        s.begin_phase()
        s.end_phase(final=True)
        return nc
    s.begin_phase()
    phase_kvq(s, C, X1, d["gkv"], d["gq"], d["kv_w"], d["kv_wp"], d["wq"], d["wqp"], d["ropeC"], d["ropeS"], Q3, K3, V3, after_tb=kvq_after_tb)
    s.end_phase()
    if stop_after == 6:
        s.begin_phase()
        s.end_phase(final=True)
        return nc
    s.begin_phase()
    phase_attn(s, C, G3q, G3k, G3v, d["lamT"], O4, LAM_INIT1, S=SEQ, IDX_QK=IDX_QK, IDX_V=IDX_V, after_block=o4_after_block)
    s.end_phase()
    if stop_after == 7:
        s.begin_phase()
        s.end_phase(final=True)
        return nc
    s.begin_phase()
    phase_post(s, C, 'attn', G4, IDX_G4, None, X1.h[:, :], X1, 64, d["og_b"], d["b_w_out"], XM2, lam_init=LAM_INIT1)
    s.end_phase()
    if stop_after == 8:
        s.begin_phase()
        s.end_phase(final=True)
        return nc
    s.begin_phase()
    phase_mlp(s, C, XM2, 72, d["gm1"], d["m1_1"], d["m2_1"], outT, final_g_d=d["fg"])
    s.end_phase(final=True)
    return nc


def _fm8(v):
    return np.ascontiguousarray(np.asarray(v, np.float32).reshape(-1, 128).T)


def _perm_cols(w):
    w = np.asarray(w)
    out = w.copy()
    for c in range(w.shape[1] // 128):
        b = c * 128
        out[:, b:b + 16] = w[:, b + 16:b + 32]
        out[:, b + 16:b + 32] = w[:, b:b + 16]
    return out


def _rope_tables(t0, n):
    pos = (t0 + np.arange(n)).astype(np.float32)
    inv = (np.float32(ROPE_THETA) ** (-(np.arange(0, ROT_DIM, 2, dtype=np.float32)) / np.float32(ROT_DIM))).astype(np.float32)
    fr = (pos[:, None] * inv[None, :]).astype(np.float32)
    cos = np.cos(fr).astype(np.float32).T
    sin = np.sin(fr).astype(np.float32).T
    Cc = np.ones((128, n), np.float32)
    Ss = np.zeros((128, n), np.float32)
    Cc[0:16] = cos
    Cc[16:32] = cos
    Ss[0:16] = -sin
    Ss[16:32] = sin
    return Cc, Ss


def _grow(src, row, RS):
    return ((row // RS) * 4 + src) * RS + row % RS


def _idx_table(r):
    p = np.arange(128, dtype=np.int64)
    t = np.zeros((128, NIDX), np.int64)
    for hh in range(2):
        h = 2 * r + hh
        for X in range(3):
            for sb in range(32):
                t[:, IDX_G1 + (hh * 3 + X) * 32 + sb] = _grow(sb // 8, (sb % 8) * 3136 + X * 1024 + h * 128 + p, 512)
        t[:, IDX_AB + 2 * hh] = _grow(p // 32, ((p % 32) // 4) * 3136 + 3072 + h, 512) * 4 + p % 4
        t[:, IDX_AB + 2 * hh + 1] = _grow(p // 32, ((p % 32) // 4) * 3136 + 3080 + h, 512) * 4 + p % 4
    for kc in range(8):
        for tb in range(8):
            t[:, IDX_G2 + kc * 8 + tb] = _grow(kc // 2, ((kc % 2) * 4 + r) * 128 + p, 64) * 8 + tb
            t[:, IDX_G4 + kc * 8 + tb] = _grow(kc // 2, r * 256 + (kc % 2) * 128 + p, 64) * 8 + tb
    for m in range(2):
        for j in range(32):
            t[:, IDX_QK + m * 32 + j] = _grow(j // 8, (j % 8) * 1024 + r * 256 + m * 128 + p, 1024)
    for kb in range(128):
        t[:, IDX_V + kb] = _grow(kb // 32, (((kb % 32) // 4) * 4 + r) * 512 + ((kb % 32) % 4) * 128 + p, 2048)
    return t.astype(np.uint32)


def kernel(x, c, mod_w, mod_b, norm_mix_g, norm_mlp_g, a_w_in, a_conv_w, a_log, a_dt_bias, a_out_norm_g, a_w_out,
           kv_mod_w, kv_mod_b, kv_norm_g, kv_w, b_w_q, b_lambda, b_subln_g, b_w_out, mlp_w1, mlp_w2, final_g):
    f32 = np.float32
    x = np.asarray(x, f32)
    B, S, D = x.shape
    A = lambda v: np.asarray(v, f32)
    jj, ii = np.meshgrid(np.arange(128), np.arange(128), indexing='ij')
    shared = {
        "w0": A(mod_w[0]), "w1": A(mod_w[1]), "w2": A(kv_mod_w),
        "mb": np.ascontiguousarray(np.concatenate([A(mod_b[0]), A(mod_b[1]), A(kv_mod_b)]).reshape(112, 128).T),
        "gT0": _fm8(norm_mix_g[0]), "w_in": A(a_w_in[0]), "ident": np.eye(128, dtype=f32),
        "negTi": np.where(ii >= jj, 0.0, -30000.0).astype(f32), "negTs": np.where(ii > jj, 0.0, -30000.0).astype(f32),
        "tri": (np.arange(128)[:, None] <= np.arange(128)[None, :]).astype(f32),
        "og_a": A(a_out_norm_g[0]).reshape(128, 1), "a_w_out": A(a_w_out[0]), "gm0": _fm8(norm_mlp_g[0]),
        "m1_0": A(mlp_w1[0]), "m2_0": A(mlp_w2[0]), "gkv": _fm8(kv_norm_g), "gq": _fm8(norm_mix_g[1]),
        "kv_w": A(kv_w), "kv_wp": _perm_cols(A(kv_w)[:, :1024]), "wq": A(b_w_q[0]), "wqp": _perm_cols(A(b_w_q[0])),
        "lamT": np.ascontiguousarray(A(b_lambda[0]).T), "og_b": np.ascontiguousarray(A(b_subln_g[0]).reshape(2, 128).T),
        "b_w_out": A(b_w_out[0]), "gm1": _fm8(norm_mlp_g[1]), "m1_1": A(mlp_w1[1]), "m2_1": A(mlp_w2[1]), "fg": _fm8(final_g),
    }
    maps = []
    for cc in range(NCORES):
        b, r = cc // 4, cc % 4
        t0 = r * TPC
        m = dict(shared)
        m["xT"] = np.ascontiguousarray(x[b, t0:t0 + TPC].T)
        m["cT"] = np.ascontiguousarray(A(c)[b].reshape(8, 128).T.reshape(128, 8, 1))
        cw = np.empty((128, 24), f32)
        hp = np.empty((128, 4), f32)
        for hh in range(2):
            h = 2 * r + hh
            for X in range(3):
                r0 = X * 1024 + h * 128
                for j in range(4):
                    cw[:, (hh * 3 + X) * 4 + j] = A(a_conv_w[0])[j, r0:r0 + 128]
            hp[:, 2 * hh] = A(a_log[0])[h]
            hp[:, 2 * hh + 1] = A(a_dt_bias[0])[h]
        m["cw"], m["hp"] = cw, hp
        m["ropeC"], m["ropeS"] = _rope_tables(t0, TPC)
        m["idx"] = _idx_table(r)
        maps.append({k: np.ascontiguousarray(m[k]) for k in IN_SPECS})
    nc = build_program()
    res = run_bass_kernel_spmd(nc, maps, core_ids=list(range(NCORES)))
    out = np.empty((B, S, D), f32)
    for cc in range(NCORES):
        b, r = cc // 4, cc % 4
        out[b, r * TPC:(r + 1) * TPC] = np.asarray(res.results[cc]["outT"]).T
    return out
```

```python
import contextlib
from collections import defaultdict
import numpy as np
import concourse.bass as bass
import concourse.mybir as mybir

F32 = mybir.dt.float32
BF16 = mybir.dt.bfloat16
U32 = mybir.dt.uint32
AF = mybir.ActivationFunctionType
ALU = mybir.AluOpType
AX = mybir.AxisListType

ENGS = ['sp', 'act', 'dve', 'pool', 'pe']


class Buf:
    def __init__(self, h, name, tr=None, multi=False):
        self.h = h
        self.name = name
        self.multi = multi
        self.tr = tr if tr is not None else {'lw': None, 'rd': {}, 'ws': {}}

    def __getitem__(self, k):
        return self.h[k]

    def view(self, pattern, **kw):
        return Buf(self.h[:].rearrange(pattern, **kw), self.name, self.tr, self.multi)


class Sched:
    def __init__(self, nc):
        self.nc = nc
        self.es = contextlib.ExitStack()
        self.ops = {e: [] for e in ENGS}
        self.val = defaultdict(int)
        self.known = {e: defaultdict(int) for e in ENGS}
        self.sem = {}
        for e in ENGS:
            self.sem['c_' + e] = self.es.enter_context(nc.semaphore('c_' + e))
        self.NS = 8
        self.ndma = defaultdict(int)
        for e in ['sp', 'act', 'pool']:
            for k in range(self.NS):
                nm = f'd_{e}_{k}'
                self.sem[nm] = self.es.enter_context(nc.semaphore(nm))
        self.nbuf = 0
        self.pes = None
        self.sem['cc'] = self.es.enter_context(nc.semaphore('cc'))

    def begin_phase(self):
        self.pes = contextlib.ExitStack()
        if hasattr(self, '_stage'):
            del self._stage

    def end_phase(self, final=False):
        allc = [(c, v) for c, v in self.val.items() if v > 0]
        for eng in ENGS:
            waits = []
            for c, v in allc:
                if self.known[eng][c] < v:
                    waits.append((c, v))
                    self.known[eng][c] = v
            self.ops[eng].append((waits, None, None, 0))
        self._emit_block()
        if self.pes is not None:
            self.pes.close()
            self.pes = None
        if final:
            self.es.close()

    def _stack(self):
        return self.pes if self.pes is not None else self.es

    def sb(self, name, shape, dt):
        self.nbuf += 1
        return Buf(self._stack().enter_context(self.nc.sbuf_tensor(f"{name}_{self.nbuf}", list(shape), dt)), name)

    def ps(self, name, shape, dt=F32):
        self.nbuf += 1
        return Buf(self._stack().enter_context(self.nc.psum_tensor(f"{name}_{self.nbuf}", list(shape), dt)), name)

    def dram(self, h, name):
        return Buf(h, name, multi=True)

    def op(self, eng, fn, reads=(), writes=(), dma=False, cc=False):
        deps = {}
        if cc:
            ctr = 'cc'
        elif dma:
            k = self.ndma[eng] % self.NS
            self.ndma[eng] += 1
            ctr = f'd_{eng}_{k}'
            if self.val[ctr] > 0:
                deps[ctr] = self.val[ctr]
        else:
            ctr = 'c_' + eng
        for b in reads:
            lw = b.tr['lw']
            if lw is not None:
                deps[lw[0]] = max(deps.get(lw[0], 0), lw[1])
            if b.multi:
                for c, v in b.tr['ws'].items():
                    deps[c] = max(deps.get(c, 0), v)
        for b in writes:
            if b.multi:
                continue
            lw = b.tr['lw']
            if lw is not None:
                deps[lw[0]] = max(deps.get(lw[0], 0), lw[1])
            for c, v in b.tr['rd'].items():
                deps[c] = max(deps.get(c, 0), v)
        waits = []
        for c, v in deps.items():
            if eng == 'pe' and c == 'c_pe':
                continue
            if self.known[eng][c] < v:
                waits.append((c, v))
                self.known[eng][c] = v
        inc = 1 if cc else (16 if dma else 1)
        self.val[ctr] += inc
        v = self.val[ctr]
        self.ops[eng].append((waits, fn, ctr, inc))
        for b in reads:
            b.tr['rd'][ctr] = max(b.tr['rd'].get(ctr, 0), v)
        for b in writes:
            if b.multi:
                b.tr['ws'][ctr] = max(b.tr['ws'].get(ctr, 0), v)
                continue
            b.tr['lw'] = (ctr, v)
            b.tr['rd'] = {}

    def I(self, eng, method, reads, writes, *a, **kw):
        self.op(eng, lambda e: getattr(e, method)(*a, **kw), reads, writes)

    def dma(self, eng, out_ap, in_ap, reads=(), writes=(), **kw):
        self.op(eng, lambda e: e.dma_start(out=out_ap, in_=in_ap, **kw), reads, writes, dma=True)

    def allgather(self, in_ap, out_ap, in_buf, out_buf, groups):
        self.op('pool', lambda e: e.collective_compute("AllGather", ALU.bypass, replica_groups=groups,
                                                       ins=[in_ap.opt()], outs=[out_ap.opt()]),
                reads=[in_buf], writes=[out_buf], cc=True)

    def gather(self, out_ap, in_view, idx_ap, reads, writes):
        self.op('pool', lambda e: e.indirect_dma_start(out=out_ap, out_offset=None, in_=in_view,
                                                       in_offset=bass.IndirectOffsetOnAxis(ap=idx_ap, axis=0)),
                reads, writes, dma=True)

    def _emit_block(self):
        nc = self.nc
        with nc.Block() as block:
            decos = {'sp': block.sync, 'act': block.scalar, 'dve': block.vector,
                     'pool': block.gpsimd, 'pe': block.tensor}
            for eng in ENGS:
                ops = self.ops[eng]

                def body(e, ops=ops):
                    for waits, fn, ctr, inc in ops:
                        for c, v in waits:
                            e.wait_ge(self.sem[c], v)
                        if fn is not None:
                            ins = fn(e)
                            if inc == 1 and ctr == 'cc':
                                ins.then_inc(self.sem[ctr])
                            else:
                                ins.then_inc(self.sem[ctr], inc)
                decos[eng](body)
        self.nops_total = getattr(self, 'nops_total', 0) + sum(len(v) for v in self.ops.values())
        self.ops = {e: [] for e in ENGS}

    def emit(self):
        self.end_phase(final=True)

    def n_ops(self):
        return {e: len(self.ops[e]) for e in ENGS}


EPS = 1e-6


def consts(s):
    c = {}
    c['ones_bf'] = s.sb('ones_bf', [128, 128], BF16)
    s.op('pool', lambda e: e.memset(c['ones_bf'][:], 1.0), writes=[c['ones_bf']])
    return c


def phase_mod(s, C, cT_d, ws, modbT_d):
    NJ = sum(w.shape[1] for w in ws) // 128
    cs = s.sb('cs', [128, 8, 1], F32)
    s.dma('sp', cs[:], cT_d, writes=[cs])
    s.op('act', lambda e: e.activation(out=cs[:], in_=cs[:], func=AF.Silu), reads=[cs], writes=[cs])
    mb = s.sb('mb', [128, NJ], F32)
    s.dma('sp', mb[:], modbT_d, writes=[mb])
    ps = s.ps('modps', [128, 512])
    st = [s.sb('mst', [128, 8, 512], F32) for _ in range(2)]
    j = 0
    k = 0
    for w in ws:
        wv = w.rearrange("(kc p) n -> p kc n", p=128)
        for n0 in range(0, w.shape[1], 512):
            b = st[k % 2]
            s.dma('sp' if k % 2 == 0 else 'pool', b[:], wv[:, :, n0:n0 + 512], writes=[b])
            k += 1
            for jj in range(4):
                for kc in range(8):
                    s.I('pe', 'matmul', [b, cs], [ps], ps[:, j:j + 1], lhsT=b[:, kc, jj * 128:(jj + 1) * 128], rhs=cs[:, kc, :],
                        start=(kc == 0), stop=(kc == 7))
                j += 1
    modv = C['modv']
    s.I('dve', 'tensor_tensor', [mb], [ps, modv], out=modv[:], in0=ps[:, 0:NJ], in1=mb[:], op=ALU.add)


def get_stage(s):
    if not hasattr(s, '_stage'):
        s._stage = [s.sb('wst', [128, 2048], F32) for _ in range(2)]
        s._stage_i = 0
    return s._stage


def load_cast_w(s, Wb, w_d, piece=512, engs=('dve', 'pool')):
    K, N = w_d.shape
    KC = K // 128
    nk = 2048 // piece
    wv = w_d.rearrange("(kc p) n -> p kc n", p=128)
    st = get_stage(s)
    for k0 in range(0, KC, nk):
        kw = min(nk, KC - k0)
        for n0 in range(0, N, piece):
            nw = min(piece, N - n0)
            i = s._stage_i
            s._stage_i += 1
            b = st[i % 2]
            bv = b[:].rearrange("p (k n) -> p k n", n=piece)
            s.dma('sp' if i % 2 == 0 else 'pool', bv[:, 0:kw, 0:nw], wv[:, k0:k0 + kw, n0:n0 + nw], writes=[b])
            eng = engs[i % len(engs)]
            s.I(eng, 'tensor_copy', [b], [Wb], out=Wb[:, k0:k0 + kw, n0:n0 + nw], in_=bv[:, 0:kw, 0:nw])


def norm_mod(s, C, xs, G, SH, h, tmp, sq, ss_ps, rstd, D=1024, TB=512):
    KC = D // 128
    s.op('act', lambda e: e.activation(out=sq[:], in_=xs[:], func=AF.Square), reads=[xs], writes=[sq])
    for kc in range(KC):
        s.op('pe', lambda e, kc=kc: e.matmul(ss_ps[:, 0:TB], lhsT=C['ones_bf'][:], rhs=sq[:, kc, :],
                                              start=(kc == 0), stop=(kc == KC - 1)),
             reads=[sq, C['ones_bf']], writes=[ss_ps])
    s.op('dve', lambda e: e.tensor_scalar(out=rstd[:], in0=ss_ps[:, 0:TB], scalar1=1.0 / D, scalar2=EPS,
                                          op0=ALU.mult, op1=ALU.add), reads=[ss_ps], writes=[rstd])
    s.op('dve', lambda e: e.reciprocal(out=rstd[:], in_=rstd[:]), reads=[rstd], writes=[rstd])
    s.op('act', lambda e: e.activation(out=rstd[:], in_=rstd[:], func=AF.Sqrt), reads=[rstd], writes=[rstd])
    for kc in range(KC):
        t = tmp[kc % len(tmp)]
        s.op('dve', lambda e, kc=kc, t=t: e.scalar_tensor_tensor(
            out=t[:], in0=xs[:, kc, :], scalar=G[0][:, G[1] + kc:G[1] + kc + 1], in1=rstd[:],
            op0=ALU.mult, op1=ALU.mult), reads=[xs, G[0], rstd], writes=[t])
        s.op('act', lambda e, kc=kc, t=t: e.activation(
            out=h[:, kc, :], in_=t[:], func=AF.Identity, bias=SH[0][:, SH[1] + kc:SH[1] + kc + 1], scale=1.0),
             reads=[t, SH[0]], writes=[h])


def phase_l1(s, C, xT_d, gT_d, w_in_d, P1, Z1, after_tb=None, NT=4096, TB=512):
    NOUT = w_in_d.shape[1]
    modv = C['modv']
    gT = s.sb('gT', [128, 8], F32)
    s.dma('sp', gT[:], gT_d, writes=[gT])
    G = s.sb('G', [128, 8], F32)
    s.op('dve', lambda e: e.scalar_tensor_tensor(out=G[:], in0=modv[:, 8:16], scalar=1.0, in1=gT[:],
                                                 op0=ALU.add, op1=ALU.mult), reads=[modv, gT], writes=[G])
    Wb = s.sb('Wb', [128, 8, NOUT], BF16)
    load_cast_w(s, Wb, w_in_d)
    xs = [s.sb('xs', [128, 8, TB], F32) for _ in range(2)]
    sq = s.sb('sq', [128, 8, TB], BF16)
    hs = [s.sb('h', [128, 8, TB], BF16) for _ in range(2)]
    tmp = [s.sb('tmp', [128, TB], F32) for _ in range(2)]
    rstd = s.sb('rstd', [128, TB], F32)
    ss_ps = s.ps('ss', [128, 512])
    pps = [s.ps('pp', [128, 512]) for _ in range(4)]
    outs = [s.sb('ost', [128, TB], F32) for _ in range(4)]
    xv = xT_d.rearrange("(kc p) t -> p kc t", p=128)
    nchunks = [(n0, min(128, NOUT - n0)) for n0 in range(0, NOUT, 128)]
    it = 0
    for tb in range(NT // TB):
        x = xs[tb % 2]
        h = hs[tb % 2]
        s.dma('sp', x[:], xv[:, :, tb * TB:(tb + 1) * TB], writes=[x])
        norm_mod(s, C, x, (G, 0), (modv, 0), h, tmp, sq, ss_ps, rstd)
        for (n0, nw) in nchunks:
            pp = pps[it % 4]
            o = outs[it % 4]
            for kc in range(8):
                s.op('pe', lambda e, pp=pp, kc=kc, n0=n0, nw=nw, h=h: e.matmul(
                    pp[0:nw, 0:TB], lhsT=Wb[:, kc, n0:n0 + nw], rhs=h[:, kc, :], start=(kc == 0), stop=(kc == 7)),
                     reads=[Wb, h], writes=[pp])
            if it % 2 == 0:
                s.op('act', lambda e, pp=pp, o=o, nw=nw: e.copy(out=o[0:nw, :], in_=pp[0:nw, 0:TB]),
                     reads=[pp], writes=[o])
            else:
                s.op('dve', lambda e, pp=pp, o=o, nw=nw: e.tensor_copy(out=o[0:nw, :], in_=pp[0:nw, 0:TB]),
                     reads=[pp], writes=[o])
            if n0 < 3072:
                dst, db = P1.h[tb * 3136 + n0:tb * 3136 + n0 + nw, :], P1
            elif n0 < 4096:
                dst, db = Z1.h[n0 - 3072:n0 - 3072 + nw, tb * TB:(tb + 1) * TB], Z1
            else:
                dst, db = P1.h[tb * 3136 + 3072:tb * 3136 + 3072 + nw, :], P1
            s.dma('sp', dst, o[0:nw, :], reads=[o], writes=[db])
            it += 1
        if after_tb is not None:
            after_tb(tb)


L2EPS = 1e-6


def gdn_consts(s, C, ident_d, negTi_d, negTs_d):
    for nm, d in (('ident', ident_d), ('negTi', negTi_d), ('negTs', negTs_d)):
        C[nm] = s.sb(nm, [128, 128], F32)
        s.dma('sp', C[nm][:], d, writes=[C[nm]])
        C[nm + '_bf'] = s.sb(nm + '_bf', [128, 128], BF16)
        s.I('dve', 'tensor_copy', [C[nm]], [C[nm + '_bf']], out=C[nm + '_bf'][:], in_=C[nm][:])


class PsumPool:
    def __init__(self, s, nbanks, name):
        self.banks = [s.ps(f'{name}{i}', [128, 512]) for i in range(nbanks)]
        self.i = 0

    def get(self):
        b = self.banks[(self.i // 4) % len(self.banks)]
        sl = self.i % 4
        self.i += 1
        return b, b[:, sl * 128:(sl + 1) * 128]

    def get2(self):
        if self.i % 2:
            self.i += 1
        b = self.banks[(self.i // 4) % len(self.banks)]
        sl = self.i % 4
        self.i += 2
        return b, b[:, sl * 128:(sl + 2) * 128]

    def get_bank(self):
        self.i = ((self.i + 3) // 4) * 4
        b = self.banks[(self.i // 4) % len(self.banks)]
        self.i += 4
        return b


def phase_gdn(s, C, G1, cw_d, hp_d, O2, S=16384, NH=2, IDX_G1=0, IDX_AB=192, after_block=None):
    SB = 512
    NSB = S // SB
    scale = 128 ** -0.5
    ident = C['ident']
    idx = C['idx']
    g1v512 = G1.h[:, :]
    I = s.I
    cw = s.sb('cw', [128, NH * 3 * 4], F32)
    s.dma('sp', cw[:], cw_d, writes=[cw])
    hp = s.sb('hp', [128, NH * 2], F32)
    s.dma('sp', hp[:], hp_d, writes=[hp])
    pp = PsumPool(s, 6, 'lv')
    seqbank = [s.ps(f'seq{h}', [128, 512]) for h in range(NH)]
    bc127 = ident[:, 127:128].to_broadcast([128, 128])
    import math
    lnsc = s.sb('lnsc', [128, 1], F32)
    I('pool', 'memset', [], [lnsc], lnsc[:], math.log(scale))
    lnscale_c = lnsc[:, 0:1]

    tabs = []
    for h in range(NH):
        T = {}
        for nm in ('a', 'b', 'x', 't', 'gcum', 'rtab', 'gcumT', 'betaT', 'negg', 'be', 'kdf', 'glc', 'ones'):
            T[nm] = s.sb(f'{nm}{h}', [128, 128], F32)
        T['nA'] = s.sb(f'nA{h}', [128, 1], F32)
        a, b, x, t, nA = T['a'], T['b'], T['x'], T['t'], T['nA']
        g1v128 = G1.h[:, :].rearrange("r (b w) -> (r b) w", w=128)
        s.gather(a[:], g1v128, idx[:, IDX_AB + 2 * h:IDX_AB + 2 * h + 1], [G1, idx], [a])
        s.gather(b[:], g1v128, idx[:, IDX_AB + 2 * h + 1:IDX_AB + 2 * h + 2], [G1, idx], [b])
        I('pool', 'memset', [], [T['ones']], T['ones'][:], 1.0)
        I('dve', 'tensor_scalar', [a, hp], [x], out=x[:], in0=a[:], scalar1=hp[:, 2 * h + 1:2 * h + 2], scalar2=None, op0=ALU.add)
        I('act', 'activation', [x], [t], out=t[:], in_=x[:], func=AF.Abs)
        I('act', 'activation', [t], [t], out=t[:], in_=t[:], func=AF.Exp, scale=-1.0)
        I('act', 'activation', [t], [t], out=t[:], in_=t[:], func=AF.Ln, bias=1.0)
        I('dve', 'tensor_scalar_max', [x], [x], out=x[:], in0=x[:], scalar1=0.0)
        I('dve', 'tensor_tensor', [x, t], [x], out=x[:], in0=x[:], in1=t[:], op=ALU.add)
        I('act', 'activation', [hp], [nA], out=nA[:], in_=hp[:, 2 * h:2 * h + 1], func=AF.Exp)
        I('dve', 'tensor_scalar', [nA], [nA], out=nA[:], in0=nA[:], scalar1=-1.0, scalar2=None, op0=ALU.mult)
        I('dve', 'tensor_scalar', [x, nA], [x], out=x[:], in0=x[:], scalar1=nA[:, 0:1], scalar2=None, op0=ALU.mult)
        I('dve', 'tensor_tensor_scan', [T['ones'], x], [T['gcum']], out=T['gcum'][:], data0=T['ones'][:], data1=x[:],
          initial=0.0, op0=ALU.mult, op1=ALU.add)
        I('act', 'activation', [b], [b], out=b[:], in_=b[:], func=AF.Sigmoid)
        I('act', 'activation', [b], [t], out=t[:], in_=b[:], func=AF.Ln)
        I('dve', 'tensor_tensor', [T['gcum'], t], [T['rtab']], out=T['rtab'][:], in0=T['gcum'][:], in1=t[:], op=ALU.add)
        bk, p1 = pp.get()
        I('pe', 'transpose', [T['gcum'], ident], [bk], p1, T['gcum'][:], ident[:])
        I('act', 'copy', [], [bk, T['gcumT']], out=T['gcumT'][:], in_=p1)
        bk2, p2 = pp.get()
        I('pe', 'transpose', [b, ident], [bk2], p2, b[:], ident[:])
        I('act', 'copy', [], [bk2, T['betaT']], out=T['betaT'][:], in_=p2)
        bk3, p3 = pp.get()
        I('pe', 'matmul', [ident, T['gcumT']], [bk3], p3, lhsT=bc127, rhs=T['gcumT'][:], start=True, stop=True)
        I('act', 'activation', [], [bk3, T['glc']], out=T['glc'][:], in_=p3, func=AF.Exp)
        I('dve', 'tensor_tensor', [T['gcumT']], [bk3, T['kdf']], out=T['kdf'][:], in0=p3, in1=T['gcumT'][:], op=ALU.subtract)
        I('act', 'activation', [T['kdf']], [T['kdf']], out=T['kdf'][:], in_=T['kdf'][:], func=AF.Exp)
        I('dve', 'tensor_scalar', [T['gcumT']], [T['negg']], out=T['negg'][:], in0=T['gcumT'][:], scalar1=-1.0, scalar2=None, op0=ALU.mult)
        I('act', 'activation', [T['gcumT']], [T['be']], out=T['be'][:], in_=T['gcumT'][:], func=AF.Exp)
        I('dve', 'tensor_tensor', [T['be'], T['betaT']], [T['be']], out=T['be'][:], in0=T['be'][:], in1=T['betaT'][:], op=ALU.mult)
        T['cat'] = s.sb(f'cat{h}', [128, 384], F32)
        I('pool', 'tensor_copy', [T['gcum']], [T['cat']], out=T['cat'][:, 0:128], in_=T['gcum'][:])
        I('pool', 'tensor_copy', [T['rtab']], [T['cat']], out=T['cat'][:, 128:256], in_=T['rtab'][:])
        I('pool', 'tensor_copy', [T['gcum']], [T['cat']], out=T['cat'][:, 256:384], in_=T['gcum'][:])
        tabs.append(T)

    S32 = [s.sb(f'S32_{h}', [128, 128], F32) for h in range(NH)]
    Sbf = [s.sb(f'Sbf_{h}', [128, 128], BF16) for h in range(NH)]
    for h in range(NH):
        I('pool', 'memset', [], [S32[h]], S32[h][:], 0.0)
        I('pool', 'memset', [], [Sbf[h]], Sbf[h][:], 0.0)

    NSET = 3
    sets = []
    for u in range(NSET):
        U = {}
        U['xin'] = [s.sb(f'xin{u}{X}', [128, 3 + SB], F32) for X in range(3)]
        U['acc'] = [s.sb(f'acc{u}{X}', [128, SB], F32) for X in range(3)]
        U['cv'] = [s.sb(f'cv{u}{X}', [128, SB], F32) for X in range(3)]
        U['sq'] = s.sb(f'sq{u}', [128, SB], BF16)
        U['rn'] = s.sb(f'rn{u}', [128, SB], F32)
        U['xn'] = [s.sb(f'xn{u}{X}', [128, SB], F32) for X in range(2)]
        U['ost'] = s.sb(f'ost{u}', [128, 4, 128], F32)
        U['ch'] = []
        for c in range(4):
            D = {}
            for nm in ('kbg', 'vb', 'CT', 'DT', 'EGB', 'Q0', 'Q1', 'u'):
                D[nm] = s.sb(f'{nm}{u}{c}', [128, 128], F32)
            for nm in ('YP0', 'YP1'):
                D[nm] = s.sb(f'{nm}{u}{c}', [128, 256], F32)
            for nm in ('kdec', 'qkT', 'qg', 'wT', 'vnew'):
                D[nm] = s.sb(f'{nm}{u}{c}', [128, 128], BF16)
            U['ch'].append(D)
        sets.append(U)

    def stageA_gen(h, sb, U, Uprev):
        T = tabs[h]
        t0 = sb * SB
        for X in range(3):
            xin = U['xin'][X]
            if sb == 0:
                I('pool', 'memset', [], [xin], xin[:, 0:3], 0.0)
            else:
                pxin = Uprev['xin'][X]
                I('pool', 'tensor_copy', [pxin], [xin], out=xin[:, 0:3], in_=pxin[:, SB:SB + 3])
            col = IDX_G1 + (h * 3 + X) * 32 + sb
            s.gather(xin[:, 3:3 + SB], g1v512, idx[:, col:col + 1], [G1, idx], [xin])
            eng = 'dve'
            acc = U['acc'][X]
            wc = (h * 3 + X) * 4
            I(eng, 'tensor_scalar', [xin, cw], [acc], out=acc[:], in0=xin[:, 0:SB], scalar1=cw[:, wc:wc + 1], scalar2=None, op0=ALU.mult)
            for j in range(1, 4):
                if eng == 'dve':
                    I(eng, 'scalar_tensor_tensor', [xin, cw, acc], [acc], out=acc[:], in0=xin[:, j:j + SB],
                      scalar=cw[:, wc + j:wc + j + 1], in1=acc[:], op0=ALU.mult, op1=ALU.add)
                else:
                    tq = U['rn']
                    I(eng, 'tensor_scalar', [xin, cw], [tq], out=tq[:], in0=xin[:, j:j + SB], scalar1=cw[:, wc + j:wc + j + 1],
                      scalar2=None, op0=ALU.mult)
                    I(eng, 'tensor_tensor', [tq, acc], [acc], out=acc[:], in0=acc[:], in1=tq[:], op=ALU.add)
            cv = U['cv'][X]
            I('act', 'activation', [acc], [cv], out=cv[:], in_=acc[:], func=AF.Silu)
            yield
        for X in range(2):
            cv = U['cv'][X]
            sq, rn, xn = U['sq'], U['rn'], U['xn'][X]
            I('act', 'activation', [cv], [sq], out=sq[:], in_=cv[:], func=AF.Square)
            bk = pp.get_bank()
            I('pe', 'matmul', [C['ones_bf'], sq], [bk], bk[:, 0:SB], lhsT=C['ones_bf'][:], rhs=sq[:], start=True, stop=True)
            I('dve', 'tensor_scalar', [], [bk, rn], out=rn[:], in0=bk[:, 0:SB], scalar1=L2EPS, scalar2=None, op0=ALU.add)
            I('dve', 'reciprocal', [rn], [rn], out=rn[:], in_=rn[:])
            I('act', 'activation', [rn], [rn], out=rn[:], in_=rn[:], func=AF.Sqrt)
            I('pool', 'tensor_tensor', [cv, rn], [xn], out=xn[:], in0=cv[:], in1=rn[:], op=ALU.mult)
            yield

    def prep_gen(h, sb, U):
        T = tabs[h]
        qn, kn, vv = U['xn'][0], U['xn'][1], U['cv'][2]
        for c in range(4):
            D = U['ch'][c]
            n = sb * 4 + c
            cs = slice(c * 128, (c + 1) * 128)
            icol = ident[:, n:n + 1].to_broadcast([128, 128])
            be_c, kdf_c, bet_c, ng_c = (T[k][:, n:n + 1] for k in ('be', 'kdf', 'betaT', 'negg'))
            b_trk, p_trk = pp.get()
            I('pe', 'transpose', [kn, ident], [b_trk], p_trk, kn[:, cs], ident[:])
            b_trv, p_trv = pp.get()
            I('pe', 'transpose', [vv, ident], [b_trv], p_trv, vv[:, cs], ident[:])
            b_kk, p_kk = pp.get()
            I('pe', 'matmul', [kn], [b_kk], p_kk, lhsT=kn[:, cs], rhs=kn[:, cs], start=True, stop=True)
            b_bc = pp.get_bank()
            p_dt, p_ct, p_gb = b_bc[:, 0:128], b_bc[:, 128:256], b_bc[:, 256:384]
            I('pe', 'matmul', [ident, T['cat']], [b_bc], b_bc[:, 0:384], lhsT=icol, rhs=T['cat'][:], start=True, stop=False, skip_group_check=True)
            I('pe', 'matmul', [C['ident_bf'], C['negTi_bf']], [b_bc], p_dt, lhsT=C['ident_bf'][:], rhs=C['negTi_bf'][:],
              start=False, stop=False, skip_group_check=True)
            I('pe', 'matmul', [C['ident_bf'], C['negTs_bf']], [b_bc], p_ct, lhsT=C['ident_bf'][:], rhs=C['negTs_bf'][:],
              start=False, stop=True, skip_group_check=True)
            I('act', 'mul', [T['be']], [b_trk, D['kbg']], out=D['kbg'][:], in_=p_trk, mul=be_c)
            I('dve', 'tensor_scalar', [T['kdf']], [b_trk, D['kdec']], out=D['kdec'][:], in0=p_trk, scalar1=kdf_c, scalar2=None, op0=ALU.mult)
            I('act', 'mul', [T['betaT']], [b_trv, D['vb']], out=D['vb'][:], in_=p_trv, mul=bet_c)
            I('act', 'activation', [T['negg']], [b_bc, D['CT']], out=D['CT'][:], in_=p_ct, func=AF.Exp, bias=ng_c, scale=1.0)
            YP0 = D['YP0']
            I('dve', 'scalar_tensor_tensor', [D['CT']], [b_kk, YP0], out=YP0[:, 128:256], in0=p_kk, scalar=-1.0, in1=D['CT'][:],
              op0=ALU.mult, op1=ALU.mult)
            I('pool', 'tensor_copy', [ident], [YP0], out=YP0[:, 0:128], in_=ident[:])
            b_qk, p_qk = pp.get()
            I('pe', 'matmul', [kn, qn], [b_qk], p_qk, lhsT=kn[:, cs], rhs=qn[:, cs], start=True, stop=True)
            b_q, p_q = pp.get()
            I('pe', 'transpose', [YP0, ident], [b_q], p_q, YP0[:, 128:256], ident[:])
            I('act', 'activation', [T['negg']], [b_bc, D['DT']], out=D['DT'][:], in_=p_dt, func=AF.Exp, bias=ng_c, scale=1.0)
            I('dve', 'scalar_tensor_tensor', [D['DT']], [b_qk, D['qkT']], out=D['qkT'][:], in0=p_qk, scalar=scale, in1=D['DT'][:],
              op0=ALU.mult, op1=ALU.mult)
            I('act', 'activation', [lnsc], [b_bc, D['EGB']], out=D['EGB'][:], in_=p_gb, func=AF.Exp, bias=lnscale_c, scale=1.0)
            I('pool', 'tensor_tensor', [qn, D['EGB']], [D['qg']], out=D['qg'][:], in0=qn[:, cs], in1=D['EGB'][:], op=ALU.mult)
            I('act', 'copy', [], [b_q, D['Q0']], out=D['Q0'][:], in_=p_q)
            yield
        m = 1
        cur = 0
        while m <= 64:
            nxt = 1 - cur
            YPc, YPn, Qc, Qn_ = f'YP{cur}', f'YP{nxt}', f'Q{cur}', f'Q{nxt}'
            for c in range(4):
                D = U['ch'][c]
                wa = 256 if m < 32 else 128
                if wa == 256:
                    ba, pa_ = pp.get2()
                else:
                    ba, pa_ = pp.get()
                I('pe', 'matmul', [D[Qc], D[YPc]], [ba], pa_, lhsT=D[Qc][:], rhs=D[YPc][:, 0:wa], start=True, stop=True)
                if m < 64:
                    b2, p2 = pp.get()
                    I('pe', 'matmul', [D[Qc], D[YPc]], [b2], p2, lhsT=D[YPc][:, 128:256], rhs=D[Qc][:], start=True, stop=True)
                I('dve', 'tensor_tensor', [D[YPc]], [ba, D[YPn]], out=D[YPn][:, 0:128], in0=pa_[:, 0:128], in1=D[YPc][:, 0:128], op=ALU.add)
                if wa == 256:
                    I('act', 'copy', [], [ba, D[YPn]], out=D[YPn][:, 128:256], in_=pa_[:, 128:256])
                if m < 64:
                    I('act', 'copy', [], [b2, D[Qn_]], out=D[Qn_][:], in_=p2)
            cur = nxt
            m *= 2
            yield
        Yf = f'YP{cur}'
        for c in range(4):
            D = U['ch'][c]
            bu, pu = pp.get()
            I('pe', 'matmul', [D[Yf], D['vb']], [bu], pu, lhsT=D[Yf][:, 0:128], rhs=D['vb'][:], start=True, stop=True)
            bw, pw = pp.get()
            I('pe', 'matmul', [D[Yf], D['kbg']], [bw], pw, lhsT=D['kbg'][:], rhs=D[Yf][:, 0:128], start=True, stop=True)
            I('act', 'copy', [], [bu, D['u']], out=D['u'][:], in_=pu)
            I('dve', 'tensor_copy', [], [bw, D['wT']], out=D['wT'][:], in_=pw)
        yield

    def seq_gen(h, sb, U):
        T = tabs[h]
        bk = seqbank[h]
        pvn, po, psu = bk[:, 0:128], bk[:, 128:256], bk[:, 256:384]
        Sb, Sf = Sbf[h], S32[h]
        for c in range(4):
            D = U['ch'][c]
            n = sb * 4 + c
            I('pe', 'matmul', [D['wT'], Sb], [bk], pvn, lhsT=D['wT'][:], rhs=Sb[:], start=True, stop=True)
            I('dve', 'tensor_tensor', [D['u']], [bk, D['vnew']], out=D['vnew'][:], in0=D['u'][:], in1=pvn, op=ALU.subtract)
            I('pe', 'matmul', [D['qg'], Sb], [bk], po, lhsT=Sb[:], rhs=D['qg'][:], start=True, stop=False)
            I('pe', 'matmul', [D['qkT'], D['vnew']], [bk], po, lhsT=D['vnew'][:], rhs=D['qkT'][:], start=False, stop=True)
            I('pe', 'matmul', [D['kdec'], D['vnew']], [bk], psu, lhsT=D['kdec'][:], rhs=D['vnew'][:], start=True, stop=True)
            I('dve', 'scalar_tensor_tensor', [Sf, T['glc']], [bk, Sf], out=Sf[:], in0=Sf[:], scalar=T['glc'][:, n:n + 1], in1=psu,
              op0=ALU.mult, op1=ALU.add)
            I('act', 'copy', [Sf], [Sb], out=Sb[:], in_=Sf[:])
            I('act', 'copy', [], [bk, U['ost']], out=U['ost'][:, c, :], in_=po)
            yield
        t0 = sb * SB
        r0 = (h * 4 + sb // 8) * 128
        c0 = (sb % 8) * SB
        s.dma('sp', O2.h[r0:r0 + 128, c0:c0 + SB], U['ost'][:].rearrange("p c e -> p (c e)"), reads=[U['ost']], writes=[O2])
        if after_block is not None and sb % 8 == 7:
            after_block(h * 4 + sb // 8)
        yield

    def run_rr(gens):
        gens = list(gens)
        while gens:
            for g in list(gens):
                try:
                    next(g)
                except StopIteration:
                    gens.remove(g)

    units = [(h, sb) for sb in range(NSB) for h in range(NH)]
    NU = len(units)

    def uset(u):
        return sets[u % NSET]

    run_rr([stageA_gen(units[0][0], units[0][1], uset(0), uset(-2))])
    for u in range(NU):
        h, sb = units[u]
        gens = [prep_gen(h, sb, uset(u))]
        if u + 1 < NU:
            h2, sb2 = units[u + 1]
            gens.append(stageA_gen(h2, sb2, uset(u + 1), uset(u - 1)))
        if u >= 1:
            hp_, sbp = units[u - 1]
            gens.append(seq_gen(hp_, sbp, uset(u - 1)))
        run_rr(gens)
    hl, sbl = units[NU - 1]
    run_rr([seq_gen(hl, sbl, uset(NU - 1))])


def rstd_from_ps(s, ss_ps_bank, ss_ap, rstd, D):
    I = s.I
    I('dve', 'tensor_scalar', [], [ss_ps_bank, rstd], out=rstd[:], in0=ss_ap, scalar1=1.0 / D, scalar2=EPS, op0=ALU.mult, op1=ALU.add)
    I('dve', 'reciprocal', [rstd], [rstd], out=rstd[:], in_=rstd[:])
    I('act', 'activation', [rstd], [rstd], out=rstd[:], in_=rstd[:], func=AF.Sqrt)


def phase_post(s, C, mode, Gin, idx_base, Z1, xsrc, xsrc_buf, gt_off, og_d, w_out_d, XO, lam_init=0.0, NT=4096, TB=512):
    I = s.I
    modv = C['modv']
    idx = C['idx']
    gv = Gin.h[:, :].rearrange("r (b w) -> (r b) w", w=TB)
    ncol = 1 if mode == 'gdn' else 2
    og = s.sb('og', [128, ncol], F32)
    s.dma('sp', og[:], og_d, writes=[og])
    if mode == 'attn':
        I('dve', 'tensor_scalar', [og], [og], out=og[:], in0=og[:], scalar1=1.0 - lam_init, scalar2=None, op0=ALU.mult)
    Wb = s.sb('Wb', [128, 8, 1024], BF16)
    load_cast_w(s, Wb, w_out_d)
    os_ = [s.sb('o', [128, 8, TB], F32) for _ in range(2)]
    zs = [s.sb('z', [128, 8, TB], F32) for _ in range(2)] if mode == 'gdn' else None
    xs = [s.sb('x', [128, 8, TB], F32) for _ in range(2)]
    sq = s.sb('sq', [128, 8, TB], BF16)
    hb = [s.sb('hb', [128, 8, TB], BF16) for _ in range(2)]
    rstd = [s.sb('rstd', [128, TB], F32) for _ in range(2)]
    t1 = [s.sb('t1', [128, TB], F32) for _ in range(2)]
    t2 = [s.sb('t2', [128, TB], F32) for _ in range(2)]
    ssb = [s.ps('ss', [128, 512]) for _ in range(2)]
    pps = [s.ps('pp', [128, 512]) for _ in range(4)]
    outs = [s.sb('ost', [128, TB], F32) for _ in range(4)]
    xv = xsrc.rearrange("(kc p) t -> p kc t", p=128)
    zv = Z1.h[:, :].rearrange("(kc p) t -> p kc t", p=128) if mode == 'gdn' else None
    it = 0
    k2 = 0
    for tb in range(NT // TB):
        tsl = slice(tb * TB, (tb + 1) * TB)
        o = os_[tb % 2]
        x = xs[tb % 2]
        h = hb[tb % 2]
        for kc in range(8):
            col = idx_base + kc * 8 + tb
            s.gather(o[:, kc, :], gv, idx[:, col:col + 1], [Gin, idx], [o])
        s.dma('sp', x[:], xv[:, :, tsl], reads=[xsrc_buf] if xsrc_buf is not None else [], writes=[x])
        if mode == 'gdn':
            z = zs[tb % 2]
            s.dma('sp', z[:], zv[:, :, tsl], reads=[Z1], writes=[z])
        I('act', 'activation', [o], [sq], out=sq[:], in_=o[:], func=AF.Square)
        if mode == 'gdn':
            for kc in range(8):
                sb_ = ssb[k2 % 2]
                r = rstd[k2 % 2]
                a1 = t1[k2 % 2]
                a2 = t2[k2 % 2]
                k2 += 1
                I('pe', 'matmul', [C['ones_bf'], sq], [sb_], sb_[:, 0:TB], lhsT=C['ones_bf'][:], rhs=sq[:, kc, :], start=True, stop=True)
                rstd_from_ps(s, sb_, sb_[:, 0:TB], r, 128)
                I('dve', 'tensor_tensor', [o, r], [a1], out=a1[:], in0=o[:, kc, :], in1=r[:], op=ALU.mult)
                I('act', 'activation', [z], [a2], out=a2[:], in_=z[:, kc, :], func=AF.Silu)
                I('pool', 'tensor_tensor', [a1, a2], [a1], out=a1[:], in0=a1[:], in1=a2[:], op=ALU.mult)
                I('act', 'mul', [a1, og], [h], out=h[:, kc, :], in_=a1[:], mul=og[:, 0:1])
        else:
            for hd in range(4):
                sb_ = ssb[k2 % 2]
                r = rstd[k2 % 2]
                k2 += 1
                for j in range(2):
                    I('pe', 'matmul', [C['ones_bf'], sq], [sb_], sb_[:, 0:TB], lhsT=C['ones_bf'][:], rhs=sq[:, 2 * hd + j, :],
                      start=(j == 0), stop=(j == 1))
                rstd_from_ps(s, sb_, sb_[:, 0:TB], r, 256)
                for j in range(2):
                    I('dve', 'scalar_tensor_tensor', [o, og, r], [h], out=h[:, 2 * hd + j, :], in0=o[:, 2 * hd + j, :],
                      scalar=og[:, j:j + 1], in1=r[:], op0=ALU.mult, op1=ALU.mult)
        for n in range(8):
            pp = pps[it % 4]
            ot = outs[it % 4]
            for kc in range(8):
                I('pe', 'matmul', [Wb, h], [pp], pp[:, 0:TB], lhsT=Wb[:, kc, n * 128:(n + 1) * 128], rhs=h[:, kc, :],
                  start=(kc == 0), stop=(kc == 7))
            I('dve', 'scalar_tensor_tensor', [modv, x], [pp, ot], out=ot[:], in0=pp[:, 0:TB],
              scalar=modv[:, gt_off + n:gt_off + n + 1], in1=x[:, n, :], op0=ALU.mult, op1=ALU.add)
            s.dma('sp', XO.h[n * 128:(n + 1) * 128, tsl], ot[:], reads=[ot], writes=[XO])
            it += 1


def phase_mlp(s, C, XI, moff, gT_d, w1_d, w2_d, xoT_d, XO=None, final_g_d=None, NT=4096, TB=256):
    I = s.I
    modv = C['modv']
    gT = s.sb('gT', [128, 8], F32)
    s.dma('sp', gT[:], gT_d, writes=[gT])
    G = s.sb('G', [128, 8], F32)
    I('dve', 'scalar_tensor_tensor', [modv, gT], [G], out=G[:], in0=modv[:, moff + 8:moff + 16], scalar=1.0, in1=gT[:],
      op0=ALU.add, op1=ALU.mult)
    if final_g_d is not None:
        fg = s.sb('fg', [128, 8], F32)
        s.dma('sp', fg[:], final_g_d, writes=[fg])
    W1 = s.sb('W1', [128, 8, 4096], BF16)
    W2 = s.sb('W2', [128, 32, 1024], BF16)
    load_cast_w(s, W1, w1_d)
    load_cast_w(s, W2, w2_d)
    xs = [b.view("p (k t) -> p k t", t=TB) for b in get_stage(s)]
    assert TB == 256
    sq = s.sb('sq', [128, 8, TB], BF16)
    hs = [s.sb('h', [128, 8, TB], BF16) for _ in range(2)]
    tmp = [s.sb('tmp', [128, TB], F32) for _ in range(2)]
    rstd = s.sb('rstd', [128, TB], F32)
    hid = s.sb('hid', [128, 32, TB], BF16)
    rl = [s.sb('rl', [128, TB], F32) for _ in range(2)]
    xo = [s.sb('xo', [128, 8, TB], F32) for _ in range(1)]
    ss_ps = s.ps('ss', [128, 512])
    pps = [s.ps('pp', [128, 512]) for _ in range(6)]
    xv = XI.h[:, :].rearrange("(kc p) t -> p kc t", p=128)
    ov = xoT_d.rearrange("(kc p) t -> p kc t", p=128)
    owr = [XO] if XO is not None else []
    it = 0
    for tb in range(NT // TB):
        tsl = slice(tb * TB, (tb + 1) * TB)
        x = xs[tb % 2]
        h = hs[tb % 2]
        xout = xo[0]
        s.dma('sp', x[:], xv[:, :, tsl], reads=[XI], writes=[x])
        norm_mod(s, C, x, (G, 0), (modv, moff), h, tmp, sq, ss_ps, rstd, TB=TB)
        for f in range(32):
            pp = pps[it % 6]
            r = rl[it % 2]
            it += 1
            for kc in range(8):
                I('pe', 'matmul', [W1, h], [pp], pp[:, 0:TB], lhsT=W1[:, kc, f * 128:(f + 1) * 128], rhs=h[:, kc, :],
                  start=(kc == 0), stop=(kc == 7))
            I('act', 'activation', [], [pp, r], out=r[:], in_=pp[:, 0:TB], func=AF.Relu)
            I('pool', 'tensor_tensor', [r], [hid], out=hid[:, f, :], in0=r[:], in1=r[:], op=ALU.mult)
        for n in range(8):
            pp = pps[it % 6]
            it += 1
            for f in range(32):
                I('pe', 'matmul', [W2, hid], [pp], pp[:, 0:TB], lhsT=W2[:, f, n * 128:(n + 1) * 128], rhs=hid[:, f, :],
                  start=(f == 0), stop=(f == 31))
            I('dve', 'scalar_tensor_tensor', [modv, x], [pp, xout], out=xout[:, n, :], in0=pp[:, 0:TB],
              scalar=modv[:, moff + 16 + n:moff + 17 + n], in1=x[:, n, :], op0=ALU.mult, op1=ALU.add)
        if final_g_d is None:
            s.dma('sp', ov[:, :, tsl], xout[:], reads=[xout], writes=owr)
        else:
            I('act', 'activation', [xout], [sq], out=sq[:], in_=xout[:], func=AF.Square)
            for kc in range(8):
                I('pe', 'matmul', [C['ones_bf'], sq], [ss_ps], ss_ps[:, 0:TB], lhsT=C['ones_bf'][:], rhs=sq[:, kc, :],
                  start=(kc == 0), stop=(kc == 7))
            rstd_from_ps(s, ss_ps, ss_ps[:, 0:TB], rstd, 1024)
            for kc in range(8):
                I('dve', 'scalar_tensor_tensor', [xout, fg, rstd], [xout], out=xout[:, kc, :], in0=xout[:, kc, :],
                  scalar=fg[:, kc:kc + 1], in1=rstd[:], op0=ALU.mult, op1=ALU.mult)
            s.dma('sp', ov[:, :, tsl], xout[:], reads=[xout], writes=owr)


def phase_kvq(s, C, XI, gkvT_d, gqT_d, kv_w_d, kv_wp_d, wq_d, wqp_d, ropeC_d, ropeS_d, Q3, K3, V3, after_tb=None, NT=4096, TB=512):
    I = s.I
    modv = C['modv']
    gkv = s.sb('gkv', [128, 8], F32)
    s.dma('sp', gkv[:], gkvT_d, writes=[gkv])
    gq = s.sb('gq', [128, 8], F32)
    s.dma('sp', gq[:], gqT_d, writes=[gq])
    Gkv = s.sb('Gkv', [128, 8], F32)
    Gq = s.sb('Gq', [128, 8], F32)
    I('dve', 'scalar_tensor_tensor', [modv, gkv], [Gkv], out=Gkv[:], in0=modv[:, 104:112], scalar=1.0, in1=gkv[:], op0=ALU.add, op1=ALU.mult)
    I('dve', 'scalar_tensor_tensor', [modv, gq], [Gq], out=Gq[:], in0=modv[:, 56:64], scalar=1.0, in1=gq[:], op0=ALU.add, op1=ALU.mult)
    Wkv = s.sb('Wkv', [128, 8, 2048], BF16)
    Wkp = s.sb('Wkp', [128, 8, 1024], BF16)
    Wq = s.sb('Wq', [128, 8, 1024], BF16)
    Wqp = s.sb('Wqp', [128, 8, 1024], BF16)
    load_cast_w(s, Wkv, kv_w_d)
    load_cast_w(s, Wkp, kv_wp_d)
    load_cast_w(s, Wq, wq_d)
    load_cast_w(s, Wqp, wqp_d)
    xs = [s.sb('xs', [128, 8, TB], F32) for _ in range(2)]
    sq = s.sb('sq', [128, 8, TB], BF16)
    hkv = s.sb('hkv', [128, 8, TB], BF16)
    hq = s.sb('hq', [128, 8, TB], BF16)
    tmp = [s.sb('tmp', [128, TB], F32) for _ in range(2)]
    rstd = s.sb('rstd', [128, TB], F32)
    rc = [s.sb('rc', [128, TB], F32) for _ in range(2)]
    rs = [s.sb('rs', [128, TB], F32) for _ in range(2)]
    ta = [s.sb('ta', [128, TB], F32) for _ in range(3)]
    tb_ = [s.sb('tb', [128, TB], F32) for _ in range(3)]
    outs = [s.sb('ost', [128, TB], BF16) for _ in range(4)]
    vouts = [s.sb('vost', [128, 256], BF16) for _ in range(4)]
    ss_ps = s.ps('ss', [128, 512])
    pps = [s.ps('pp', [128, 512]) for _ in range(6)]
    xv = XI.h[:, :].rearrange("(kc p) t -> p kc t", p=128)
    it = 0
    io = 0
    for tb in range(NT // TB):
        tsl = slice(tb * TB, (tb + 1) * TB)
        x = xs[tb % 2]
        cc = rc[tb % 2]
        sn = rs[tb % 2]
        s.dma('sp', x[:], xv[:, :, tsl], reads=[XI], writes=[x])
        s.dma('sp', cc[:], ropeC_d[:, tsl], writes=[cc])
        s.dma('sp', sn[:], ropeS_d[:, tsl], writes=[sn])
        norm_mod(s, C, x, (Gkv, 0), (modv, 96), hkv, tmp, sq, ss_ps, rstd, TB=TB)
        norm_mod(s, C, x, (Gq, 0), (modv, 48), hq, tmp, sq, ss_ps, rstd, TB=TB)
        for (W, Wp, hh, dst) in ((Wkv, Wkp, hkv, K3), (Wq, Wqp, hq, Q3)):
            for n in range(8):
                pa = pps[it % 6]
                it += 1
                for kc in range(8):
                    I('pe', 'matmul', [W, hh], [pa], pa[:, 0:TB], lhsT=W[:, kc, n * 128:(n + 1) * 128], rhs=hh[:, kc, :],
                      start=(kc == 0), stop=(kc == 7))
                pb = pps[it % 6]
                it += 1
                for kc in range(8):
                    I('pe', 'matmul', [Wp, hh], [pb], pb[0:32, 0:TB], lhsT=Wp[:, kc, n * 128:n * 128 + 32], rhs=hh[:, kc, :],
                      start=(kc == 0), stop=(kc == 7))
                ot = outs[io % 4]
                a = ta[io % 3]
                b = tb_[io % 3]
                I('dve', 'tensor_tensor', [cc], [pa, a], out=a[:], in0=pa[:, 0:TB], in1=cc[:], op=ALU.mult)
                I('dve', 'tensor_tensor', [sn], [pb, b], out=b[0:32, :], in0=pb[0:32, 0:TB], in1=sn[0:32, :], op=ALU.mult)
                I('pool', 'tensor_tensor', [a, b], [a], out=a[0:32, :], in0=a[0:32, :], in1=b[0:32, :], op=ALU.add)
                I('act', 'copy', [a], [ot], out=ot[:], in_=a[:])
                s.dma('sp', dst.h[tb * 1024 + n * 128:tb * 1024 + (n + 1) * 128, :], ot[:], reads=[ot], writes=[dst])
                io += 1
        for tsb in range(TB // 128):
            for hd in range(4):
                pa = pps[it % 6]
                it += 1
                for kc in range(8):
                    I('pe', 'matmul', [Wkv, hkv], [pa], pa[:, 0:256], lhsT=hkv[:, kc, tsb * 128:(tsb + 1) * 128],
                      rhs=Wkv[:, kc, 1024 + hd * 256:1024 + (hd + 1) * 256], start=(kc == 0), stop=(kc == 7))
                vo = vouts[io % 4]
                io += 1
                I('act', 'copy', [], [pa, vo], out=vo[:], in_=pa[:, 0:256])
                r0 = (tb * 4 + hd) * TB + tsb * 128
                s.dma('sp', V3.h[r0:r0 + 128, :], vo[:], reads=[vo], writes=[V3])
        if after_tb is not None:
            after_tb(tb)


def phase_attn(s, C, G3q, G3k, G3v, lamT_d, O4, lam_init, S=16384, IDX_QK=260, IDX_V=324, after_block=None):
    I = s.I
    scale = 128 ** -0.5
    NKB = S // 128
    NQB = S // 512
    ones_bf = C['ones_bf']
    ones_f = s.sb('ones_f', [128, 128], F32)
    I('pool', 'memset', [], [ones_f], ones_f[:], 1.0)
    tri = C['tri']
    idx = C['idx']
    ident = C['ident']
    gq = G3q.h[:, :]
    gk = G3k.h[:, :]
    misc = s.ps('misc', [128, 512])
    lp = s.sb('lp', [128, 4], F32)
    s.dma('sp', lp[:], lamT_d, writes=[lp])
    pr = s.sb('pr', [128, 2], F32)
    I('dve', 'tensor_tensor', [lp], [pr], out=pr[:, 0:1], in0=lp[:, 0:1], in1=lp[:, 1:2], op=ALU.mult)
    I('dve', 'tensor_tensor', [lp], [pr], out=pr[:, 1:2], in0=lp[:, 2:3], in1=lp[:, 3:4], op=ALU.mult)
    I('pe', 'matmul', [ones_f, pr], [misc], misc[:, 0:2], lhsT=ones_f[:], rhs=pr[:], start=True, stop=True)
    ex = s.sb('ex', [128, 2], F32)
    I('act', 'activation', [], [misc, ex], out=ex[:], in_=misc[:, 0:2], func=AF.Exp)
    nlam = s.sb('nlam', [128, 1], F32)
    I('dve', 'tensor_tensor', [ex], [nlam], out=nlam[:], in0=ex[:, 1:2], in1=ex[:, 0:1], op=ALU.subtract)
    I('dve', 'tensor_scalar', [nlam], [nlam], out=nlam[:], in0=nlam[:], scalar1=-lam_init, scalar2=None, op0=ALU.add)
    KbB = [[s.sb(f'Kb{m}_{j}', [128, 512], BF16) for j in range(NQB)] for m in range(2)]
    VbB = [s.sb(f'Vb{j}', [128, 4, 257], BF16) for j in range(NQB)]
    sqs = [s.sb('sqs', [128, 512], BF16) for _ in range(2)]
    kmx = [s.sb(f'kmx{m}', [128, NQB], F32) for m in range(2)]
    krun = [s.sb(f'krun{m}', [128, NQB], F32) for m in range(2)]
    lb_i = [0]

    def load_block(j):
        for m in range(2):
            sq = sqs[lb_i[0] % 2]
            lb_i[0] += 1
            col = IDX_QK + m * 32 + j
            kb_ = KbB[m][j]
            s.gather(kb_[:], gk, idx[:, col:col + 1], [G3k, idx], [kb_])
            I('act', 'activation', [kb_], [sq], out=sq[:], in_=kb_[:], func=AF.Square)
            I('pe', 'matmul', [ones_bf, sq], [misc], misc[:, 0:512], lhsT=ones_bf[:], rhs=sq[:], start=True, stop=True)
            I('dve', 'reduce_max', [], [misc, kmx[m]], out=kmx[m][:, j:j + 1], in_=misc[:, 0:512], axis=AX.X)
            if j == 0:
                I('dve', 'tensor_copy', [kmx[m]], [krun[m]], out=krun[m][:, 0:1], in_=kmx[m][:, 0:1])
            else:
                I('dve', 'tensor_tensor', [kmx[m], krun[m]], [krun[m]], out=krun[m][:, j:j + 1], in0=krun[m][:, j - 1:j],
                  in1=kmx[m][:, j:j + 1], op=ALU.max)
        vb_ = VbB[j]
        I('pool', 'memset', [], [vb_], vb_[:, :, 256:257], 1.0)
        for q4 in range(4):
            kb = 4 * j + q4
            s.gather(vb_[:, q4, 0:256], G3v.h[:, :], idx[:, IDX_V + kb:IDX_V + kb + 1], [G3v, idx], [vb_])

    load_block(0)
    acc = [s.ps(f'acc{j}', [128, 512]) for j in range(4)]
    stp = [s.ps(f'st{j}', [128, 512]) for j in range(3)]
    qbf = [s.sb('qbf', [128, 512], BF16) for _ in range(2)]
    PT = [s.sb('PT', [128, 512], BF16) for _ in range(3)]
    mq = [s.sb('mq', [128, 1], F32) for _ in range(2)]
    bias = [s.sb('bias', [128, 1], F32) for _ in range(2)]
    o1 = s.sb('o1', [128, 4, 256], F32)
    rl = [s.sb('rl', [128, 1], F32) for _ in range(4)]
    ot = [s.sb('ot', [128, 256], F32) for _ in range(4)]
    oT_st = [s.sb('oTs', [128, 256], F32) for _ in range(2)]
    it = 0
    qi = 0
    for qb in range(NQB):
        if qb + 1 < NQB:
            load_block(qb + 1)
        for m in range(2):
            qb_ = qbf[qi % 2]
            sq = sqs[qi % 2]
            mq_ = mq[qi % 2]
            bi = bias[qi % 2]
            qi += 1
            col = IDX_QK + m * 32 + qb
            s.gather(qb_[:], gq, idx[:, col:col + 1], [G3q, idx], [qb_])
            I('act', 'activation', [qb_], [sq], out=sq[:], in_=qb_[:], func=AF.Square)
            I('pe', 'matmul', [ones_bf, sq], [misc], misc[:, 0:512], lhsT=ones_bf[:], rhs=sq[:], start=True, stop=True)
            I('dve', 'reduce_max', [], [misc, mq_], out=mq_[:], in_=misc[:, 0:512], axis=AX.X)
            I('dve', 'tensor_tensor', [mq_, krun[m]], [mq_], out=mq_[:], in0=mq_[:], in1=krun[m][:, qb:qb + 1], op=ALU.mult)
            I('act', 'activation', [mq_], [mq_], out=mq_[:], in_=mq_[:], func=AF.Sqrt)
            I('dve', 'tensor_scalar', [mq_], [bi], out=bi[:], in0=mq_[:], scalar1=-scale, scalar2=None, op0=ALU.mult)
            nkb = 4 * qb + 4
            LOOK = 2
            slots = {}

            def issue_st(kb):
                nonlocal it
                r = kb - 4 * qb
                c0 = 128 * max(r, 0)
                sp_ = stp[it % 3]
                pt = PT[it % 3]
                it += 1
                slots[kb] = (sp_, pt, r, c0)
                kbuf = KbB[m][kb // 4]
                I('pe', 'matmul', [kbuf, qb_], [sp_], sp_[:, c0:512], lhsT=kbuf[:, (kb % 4) * 128:(kb % 4 + 1) * 128], rhs=qb_[:, c0:512],
                  start=True, stop=True)

            for kb in range(min(LOOK, nkb)):
                issue_st(kb)
            for kb in range(nkb):
                sp_, pt, r, c0 = slots.pop(kb)
                I('act', 'activation', [bi], [sp_, pt], out=pt[:, c0:512], in_=sp_[:, c0:512], func=AF.Exp, bias=bi[:, 0:1], scale=scale)
                if r >= 0:
                    I('pool', 'tensor_tensor', [pt, tri], [pt], out=pt[:, c0:c0 + 128], in0=pt[:, c0:c0 + 128], in1=tri[:], op=ALU.mult)
                if kb + LOOK < nkb:
                    issue_st(kb + LOOK)
                for j in range(max(r, 0), 4):
                    vbuf = VbB[kb // 4]
                    I('pe', 'matmul', [pt, vbuf], [acc[j]], acc[j][:, 0:257], lhsT=pt[:, 128 * j:128 * (j + 1)], rhs=vbuf[:, kb % 4, :],
                      start=(kb == 0), stop=(kb == 4 * qb + j))
            for j in range(4):
                a = acc[j]
                I('dve', 'reciprocal', [], [a, rl[j]], out=rl[j][:], in_=a[:, 256:257])
                if m == 0:
                    I('act', 'mul', [rl[j]], [a, o1], out=o1[:, j, :], in_=a[:, 0:256], mul=rl[j][:, 0:1])
                else:
                    I('dve', 'tensor_tensor', [rl[j], nlam], [rl[j]], out=rl[j][:], in0=rl[j][:], in1=nlam[:], op=ALU.mult)
                    I('dve', 'scalar_tensor_tensor', [rl[j], o1], [a, ot[j]], out=ot[j][:], in0=a[:, 0:256], scalar=rl[j][:, 0:1],
                      in1=o1[:, j, :], op0=ALU.mult, op1=ALU.add)
                    for cch in range(2):
                        I('pe', 'transpose', [ot[j], ident], [misc], misc[:, cch * 128:(cch + 1) * 128],
                          ot[j][:, cch * 128:(cch + 1) * 128], ident[:])
                    oTs = oT_st[(qb * 4 + j) % 2]
                    I('act', 'copy', [], [misc, oTs], out=oTs[:], in_=misc[:, 0:256])
                    tq = qb // 8
                    c0 = (qb % 8) * 512 + j * 128
                    for cch in range(2):
                        r0 = tq * 256 + cch * 128
                        s.dma('sp', O4.h[r0:r0 + 128, c0:c0 + 128], oTs[:, cch * 128:(cch + 1) * 128], reads=[oTs], writes=[O4])
                    if after_block is not None and qb % 8 == 7 and j == 3:
                        after_block(qb // 8)

import math
from concourse.bass_utils import run_bass_kernel_spmd

NCORES = 8
SEQ = 16384
TPC = 4096
LAM_INIT1 = 0.8 - 0.6 * math.exp(-0.3 * 1)
ROT_DIM = 32
ROPE_THETA = 500000.0
NIDX = 516
IDX_G1, IDX_AB, IDX_G2, IDX_QK, IDX_V, IDX_G4 = 0, 192, 196, 260, 324, 452
GROUPS = [[0, 1, 2, 3], [4, 5, 6, 7]]

IN_SPECS = {
    "xT": ((1024, TPC), F32), "cT": ((128, 8, 1), F32), "w0": ((1024, 6144), F32), "w1": ((1024, 6144), F32),
    "w2": ((1024, 2048), F32), "mb": ((128, 112), F32), "gT0": ((128, 8), F32), "w_in": ((1024, 4112), F32),
    "cw": ((128, 24), F32), "hp": ((128, 4), F32), "ident": ((128, 128), F32), "negTi": ((128, 128), F32),
    "negTs": ((128, 128), F32), "tri": ((128, 128), F32), "idx": ((128, NIDX), U32), "og_a": ((128, 1), F32),
    "a_w_out": ((1024, 1024), F32), "gm0": ((128, 8), F32), "m1_0": ((1024, 4096), F32), "m2_0": ((4096, 1024), F32),
    "gkv": ((128, 8), F32), "gq": ((128, 8), F32), "kv_w": ((1024, 2048), F32), "kv_wp": ((1024, 1024), F32),
    "wq": ((1024, 1024), F32), "wqp": ((1024, 1024), F32), "ropeC": ((128, TPC), F32), "ropeS": ((128, TPC), F32),
    "lamT": ((128, 4), F32), "og_b": ((128, 2), F32), "b_w_out": ((1024, 1024), F32), "gm1": ((128, 8), F32),
    "m1_1": ((1024, 4096), F32), "m2_1": ((4096, 1024), F32), "fg": ((128, 8), F32),
}


def build_program(stop_after=99):
    nc = bass.Bass("TRN2", target_bir_lowering=False)
    d = {k: nc.dram_tensor(k, list(shp), dt, kind="ExternalInput").ap() for k, (shp, dt) in IN_SPECS.items()}
    outT = nc.dram_tensor("outT", [1024, TPC], F32, kind="ExternalOutput").ap()
    s = Sched(nc)

    def scratch(name, shape, dt=F32):
        return s.dram(nc.dram_tensor(name, list(shape), dt), name)

    P1 = scratch("P1", (8 * 3136, 512))
    Z1 = scratch("Z1", (1024, TPC))
    G1 = scratch("G1", (4 * 8 * 3136, 512))
    O2 = scratch("O2", (1024, TPC))
    G2 = scratch("G2", (4096, TPC))
    XM = scratch("XM", (1024, TPC))
    X1 = scratch("X1", (1024, TPC))
    Q3 = scratch("Q3", (8 * 1024, 512), BF16)
    K3 = scratch("K3", (8 * 1024, 512), BF16)
    V3 = scratch("V3", (4 * TPC, 256), BF16)
    G3q = scratch("G3q", (4 * 8 * 1024, 512), BF16)
    G3k = scratch("G3k", (4 * 8 * 1024, 512), BF16)
    G3v = scratch("G3v", (16 * TPC, 256), BF16)
    O4 = scratch("O4", (1024, TPC))
    G4 = scratch("G4", (4096, TPC))
    XM2 = scratch("XM2", (1024, TPC))

    C = consts(s)
    gdn_consts(s, C, d["ident"], d["negTi"], d["negTs"])
    trif = s.sb('trif', [128, 128], F32)
    s.dma('sp', trif[:], d["tri"], writes=[trif])
    C['tri'] = s.sb('tri', [128, 128], BF16)
    s.I('dve', 'tensor_copy', [trif], [C['tri']], out=C['tri'][:], in_=trif[:])
    C['idx'] = s.sb('idx', [128, NIDX], U32)
    s.dma('sp', C['idx'][:], d["idx"], writes=[C['idx']])
    C['modv'] = s.sb('modvg', [128, 112], F32)

    def ag(src, dst):
        R, Fc = src.h.shape
        esz = 2 if src.h.dtype == BF16 else 4
        RS = (1 << 20) // (Fc * esz)
        assert R % RS == 0
        for k in range(R // RS):
            ag_slice(src, dst, k, RS)

    def ag_slice(src, dst, k, RS):
        s.allgather(src.h[k * RS:(k + 1) * RS, :], dst.h[k * 4 * RS:(k + 1) * 4 * RS, :], src, dst, GROUPS)

    def p1_after_tb(tb):
        lo = ((tb * 3136 + 511) // 512) if tb > 0 else 0
        for k in range(49):
            if (512 * k + 511) // 3136 == tb:
                ag_slice(P1, G1, k, 512)

    def o2_after_block(blk):
        for k in (2 * blk, 2 * blk + 1):
            ag_slice(O2, G2, k, 64)

    def kvq_after_tb(tb):
        ag_slice(Q3, G3q, tb, 1024)
        ag_slice(K3, G3k, tb, 1024)
        ag_slice(V3, G3v, tb, 2048)

    def o4_after_block(tq):
        for k in range(4 * tq, 4 * tq + 4):
            ag_slice(O4, G4, k, 64)

    s.begin_phase()
    phase_mod(s, C, d["cT"], [d["w0"], d["w1"], d["w2"]], d["mb"])
    s.end_phase()
    if stop_after == 1:
        s.begin_phase()
        s.end_phase(final=True)
        return nc
    s.begin_phase()
    phase_l1(s, C, d["xT"], d["gT0"], d["w_in"], P1, Z1, after_tb=p1_after_tb)
    s.end_phase()
    if stop_after == 2:
        s.begin_phase()
        s.end_phase(final=True)
        return nc
    s.begin_phase()
    phase_gdn(s, C, G1, d["cw"], d["hp"], O2, S=SEQ, IDX_G1=IDX_G1, IDX_AB=IDX_AB, after_block=o2_after_block)
    s.end_phase()
    if stop_after == 3:
        s.begin_phase()
        s.end_phase(final=True)
        return nc
    s.begin_phase()
    phase_post(s, C, 'gdn', G2, IDX_G2, Z1, d["xT"], None, 16, d["og_a"], d["a_w_out"], XM)
    s.end_phase()
    if stop_after == 4:
        s.begin_phase()
        s.end_phase(final=True)
        return nc
    s.begin_phase()
    phase_mlp(s, C, XM, 24, d["gm0"], d["m1_0"], d["m2_0"], X1.h[:, :], XO=X1)
    s.end_phase()
    if stop_after == 5:
        s.begin_phase()
        s.end_phase(final=True)
        return nc
    s.begin_phase()
    phase_kvq(s, C, X1, d["gkv"], d["gq"], d["kv_w"], d["kv_wp"], d["wq"], d["wqp"], d["ropeC"], d["ropeS"], Q3, K3, V3, after_tb=kvq_after_tb)
    s.end_phase()
    if stop_after == 6:
        s.begin_phase()
        s.end_phase(final=True)
        return nc
    s.begin_phase()
    phase_attn(s, C, G3q, G3k, G3v, d["lamT"], O4, LAM_INIT1, S=SEQ, IDX_QK=IDX_QK, IDX_V=IDX_V, after_block=o4_after_block)
    s.end_phase()
    if stop_after == 7:
        s.begin_phase()
        s.end_phase(final=True)
        return nc
    s.begin_phase()
    phase_post(s, C, 'attn', G4, IDX_G4, None, X1.h[:, :], X1, 64, d["og_b"], d["b_w_out"], XM2, lam_init=LAM_INIT1)
    s.end_phase()
    if stop_after == 8:
        s.begin_phase()
        s.end_phase(final=True)
        return nc
    s.begin_phase()
    phase_mlp(s, C, XM2, 72, d["gm1"], d["m1_1"], d["m2_1"], outT, final_g_d=d["fg"])
    s.end_phase(final=True)
    return nc


def _fm8(v):
    return np.ascontiguousarray(np.asarray(v, np.float32).reshape(-1, 128).T)


def _perm_cols(w):
    w = np.asarray(w)
    out = w.copy()
    for c in range(w.shape[1] // 128):
        b = c * 128
        out[:, b:b + 16] = w[:, b + 16:b + 32]
        out[:, b + 16:b + 32] = w[:, b:b + 16]
    return out


def _rope_tables(t0, n):
    pos = (t0 + np.arange(n)).astype(np.float32)
    inv = (np.float32(ROPE_THETA) ** (-(np.arange(0, ROT_DIM, 2, dtype=np.float32)) / np.float32(ROT_DIM))).astype(np.float32)
    fr = (pos[:, None] * inv[None, :]).astype(np.float32)
    cos = np.cos(fr).astype(np.float32).T
    sin = np.sin(fr).astype(np.float32).T
    Cc = np.ones((128, n), np.float32)
    Ss = np.zeros((128, n), np.float32)
    Cc[0:16] = cos
    Cc[16:32] = cos
    Ss[0:16] = -sin
    Ss[16:32] = sin
    return Cc, Ss


def _grow(src, row, RS):
    return ((row // RS) * 4 + src) * RS + row % RS


def _idx_table(r):
    p = np.arange(128, dtype=np.int64)
    t = np.zeros((128, NIDX), np.int64)
    for hh in range(2):
        h = 2 * r + hh
        for X in range(3):
            for sb in range(32):
                t[:, IDX_G1 + (hh * 3 + X) * 32 + sb] = _grow(sb // 8, (sb % 8) * 3136 + X * 1024 + h * 128 + p, 512)
        t[:, IDX_AB + 2 * hh] = _grow(p // 32, ((p % 32) // 4) * 3136 + 3072 + h, 512) * 4 + p % 4
        t[:, IDX_AB + 2 * hh + 1] = _grow(p // 32, ((p % 32) // 4) * 3136 + 3080 + h, 512) * 4 + p % 4
    for kc in range(8):
        for tb in range(8):
            t[:, IDX_G2 + kc * 8 + tb] = _grow(kc // 2, ((kc % 2) * 4 + r) * 128 + p, 64) * 8 + tb
            t[:, IDX_G4 + kc * 8 + tb] = _grow(kc // 2, r * 256 + (kc % 2) * 128 + p, 64) * 8 + tb
    for m in range(2):
        for j in range(32):
            t[:, IDX_QK + m * 32 + j] = _grow(j // 8, (j % 8) * 1024 + r * 256 + m * 128 + p, 1024)
    for kb in range(128):
        t[:, IDX_V + kb] = _grow(kb // 32, (((kb % 32) // 4) * 4 + r) * 512 + ((kb % 32) % 4) * 128 + p, 2048)
    return t.astype(np.uint32)


def kernel(x, c, mod_w, mod_b, norm_mix_g, norm_mlp_g, a_w_in, a_conv_w, a_log, a_dt_bias, a_out_norm_g, a_w_out,
           kv_mod_w, kv_mod_b, kv_norm_g, kv_w, b_w_q, b_lambda, b_subln_g, b_w_out, mlp_w1, mlp_w2, final_g):
    f32 = np.float32
    x = np.asarray(x, f32)
    B, S, D = x.shape
    A = lambda v: np.asarray(v, f32)
    jj, ii = np.meshgrid(np.arange(128), np.arange(128), indexing='ij')
    shared = {
        "w0": A(mod_w[0]), "w1": A(mod_w[1]), "w2": A(kv_mod_w),
        "mb": np.ascontiguousarray(np.concatenate([A(mod_b[0]), A(mod_b[1]), A(kv_mod_b)]).reshape(112, 128).T),
        "gT0": _fm8(norm_mix_g[0]), "w_in": A(a_w_in[0]), "ident": np.eye(128, dtype=f32),
        "negTi": np.where(ii >= jj, 0.0, -30000.0).astype(f32), "negTs": np.where(ii > jj, 0.0, -30000.0).astype(f32),
        "tri": (np.arange(128)[:, None] <= np.arange(128)[None, :]).astype(f32),
        "og_a": A(a_out_norm_g[0]).reshape(128, 1), "a_w_out": A(a_w_out[0]), "gm0": _fm8(norm_mlp_g[0]),
        "m1_0": A(mlp_w1[0]), "m2_0": A(mlp_w2[0]), "gkv": _fm8(kv_norm_g), "gq": _fm8(norm_mix_g[1]),
        "kv_w": A(kv_w), "kv_wp": _perm_cols(A(kv_w)[:, :1024]), "wq": A(b_w_q[0]), "wqp": _perm_cols(A(b_w_q[0])),
        "lamT": np.ascontiguousarray(A(b_lambda[0]).T), "og_b": np.ascontiguousarray(A(b_subln_g[0]).reshape(2, 128).T),
        "b_w_out": A(b_w_out[0]), "gm1": _fm8(norm_mlp_g[1]), "m1_1": A(mlp_w1[1]), "m2_1": A(mlp_w2[1]), "fg": _fm8(final_g),
    }
    maps = []
    for cc in range(NCORES):
        b, r = cc // 4, cc % 4
        t0 = r * TPC
        m = dict(shared)
        m["xT"] = np.ascontiguousarray(x[b, t0:t0 + TPC].T)
        m["cT"] = np.ascontiguousarray(A(c)[b].reshape(8, 128).T.reshape(128, 8, 1))
        cw = np.empty((128, 24), f32)
        hp = np.empty((128, 4), f32)
        for hh in range(2):
            h = 2 * r + hh
            for X in range(3):
                r0 = X * 1024 + h * 128
                for j in range(4):
                    cw[:, (hh * 3 + X) * 4 + j] = A(a_conv_w[0])[j, r0:r0 + 128]
            hp[:, 2 * hh] = A(a_log[0])[h]
            hp[:, 2 * hh + 1] = A(a_dt_bias[0])[h]
        m["cw"], m["hp"] = cw, hp
        m["ropeC"], m["ropeS"] = _rope_tables(t0, TPC)
        m["idx"] = _idx_table(r)
        maps.append({k: np.ascontiguousarray(m[k]) for k in IN_SPECS})
    nc = build_program()
    res = run_bass_kernel_spmd(nc, maps, core_ids=list(range(NCORES)))
    out = np.empty((B, S, D), f32)
    for cc in range(NCORES):
        b, r = cc // 4, cc % 4
        out[b, r * TPC:(r + 1) * TPC] = np.asarray(res.results[cc]["outT"]).T
    return out
```

```python
import contextlib
from collections import defaultdict
import numpy as np
import concourse.bass as bass
import concourse.mybir as mybir

F32 = mybir.dt.float32
BF16 = mybir.dt.bfloat16
U32 = mybir.dt.uint32
AF = mybir.ActivationFunctionType
ALU = mybir.AluOpType
AX = mybir.AxisListType

ENGS = ['sp', 'act', 'dve', 'pool', 'pe']


class Buf:
    def __init__(self, h, name, tr=None, multi=False):
        self.h = h
        self.name = name
        self.multi = multi
        self.tr = tr if tr is not None else {'lw': None, 'rd': {}, 'ws': {}}

    def __getitem__(self, k):
        return self.h[k]

    def view(self, pattern, **kw):
        return Buf(self.h[:].rearrange(pattern, **kw), self.name, self.tr, self.multi)


class Sched:
    def __init__(self, nc):
        self.nc = nc
        self.es = contextlib.ExitStack()
        self.ops = {e: [] for e in ENGS}
        self.val = defaultdict(int)
        self.known = {e: defaultdict(int) for e in ENGS}
        self.sem = {}
        for e in ENGS:
            self.sem['c_' + e] = self.es.enter_context(nc.semaphore('c_' + e))
        self.NS = 8
        self.ndma = defaultdict(int)
        for e in ['sp', 'act', 'pool']:
            for k in range(self.NS):
                nm = f'd_{e}_{k}'
                self.sem[nm] = self.es.enter_context(nc.semaphore(nm))
        self.nbuf = 0
        self.pes = None
        self.sem['cc'] = self.es.enter_context(nc.semaphore('cc'))

    def begin_phase(self):
        self.pes = contextlib.ExitStack()
        if hasattr(self, '_stage'):
            del self._stage

    def end_phase(self, final=False):
        allc = [(c, v) for c, v in self.val.items() if v > 0]
        for eng in ENGS:
            waits = []
            for c, v in allc:
                if self.known[eng][c] < v:
                    waits.append((c, v))
                    self.known[eng][c] = v
            self.ops[eng].append((waits, None, None, 0))
        self._emit_block()
        if self.pes is not None:
            self.pes.close()
            self.pes = None
        if final:
            self.es.close()

    def _stack(self):
        return self.pes if self.pes is not None else self.es

    def sb(self, name, shape, dt):
        self.nbuf += 1
        return Buf(self._stack().enter_context(self.nc.sbuf_tensor(f"{name}_{self.nbuf}", list(shape), dt)), name)

    def ps(self, name, shape, dt=F32):
        self.nbuf += 1
        return Buf(self._stack().enter_context(self.nc.psum_tensor(f"{name}_{self.nbuf}", list(shape), dt)), name)

    def dram(self, h, name):
        return Buf(h, name, multi=True)

    def op(self, eng, fn, reads=(), writes=(), dma=False, cc=False):
        deps = {}
        if cc:
            ctr = 'cc'
        elif dma:
            k = self.ndma[eng] % self.NS
            self.ndma[eng] += 1
            ctr = f'd_{eng}_{k}'
            if self.val[ctr] > 0:
                deps[ctr] = self.val[ctr]
        else:
            ctr = 'c_' + eng
        for b in reads:
            lw = b.tr['lw']
            if lw is not None:
                deps[lw[0]] = max(deps.get(lw[0], 0), lw[1])
            if b.multi:
                for c, v in b.tr['ws'].items():
                    deps[c] = max(deps.get(c, 0), v)
        for b in writes:
            if b.multi:
                continue
            lw = b.tr['lw']
            if lw is not None:
                deps[lw[0]] = max(deps.get(lw[0], 0), lw[1])
            for c, v in b.tr['rd'].items():
                deps[c] = max(deps.get(c, 0), v)
        waits = []
        for c, v in deps.items():
            if eng == 'pe' and c == 'c_pe':
                continue
            if self.known[eng][c] < v:
                waits.append((c, v))
                self.known[eng][c] = v
        inc = 1 if cc else (16 if dma else 1)
        self.val[ctr] += inc
        v = self.val[ctr]
        self.ops[eng].append((waits, fn, ctr, inc))
        for b in reads:
            b.tr['rd'][ctr] = max(b.tr['rd'].get(ctr, 0), v)
        for b in writes:
            if b.multi:
                b.tr['ws'][ctr] = max(b.tr['ws'].get(ctr, 0), v)
                continue
            b.tr['lw'] = (ctr, v)
            b.tr['rd'] = {}

    def I(self, eng, method, reads, writes, *a, **kw):
        self.op(eng, lambda e: getattr(e, method)(*a, **kw), reads, writes)

    def dma(self, eng, out_ap, in_ap, reads=(), writes=(), **kw):
        self.op(eng, lambda e: e.dma_start(out=out_ap, in_=in_ap, **kw), reads, writes, dma=True)

    def allgather(self, in_ap, out_ap, in_buf, out_buf, groups):
        self.op('pool', lambda e: e.collective_compute("AllGather", ALU.bypass, replica_groups=groups,
                                                       ins=[in_ap.opt()], outs=[out_ap.opt()]),
                reads=[in_buf], writes=[out_buf], cc=True)

    def gather(self, out_ap, in_view, idx_ap, reads, writes):
        self.op('pool', lambda e: e.indirect_dma_start(out=out_ap, out_offset=None, in_=in_view,
                                                       in_offset=bass.IndirectOffsetOnAxis(ap=idx_ap, axis=0)),
                reads, writes, dma=True)

    def _emit_block(self):
        nc = self.nc
        with nc.Block() as block:
            decos = {'sp': block.sync, 'act': block.scalar, 'dve': block.vector,
                     'pool': block.gpsimd, 'pe': block.tensor}
            for eng in ENGS:
                ops = self.ops[eng]

                def body(e, ops=ops):
                    for waits, fn, ctr, inc in ops:
                        for c, v in waits:
                            e.wait_ge(self.sem[c], v)
                        if fn is not None:
                            ins = fn(e)
                            if inc == 1 and ctr == 'cc':
                                ins.then_inc(self.sem[ctr])
                            else:
                                ins.then_inc(self.sem[ctr], inc)
                decos[eng](body)
        self.nops_total = getattr(self, 'nops_total', 0) + sum(len(v) for v in self.ops.values())
        self.ops = {e: [] for e in ENGS}

    def emit(self):
        self.end_phase(final=True)

    def n_ops(self):
        return {e: len(self.ops[e]) for e in ENGS}


EPS = 1e-6


def consts(s):
    c = {}
    c['ones_bf'] = s.sb('ones_bf', [128, 128], BF16)
    s.op('pool', lambda e: e.memset(c['ones_bf'][:], 1.0), writes=[c['ones_bf']])
    return c


def phase_mod(s, C, cT_d, ws, modbT_d):
    NJ = sum(w.shape[1] for w in ws) // 128
    cs = s.sb('cs', [128, 8, 1], F32)
    s.dma('sp', cs[:], cT_d, writes=[cs])
    s.op('act', lambda e: e.activation(out=cs[:], in_=cs[:], func=AF.Silu), reads=[cs], writes=[cs])
    mb = s.sb('mb', [128, NJ], F32)
    s.dma('sp', mb[:], modbT_d, writes=[mb])
    ps = s.ps('modps', [128, 512])
    st = [s.sb('mst', [128, 8, 512], F32) for _ in range(2)]
    j = 0
    k = 0
    for w in ws:
        wv = w.rearrange("(kc p) n -> p kc n", p=128)
        for n0 in range(0, w.shape[1], 512):
            b = st[k % 2]
            s.dma('sp' if k % 2 == 0 else 'pool', b[:], wv[:, :, n0:n0 + 512], writes=[b])
            k += 1
            for jj in range(4):
                for kc in range(8):
                    s.I('pe', 'matmul', [b, cs], [ps], ps[:, j:j + 1], lhsT=b[:, kc, jj * 128:(jj + 1) * 128], rhs=cs[:, kc, :],
                        start=(kc == 0), stop=(kc == 7))
                j += 1
    modv = C['modv']
    s.I('dve', 'tensor_tensor', [mb], [ps, modv], out=modv[:], in0=ps[:, 0:NJ], in1=mb[:], op=ALU.add)


def get_stage(s):
    if not hasattr(s, '_stage'):
        s._stage = [s.sb('wst', [128, 2048], F32) for _ in range(2)]
        s._stage_i = 0
    return s._stage


def load_cast_w(s, Wb, w_d, piece=512, engs=('dve', 'pool')):
    K, N = w_d.shape
    KC = K // 128
    nk = 2048 // piece
    wv = w_d.rearrange("(kc p) n -> p kc n", p=128)
    st = get_stage(s)
    for k0 in range(0, KC, nk):
        kw = min(nk, KC - k0)
        for n0 in range(0, N, piece):
            nw = min(piece, N - n0)
            i = s._stage_i
            s._stage_i += 1
            b = st[i % 2]
            bv = b[:].rearrange("p (k n) -> p k n", n=piece)
            s.dma('sp' if i % 2 == 0 else 'pool', bv[:, 0:kw, 0:nw], wv[:, k0:k0 + kw, n0:n0 + nw], writes=[b])
            eng = engs[i % len(engs)]
            s.I(eng, 'tensor_copy', [b], [Wb], out=Wb[:, k0:k0 + kw, n0:n0 + nw], in_=bv[:, 0:kw, 0:nw])


def norm_mod(s, C, xs, G, SH, h, tmp, sq, ss_ps, rstd, D=1024, TB=512):
    KC = D // 128
    s.op('act', lambda e: e.activation(out=sq[:], in_=xs[:], func=AF.Square), reads=[xs], writes=[sq])
    for kc in range(KC):
        s.op('pe', lambda e, kc=kc: e.matmul(ss_ps[:, 0:TB], lhsT=C['ones_bf'][:], rhs=sq[:, kc, :],
                                              start=(kc == 0), stop=(kc == KC - 1)),
             reads=[sq, C['ones_bf']], writes=[ss_ps])
    s.op('dve', lambda e: e.tensor_scalar(out=rstd[:], in0=ss_ps[:, 0:TB], scalar1=1.0 / D, scalar2=EPS,
                                          op0=ALU.mult, op1=ALU.add), reads=[ss_ps], writes=[rstd])
    s.op('dve', lambda e: e.reciprocal(out=rstd[:], in_=rstd[:]), reads=[rstd], writes=[rstd])
    s.op('act', lambda e: e.activation(out=rstd[:], in_=rstd[:], func=AF.Sqrt), reads=[rstd], writes=[rstd])
    for kc in range(KC):
        t = tmp[kc % len(tmp)]
        s.op('dve', lambda e, kc=kc, t=t: e.scalar_tensor_tensor(
            out=t[:], in0=xs[:, kc, :], scalar=G[0][:, G[1] + kc:G[1] + kc + 1], in1=rstd[:],
            op0=ALU.mult, op1=ALU.mult), reads=[xs, G[0], rstd], writes=[t])
        s.op('act', lambda e, kc=kc, t=t: e.activation(
            out=h[:, kc, :], in_=t[:], func=AF.Identity, bias=SH[0][:, SH[1] + kc:SH[1] + kc + 1], scale=1.0),
             reads=[t, SH[0]], writes=[h])


def phase_l1(s, C, xT_d, gT_d, w_in_d, P1, Z1, after_tb=None, NT=4096, TB=512):
    NOUT = w_in_d.shape[1]
    modv = C['modv']
    gT = s.sb('gT', [128, 8], F32)
    s.dma('sp', gT[:], gT_d, writes=[gT])
    G = s.sb('G', [128, 8], F32)
    s.op('dve', lambda e: e.scalar_tensor_tensor(out=G[:], in0=modv[:, 8:16], scalar=1.0, in1=gT[:],
                                                 op0=ALU.add, op1=ALU.mult), reads=[modv, gT], writes=[G])
    Wb = s.sb('Wb', [128, 8, NOUT], BF16)
    load_cast_w(s, Wb, w_in_d)
    xs = [s.sb('xs', [128, 8, TB], F32) for _ in range(2)]
    sq = s.sb('sq', [128, 8, TB], BF16)
    hs = [s.sb('h', [128, 8, TB], BF16) for _ in range(2)]
    tmp = [s.sb('tmp', [128, TB], F32) for _ in range(2)]
    rstd = s.sb('rstd', [128, TB], F32)
    ss_ps = s.ps('ss', [128, 512])
    pps = [s.ps('pp', [128, 512]) for _ in range(4)]
    outs = [s.sb('ost', [128, TB], F32) for _ in range(4)]
    xv = xT_d.rearrange("(kc p) t -> p kc t", p=128)
    nchunks = [(n0, min(128, NOUT - n0)) for n0 in range(0, NOUT, 128)]
    it = 0
    for tb in range(NT // TB):
        x = xs[tb % 2]
        h = hs[tb % 2]
        s.dma('sp', x[:], xv[:, :, tb * TB:(tb + 1) * TB], writes=[x])
        norm_mod(s, C, x, (G, 0), (modv, 0), h, tmp, sq, ss_ps, rstd)
        for (n0, nw) in nchunks:
            pp = pps[it % 4]
            o = outs[it % 4]
            for kc in range(8):
                s.op('pe', lambda e, pp=pp, kc=kc, n0=n0, nw=nw, h=h: e.matmul(
                    pp[0:nw, 0:TB], lhsT=Wb[:, kc, n0:n0 + nw], rhs=h[:, kc, :], start=(kc == 0), stop=(kc == 7)),
                     reads=[Wb, h], writes=[pp])
            if it % 2 == 0:
                s.op('act', lambda e, pp=pp, o=o, nw=nw: e.copy(out=o[0:nw, :], in_=pp[0:nw, 0:TB]),
                     reads=[pp], writes=[o])
            else:
                s.op('dve', lambda e, pp=pp, o=o, nw=nw: e.tensor_copy(out=o[0:nw, :], in_=pp[0:nw, 0:TB]),
                     reads=[pp], writes=[o])
            if n0 < 3072:
                dst, db = P1.h[tb * 3136 + n0:tb * 3136 + n0 + nw, :], P1
            elif n0 < 4096:
                dst, db = Z1.h[n0 - 3072:n0 - 3072 + nw, tb * TB:(tb + 1) * TB], Z1
            else:
                dst, db = P1.h[tb * 3136 + 3072:tb * 3136 + 3072 + nw, :], P1
            s.dma('sp', dst, o[0:nw, :], reads=[o], writes=[db])
            it += 1
        if after_tb is not None:
            after_tb(tb)


L2EPS = 1e-6


def gdn_consts(s, C, ident_d, negTi_d, negTs_d):
    for nm, d in (('ident', ident_d), ('negTi', negTi_d), ('negTs', negTs_d)):
        C[nm] = s.sb(nm, [128, 128], F32)
        s.dma('sp', C[nm][:], d, writes=[C[nm]])
        C[nm + '_bf'] = s.sb(nm + '_bf', [128, 128], BF16)
        s.I('dve', 'tensor_copy', [C[nm]], [C[nm + '_bf']], out=C[nm + '_bf'][:], in_=C[nm][:])


class PsumPool:
    def __init__(self, s, nbanks, name):
        self.banks = [s.ps(f'{name}{i}', [128, 512]) for i in range(nbanks)]
        self.i = 0

    def get(self):
        b = self.banks[(self.i // 4) % len(self.banks)]
        sl = self.i % 4
        self.i += 1
        return b, b[:, sl * 128:(sl + 1) * 128]

    def get2(self):
        if self.i % 2:
            self.i += 1
        b = self.banks[(self.i // 4) % len(self.banks)]
        sl = self.i % 4
        self.i += 2
        return b, b[:, sl * 128:(sl + 2) * 128]

    def get_bank(self):
        self.i = ((self.i + 3) // 4) * 4
        b = self.banks[(self.i // 4) % len(self.banks)]
        self.i += 4
        return b


def phase_gdn(s, C, G1, cw_d, hp_d, O2, S=16384, NH=2, IDX_G1=0, IDX_AB=192, after_block=None):
    SB = 512
    NSB = S // SB
    scale = 128 ** -0.5
    ident = C['ident']
    idx = C['idx']
    g1v512 = G1.h[:, :]
    I = s.I
    cw = s.sb('cw', [128, NH * 3 * 4], F32)
    s.dma('sp', cw[:], cw_d, writes=[cw])
    hp = s.sb('hp', [128, NH * 2], F32)
    s.dma('sp', hp[:], hp_d, writes=[hp])
    pp = PsumPool(s, 6, 'lv')
    seqbank = [s.ps(f'seq{h}', [128, 512]) for h in range(NH)]
    bc127 = ident[:, 127:128].to_broadcast([128, 128])
    import math
    lnsc = s.sb('lnsc', [128, 1], F32)
    I('pool', 'memset', [], [lnsc], lnsc[:], math.log(scale))
    lnscale_c = lnsc[:, 0:1]

    tabs = []
    for h in range(NH):
        T = {}
        for nm in ('a', 'b', 'x', 't', 'gcum', 'rtab', 'gcumT', 'betaT', 'negg', 'be', 'kdf', 'glc', 'ones'):
            T[nm] = s.sb(f'{nm}{h}', [128, 128], F32)
        T['nA'] = s.sb(f'nA{h}', [128, 1], F32)
        a, b, x, t, nA = T['a'], T['b'], T['x'], T['t'], T['nA']
        g1v128 = G1.h[:, :].rearrange("r (b w) -> (r b) w", w=128)
        s.gather(a[:], g1v128, idx[:, IDX_AB + 2 * h:IDX_AB + 2 * h + 1], [G1, idx], [a])
        s.gather(b[:], g1v128, idx[:, IDX_AB + 2 * h + 1:IDX_AB + 2 * h + 2], [G1, idx], [b])
        I('pool', 'memset', [], [T['ones']], T['ones'][:], 1.0)
        I('dve', 'tensor_scalar', [a, hp], [x], out=x[:], in0=a[:], scalar1=hp[:, 2 * h + 1:2 * h + 2], scalar2=None, op0=ALU.add)
        I('act', 'activation', [x], [t], out=t[:], in_=x[:], func=AF.Abs)
        I('act', 'activation', [t], [t], out=t[:], in_=t[:], func=AF.Exp, scale=-1.0)
        I('act', 'activation', [t], [t], out=t[:], in_=t[:], func=AF.Ln, bias=1.0)
        I('dve', 'tensor_scalar_max', [x], [x], out=x[:], in0=x[:], scalar1=0.0)
        I('dve', 'tensor_tensor', [x, t], [x], out=x[:], in0=x[:], in1=t[:], op=ALU.add)
        I('act', 'activation', [hp], [nA], out=nA[:], in_=hp[:, 2 * h:2 * h + 1], func=AF.Exp)
        I('dve', 'tensor_scalar', [nA], [nA], out=nA[:], in0=nA[:], scalar1=-1.0, scalar2=None, op0=ALU.mult)
        I('dve', 'tensor_scalar', [x, nA], [x], out=x[:], in0=x[:], scalar1=nA[:, 0:1], scalar2=None, op0=ALU.mult)
        I('dve', 'tensor_tensor_scan', [T['ones'], x], [T['gcum']], out=T['gcum'][:], data0=T['ones'][:], data1=x[:],
          initial=0.0, op0=ALU.mult, op1=ALU.add)
        I('act', 'activation', [b], [b], out=b[:], in_=b[:], func=AF.Sigmoid)
        I('act', 'activation', [b], [t], out=t[:], in_=b[:], func=AF.Ln)
        I('dve', 'tensor_tensor', [T['gcum'], t], [T['rtab']], out=T['rtab'][:], in0=T['gcum'][:], in1=t[:], op=ALU.add)
        bk, p1 = pp.get()
        I('pe', 'transpose', [T['gcum'], ident], [bk], p1, T['gcum'][:], ident[:])
        I('act', 'copy', [], [bk, T['gcumT']], out=T['gcumT'][:], in_=p1)
        bk2, p2 = pp.get()
        I('pe', 'transpose', [b, ident], [bk2], p2, b[:], ident[:])
        I('act', 'copy', [], [bk2, T['betaT']], out=T['betaT'][:], in_=p2)
        bk3, p3 = pp.get()
        I('pe', 'matmul', [ident, T['gcumT']], [bk3], p3, lhsT=bc127, rhs=T['gcumT'][:], start=True, stop=True)
        I('act', 'activation', [], [bk3, T['glc']], out=T['glc'][:], in_=p3, func=AF.Exp)
        I('dve', 'tensor_tensor', [T['gcumT']], [bk3, T['kdf']], out=T['kdf'][:], in0=p3, in1=T['gcumT'][:], op=ALU.subtract)
        I('act', 'activation', [T['kdf']], [T['kdf']], out=T['kdf'][:], in_=T['kdf'][:], func=AF.Exp)
        I('dve', 'tensor_scalar', [T['gcumT']], [T['negg']], out=T['negg'][:], in0=T['gcumT'][:], scalar1=-1.0, scalar2=None, op0=ALU.mult)
        I('act', 'activation', [T['gcumT']], [T['be']], out=T['be'][:], in_=T['gcumT'][:], func=AF.Exp)
        I('dve', 'tensor_tensor', [T['be'], T['betaT']], [T['be']], out=T['be'][:], in0=T['be'][:], in1=T['betaT'][:], op=ALU.mult)
        T['cat'] = s.sb(f'cat{h}', [128, 384], F32)
        I('pool', 'tensor_copy', [T['gcum']], [T['cat']], out=T['cat'][:, 0:128], in_=T['gcum'][:])
        I('pool', 'tensor_copy', [T['rtab']], [T['cat']], out=T['cat'][:, 128:256], in_=T['rtab'][:])
        I('pool', 'tensor_copy', [T['gcum']], [T['cat']], out=T['cat'][:, 256:384], in_=T['gcum'][:])
        tabs.append(T)

    S32 = [s.sb(f'S32_{h}', [128, 128], F32) for h in range(NH)]
    Sbf = [s.sb(f'Sbf_{h}', [128, 128], BF16) for h in range(NH)]
    for h in range(NH):
        I('pool', 'memset', [], [S32[h]], S32[h][:], 0.0)
        I('pool', 'memset', [], [Sbf[h]], Sbf[h][:], 0.0)

    NSET = 3
    sets = []
    for u in range(NSET):
        U = {}
        U['xin'] = [s.sb(f'xin{u}{X}', [128, 3 + SB], F32) for X in range(3)]
        U['acc'] = [s.sb(f'acc{u}{X}', [128, SB], F32) for X in range(3)]
        U['cv'] = [s.sb(f'cv{u}{X}', [128, SB], F32) for X in range(3)]
        U['sq'] = s.sb(f'sq{u}', [128, SB], BF16)
        U['rn'] = s.sb(f'rn{u}', [128, SB], F32)
        U['xn'] = [s.sb(f'xn{u}{X}', [128, SB], F32) for X in range(2)]
        U['ost'] = s.sb(f'ost{u}', [128, 4, 128], F32)
        U['ch'] = []
        for c in range(4):
            D = {}
            for nm in ('kbg', 'vb', 'CT', 'DT', 'EGB', 'Q0', 'Q1', 'u'):
                D[nm] = s.sb(f'{nm}{u}{c}', [128, 128], F32)
            for nm in ('YP0', 'YP1'):
                D[nm] = s.sb(f'{nm}{u}{c}', [128, 256], F32)
            for nm in ('kdec', 'qkT', 'qg', 'wT', 'vnew'):
                D[nm] = s.sb(f'{nm}{u}{c}', [128, 128], BF16)
            U['ch'].append(D)
        sets.append(U)

    def stageA_gen(h, sb, U, Uprev):
        T = tabs[h]
        t0 = sb * SB
        for X in range(3):
            xin = U['xin'][X]
            if sb == 0:
                I('pool', 'memset', [], [xin], xin[:, 0:3], 0.0)
            else:
                pxin = Uprev['xin'][X]
                I('pool', 'tensor_copy', [pxin], [xin], out=xin[:, 0:3], in_=pxin[:, SB:SB + 3])
            col = IDX_G1 + (h * 3 + X) * 32 + sb
            s.gather(xin[:, 3:3 + SB], g1v512, idx[:, col:col + 1], [G1, idx], [xin])
        yield
        for X in range(3):
            xin = U['xin'][X]
            eng = 'dve'
            acc = U['acc'][X]
            wc = (h * 3 + X) * 4
            I(eng, 'tensor_scalar', [xin, cw], [acc], out=acc[:], in0=xin[:, 0:SB], scalar1=cw[:, wc:wc + 1], scalar2=None, op0=ALU.mult)
            for j in range(1, 4):
                if eng == 'dve':
                    I(eng, 'scalar_tensor_tensor', [xin, cw, acc], [acc], out=acc[:], in0=xin[:, j:j + SB],
                      scalar=cw[:, wc + j:wc + j + 1], in1=acc[:], op0=ALU.mult, op1=ALU.add)
                else:
                    tq = U['rn']
                    I(eng, 'tensor_scalar', [xin, cw], [tq], out=tq[:], in0=xin[:, j:j + SB], scalar1=cw[:, wc + j:wc + j + 1],
                      scalar2=None, op0=ALU.mult)
                    I(eng, 'tensor_tensor', [tq, acc], [acc], out=acc[:], in0=acc[:], in1=tq[:], op=ALU.add)
            cv = U['cv'][X]
            I('act', 'activation', [acc], [cv], out=cv[:], in_=acc[:], func=AF.Silu)
            yield
        for X in range(2):
            cv = U['cv'][X]
            sq, rn, xn = U['sq'], U['rn'], U['xn'][X]
            I('act', 'activation', [cv], [sq], out=sq[:], in_=cv[:], func=AF.Square)
            bk = pp.get_bank()
            I('pe', 'matmul', [C['ones_bf'], sq], [bk], bk[:, 0:SB], lhsT=C['ones_bf'][:], rhs=sq[:], start=True, stop=True)
            I('dve', 'tensor_scalar', [], [bk, rn], out=rn[:], in0=bk[:, 0:SB], scalar1=L2EPS, scalar2=None, op0=ALU.add)
            I('dve', 'reciprocal', [rn], [rn], out=rn[:], in_=rn[:])
            I('act', 'activation', [rn], [rn], out=rn[:], in_=rn[:], func=AF.Sqrt)
            I('pool', 'tensor_tensor', [cv, rn], [xn], out=xn[:], in0=cv[:], in1=rn[:], op=ALU.mult)
            yield

    def prep_gen(h, sb, U):
        T = tabs[h]
        qn, kn, vv = U['xn'][0], U['xn'][1], U['cv'][2]
        for c in range(4):
            D = U['ch'][c]
            n = sb * 4 + c
            cs = slice(c * 128, (c + 1) * 128)
            icol = ident[:, n:n + 1].to_broadcast([128, 128])
            be_c, kdf_c, bet_c, ng_c = (T[k][:, n:n + 1] for k in ('be', 'kdf', 'betaT', 'negg'))
            b_trk, p_trk = pp.get()
            I('pe', 'transpose', [kn, ident], [b_trk], p_trk, kn[:, cs], ident[:])
            b_trv, p_trv = pp.get()
            I('pe', 'transpose', [vv, ident], [b_trv], p_trv, vv[:, cs], ident[:])
            b_kk, p_kk = pp.get()
            I('pe', 'matmul', [kn], [b_kk], p_kk, lhsT=kn[:, cs], rhs=kn[:, cs], start=True, stop=True)
            b_bc = pp.get_bank()
            p_dt, p_ct, p_gb = b_bc[:, 0:128], b_bc[:, 128:256], b_bc[:, 256:384]
            I('pe', 'matmul', [ident, T['cat']], [b_bc], b_bc[:, 0:384], lhsT=icol, rhs=T['cat'][:], start=True, stop=False, skip_group_check=True)
            I('pe', 'matmul', [C['ident_bf'], C['negTi_bf']], [b_bc], p_dt, lhsT=C['ident_bf'][:], rhs=C['negTi_bf'][:],
              start=False, stop=False, skip_group_check=True)
            I('pe', 'matmul', [C['ident_bf'], C['negTs_bf']], [b_bc], p_ct, lhsT=C['ident_bf'][:], rhs=C['negTs_bf'][:],
              start=False, stop=True, skip_group_check=True)
            I('dve', 'tensor_scalar', [T['be']], [b_trk, D['kbg']], out=D['kbg'][:], in0=p_trk, scalar1=be_c, scalar2=None, op0=ALU.mult)
            I('dve', 'tensor_scalar', [T['kdf']], [b_trk, D['kdec']], out=D['kdec'][:], in0=p_trk, scalar1=kdf_c, scalar2=None, op0=ALU.mult)
            I('dve', 'tensor_scalar', [T['betaT']], [b_trv, D['vb']], out=D['vb'][:], in0=p_trv, scalar1=bet_c, scalar2=None, op0=ALU.mult)
            I('act', 'activation', [T['negg']], [b_bc, D['CT']], out=D['CT'][:], in_=p_ct, func=AF.Exp, bias=ng_c, scale=1.0)
            YP0 = D['YP0']
            I('dve', 'scalar_tensor_tensor', [D['CT']], [b_kk, YP0], out=YP0[:, 128:256], in0=p_kk, scalar=-1.0, in1=D['CT'][:],
              op0=ALU.mult, op1=ALU.mult)
            I('pool', 'tensor_copy', [ident], [YP0], out=YP0[:, 0:128], in_=ident[:])
            b_qk, p_qk = pp.get()
            I('pe', 'matmul', [kn, qn], [b_qk], p_qk, lhsT=kn[:, cs], rhs=qn[:, cs], start=True, stop=True)
            b_q, p_q = pp.get()
            I('pe', 'transpose', [YP0, ident], [b_q], p_q, YP0[:, 128:256], ident[:])
            I('act', 'activation', [T['negg']], [b_bc, D['DT']], out=D['DT'][:], in_=p_dt, func=AF.Exp, bias=ng_c, scale=1.0)
            I('dve', 'scalar_tensor_tensor', [D['DT']], [b_qk, D['qkT']], out=D['qkT'][:], in0=p_qk, scalar=scale, in1=D['DT'][:],
              op0=ALU.mult, op1=ALU.mult)
            I('act', 'activation', [lnsc], [b_bc, D['EGB']], out=D['EGB'][:], in_=p_gb, func=AF.Exp, bias=lnscale_c, scale=1.0)
            I('pool', 'tensor_tensor', [qn, D['EGB']], [D['qg']], out=D['qg'][:], in0=qn[:, cs], in1=D['EGB'][:], op=ALU.mult)
            I('dve', 'tensor_copy', [], [b_q, D['Q0']], out=D['Q0'][:], in_=p_q)
            yield
        m = 1
        cur = 0
        while m <= 64:
            nxt = 1 - cur
            YPc, YPn, Qc, Qn_ = f'YP{cur}', f'YP{nxt}', f'Q{cur}', f'Q{nxt}'
            for c in range(4):
                D = U['ch'][c]
                wa = 256 if m < 32 else 128
                if wa == 256:
                    ba, pa_ = pp.get2()
                else:
                    ba, pa_ = pp.get()
                I('pe', 'matmul', [D[Qc], D[YPc]], [ba], pa_, lhsT=D[Qc][:], rhs=D[YPc][:, 0:wa], start=True, stop=True)
                if m < 64:
                    b2, p2 = pp.get()
                    I('pe', 'matmul', [D[Qc], D[YPc]], [b2], p2, lhsT=D[YPc][:, 128:256], rhs=D[Qc][:], start=True, stop=True)
                I('dve', 'tensor_tensor', [D[YPc]], [ba, D[YPn]], out=D[YPn][:, 0:128], in0=pa_[:, 0:128], in1=D[YPc][:, 0:128], op=ALU.add)
                if wa == 256:
                    I('dve', 'tensor_copy', [], [ba, D[YPn]], out=D[YPn][:, 128:256], in_=pa_[:, 128:256])
                if m < 64:
                    I('act', 'copy', [], [b2, D[Qn_]], out=D[Qn_][:], in_=p2)
            cur = nxt
            m *= 2
            yield
        Yf = f'YP{cur}'
        for c in range(4):
            D = U['ch'][c]
            bu, pu = pp.get()
            I('pe', 'matmul', [D[Yf], D['vb']], [bu], pu, lhsT=D[Yf][:, 0:128], rhs=D['vb'][:], start=True, stop=True)
            bw, pw = pp.get()
            I('pe', 'matmul', [D[Yf], D['kbg']], [bw], pw, lhsT=D['kbg'][:], rhs=D[Yf][:, 0:128], start=True, stop=True)
            I('dve', 'tensor_copy', [], [bu, D['u']], out=D['u'][:], in_=pu)
            I('dve', 'tensor_copy', [], [bw, D['wT']], out=D['wT'][:], in_=pw)
        yield

    def seq_gen(h, sb, U):
        T = tabs[h]
        bk = seqbank[h]
        pvn, po, psu = bk[:, 0:128], bk[:, 128:256], bk[:, 256:384]
        Sb, Sf = Sbf[h], S32[h]
        for c in range(4):
            D = U['ch'][c]
            n = sb * 4 + c
            I('pe', 'matmul', [D['wT'], Sb], [bk], pvn, lhsT=D['wT'][:], rhs=Sb[:], start=True, stop=True)
            I('dve', 'tensor_tensor', [D['u']], [bk, D['vnew']], out=D['vnew'][:], in0=D['u'][:], in1=pvn, op=ALU.subtract)
            I('pe', 'matmul', [D['qg'], Sb], [bk], po, lhsT=Sb[:], rhs=D['qg'][:], start=True, stop=False)
            I('pe', 'matmul', [D['qkT'], D['vnew']], [bk], po, lhsT=D['vnew'][:], rhs=D['qkT'][:], start=False, stop=True)
            I('pe', 'matmul', [D['kdec'], D['vnew']], [bk], psu, lhsT=D['kdec'][:], rhs=D['vnew'][:], start=True, stop=True)
            I('dve', 'scalar_tensor_tensor', [Sf, T['glc']], [bk, Sf], out=Sf[:], in0=Sf[:], scalar=T['glc'][:, n:n + 1], in1=psu,
              op0=ALU.mult, op1=ALU.add)
            I('pool', 'tensor_copy', [Sf], [Sb], out=Sb[:], in_=Sf[:])
            I('dve', 'tensor_copy', [], [bk, U['ost']], out=U['ost'][:, c, :], in_=po)
            yield
        t0 = sb * SB
        r0 = (h * 4 + sb // 8) * 128
        c0 = (sb % 8) * SB
        s.dma('sp', O2.h[r0:r0 + 128, c0:c0 + SB], U['ost'][:].rearrange("p c e -> p (c e)"), reads=[U['ost']], writes=[O2])
        if after_block is not None and sb % 8 == 7:
            after_block(h * 4 + sb // 8)
        yield

    def run_rr(gens):
        gens = list(gens)
        while gens:
            for g in list(gens):
                try:
                    next(g)
                except StopIteration:
                    gens.remove(g)

    units = [(h, sb) for sb in range(NSB) for h in range(NH)]
    NU = len(units)

    def uset(u):
        return sets[u % NSET]

    run_rr([stageA_gen(units[0][0], units[0][1], uset(0), uset(-2))])
    for u in range(NU):
        h, sb = units[u]
        gens = [prep_gen(h, sb, uset(u))]
        if u + 1 < NU:
            h2, sb2 = units[u + 1]
            gens.append(stageA_gen(h2, sb2, uset(u + 1), uset(u - 1)))
        if u >= 1:
            hp_, sbp = units[u - 1]
            gens.append(seq_gen(hp_, sbp, uset(u - 1)))
        run_rr(gens)
    hl, sbl = units[NU - 1]
    run_rr([seq_gen(hl, sbl, uset(NU - 1))])


def rstd_from_ps(s, ss_ps_bank, ss_ap, rstd, D):
    I = s.I
    I('dve', 'tensor_scalar', [], [ss_ps_bank, rstd], out=rstd[:], in0=ss_ap, scalar1=1.0 / D, scalar2=EPS, op0=ALU.mult, op1=ALU.add)
    I('dve', 'reciprocal', [rstd], [rstd], out=rstd[:], in_=rstd[:])
    I('act', 'activation', [rstd], [rstd], out=rstd[:], in_=rstd[:], func=AF.Sqrt)


def phase_post(s, C, mode, Gin, idx_base, Z1, xsrc, xsrc_buf, gt_off, og_d, w_out_d, XO, lam_init=0.0, NT=4096, TB=512):
    I = s.I
    modv = C['modv']
    idx = C['idx']
    gv = Gin.h[:, :].rearrange("r (b w) -> (r b) w", w=TB)
    ncol = 1 if mode == 'gdn' else 2
    og = s.sb('og', [128, ncol], F32)
    s.dma('sp', og[:], og_d, writes=[og])
    if mode == 'attn':
        I('dve', 'tensor_scalar', [og], [og], out=og[:], in0=og[:], scalar1=1.0 - lam_init, scalar2=None, op0=ALU.mult)
    Wb = s.sb('Wb', [128, 8, 1024], BF16)
    load_cast_w(s, Wb, w_out_d)
    os_ = [s.sb('o', [128, 8, TB], F32) for _ in range(2)]
    zs = [s.sb('z', [128, 8, TB], F32) for _ in range(2)] if mode == 'gdn' else None
    xs = [s.sb('x', [128, 8, TB], F32) for _ in range(2)]
    sq = s.sb('sq', [128, 8, TB], BF16)
    hb = [s.sb('hb', [128, 8, TB], BF16) for _ in range(2)]
    rstd = [s.sb('rstd', [128, TB], F32) for _ in range(2)]
    t1 = [s.sb('t1', [128, TB], F32) for _ in range(2)]
    t2 = [s.sb('t2', [128, TB], F32) for _ in range(2)]
    ssb = [s.ps('ss', [128, 512]) for _ in range(2)]
    pps = [s.ps('pp', [128, 512]) for _ in range(4)]
    outs = [s.sb('ost', [128, TB], F32) for _ in range(4)]
    xv = xsrc.rearrange("(kc p) t -> p kc t", p=128)
    zv = Z1.h[:, :].rearrange("(kc p) t -> p kc t", p=128) if mode == 'gdn' else None
    it = 0
    k2 = 0
    for tb in range(NT // TB):
        tsl = slice(tb * TB, (tb + 1) * TB)
        o = os_[tb % 2]
        x = xs[tb % 2]
        h = hb[tb % 2]
        for kc in range(8):
            col = idx_base + kc * 8 + tb
            s.gather(o[:, kc, :], gv, idx[:, col:col + 1], [Gin, idx], [o])
        s.dma('sp', x[:], xv[:, :, tsl], reads=[xsrc_buf] if xsrc_buf is not None else [], writes=[x])
        if mode == 'gdn':
            z = zs[tb % 2]
            s.dma('sp', z[:], zv[:, :, tsl], reads=[Z1], writes=[z])
        I('act', 'activation', [o], [sq], out=sq[:], in_=o[:], func=AF.Square)
        if mode == 'gdn':
            for kc in range(8):
                sb_ = ssb[k2 % 2]
                r = rstd[k2 % 2]
                a1 = t1[k2 % 2]
                a2 = t2[k2 % 2]
                k2 += 1
                I('pe', 'matmul', [C['ones_bf'], sq], [sb_], sb_[:, 0:TB], lhsT=C['ones_bf'][:], rhs=sq[:, kc, :], start=True, stop=True)
                rstd_from_ps(s, sb_, sb_[:, 0:TB], r, 128)
                I('dve', 'tensor_tensor', [o, r], [a1], out=a1[:], in0=o[:, kc, :], in1=r[:], op=ALU.mult)
                I('act', 'activation', [z], [a2], out=a2[:], in_=z[:, kc, :], func=AF.Silu)
                I('pool', 'tensor_tensor', [a1, a2], [a1], out=a1[:], in0=a1[:], in1=a2[:], op=ALU.mult)
                I('act', 'mul', [a1, og], [h], out=h[:, kc, :], in_=a1[:], mul=og[:, 0:1])
        else:
            for hd in range(4):
                sb_ = ssb[k2 % 2]
                r = rstd[k2 % 2]
                k2 += 1
                for j in range(2):
                    I('pe', 'matmul', [C['ones_bf'], sq], [sb_], sb_[:, 0:TB], lhsT=C['ones_bf'][:], rhs=sq[:, 2 * hd + j, :],
                      start=(j == 0), stop=(j == 1))
                rstd_from_ps(s, sb_, sb_[:, 0:TB], r, 256)
                for j in range(2):
                    I('dve', 'scalar_tensor_tensor', [o, og, r], [h], out=h[:, 2 * hd + j, :], in0=o[:, 2 * hd + j, :],
                      scalar=og[:, j:j + 1], in1=r[:], op0=ALU.mult, op1=ALU.mult)
        for n in range(8):
            pp = pps[it % 4]
            ot = outs[it % 4]
            for kc in range(8):
                I('pe', 'matmul', [Wb, h], [pp], pp[:, 0:TB], lhsT=Wb[:, kc, n * 128:(n + 1) * 128], rhs=h[:, kc, :],
                  start=(kc == 0), stop=(kc == 7))
            I('dve', 'scalar_tensor_tensor', [modv, x], [pp, ot], out=ot[:], in0=pp[:, 0:TB],
              scalar=modv[:, gt_off + n:gt_off + n + 1], in1=x[:, n, :], op0=ALU.mult, op1=ALU.add)
            s.dma('sp', XO.h[n * 128:(n + 1) * 128, tsl], ot[:], reads=[ot], writes=[XO])
            it += 1


def phase_mlp(s, C, XI, moff, gT_d, w1_d, w2_d, xoT_d, XO=None, final_g_d=None, NT=4096, TB=256):
    I = s.I
    modv = C['modv']
    gT = s.sb('gT', [128, 8], F32)
    s.dma('sp', gT[:], gT_d, writes=[gT])
    G = s.sb('G', [128, 8], F32)
    I('dve', 'scalar_tensor_tensor', [modv, gT], [G], out=G[:], in0=modv[:, moff + 8:moff + 16], scalar=1.0, in1=gT[:],
      op0=ALU.add, op1=ALU.mult)
    if final_g_d is not None:
        fg = s.sb('fg', [128, 8], F32)
        s.dma('sp', fg[:], final_g_d, writes=[fg])
    W1 = s.sb('W1', [128, 8, 4096], BF16)
    W2 = s.sb('W2', [128, 32, 1024], BF16)
    load_cast_w(s, W1, w1_d)
    load_cast_w(s, W2, w2_d)
    xs = [b.view("p (k t) -> p k t", t=TB) for b in get_stage(s)]
    assert TB == 256
    sq = s.sb('sq', [128, 8, TB], BF16)
    hs = [s.sb('h', [128, 8, TB], BF16) for _ in range(2)]
    tmp = [s.sb('tmp', [128, TB], F32) for _ in range(2)]
    rstd = s.sb('rstd', [128, TB], F32)
    hid = s.sb('hid', [128, 32, TB], BF16)
    rl = [s.sb('rl', [128, TB], F32) for _ in range(2)]
    xo = [s.sb('xo', [128, 8, TB], F32) for _ in range(1)]
    ss_ps = s.ps('ss', [128, 512])
    pps = [s.ps('pp', [128, 512]) for _ in range(6)]
    xv = XI.h[:, :].rearrange("(kc p) t -> p kc t", p=128)
    ov = xoT_d.rearrange("(kc p) t -> p kc t", p=128)
    owr = [XO] if XO is not None else []
    it = 0
    for tb in range(NT // TB):
        tsl = slice(tb * TB, (tb + 1) * TB)
        x = xs[tb % 2]
        h = hs[tb % 2]
        xout = xo[0]
        s.dma('sp', x[:], xv[:, :, tsl], reads=[XI], writes=[x])
        norm_mod(s, C, x, (G, 0), (modv, moff), h, tmp, sq, ss_ps, rstd, TB=TB)
        for f in range(32):
            pp = pps[it % 6]
            r = rl[it % 2]
            it += 1
            for kc in range(8):
                I('pe', 'matmul', [W1, h], [pp], pp[:, 0:TB], lhsT=W1[:, kc, f * 128:(f + 1) * 128], rhs=h[:, kc, :],
                  start=(kc == 0), stop=(kc == 7))
            I('act', 'activation', [], [pp, r], out=r[:], in_=pp[:, 0:TB], func=AF.Relu)
            I('pool', 'tensor_tensor', [r], [hid], out=hid[:, f, :], in0=r[:], in1=r[:], op=ALU.mult)
        for n in range(8):
            pp = pps[it % 6]
            it += 1
            for f in range(32):
                I('pe', 'matmul', [W2, hid], [pp], pp[:, 0:TB], lhsT=W2[:, f, n * 128:(n + 1) * 128], rhs=hid[:, f, :],
                  start=(f == 0), stop=(f == 31))
            I('dve', 'scalar_tensor_tensor', [modv, x], [pp, xout], out=xout[:, n, :], in0=pp[:, 0:TB],
              scalar=modv[:, moff + 16 + n:moff + 17 + n], in1=x[:, n, :], op0=ALU.mult, op1=ALU.add)
        if final_g_d is None:
            s.dma('sp', ov[:, :, tsl], xout[:], reads=[xout], writes=owr)
        else:
            I('act', 'activation', [xout], [sq], out=sq[:], in_=xout[:], func=AF.Square)
            for kc in range(8):
                I('pe', 'matmul', [C['ones_bf'], sq], [ss_ps], ss_ps[:, 0:TB], lhsT=C['ones_bf'][:], rhs=sq[:, kc, :],
                  start=(kc == 0), stop=(kc == 7))
            rstd_from_ps(s, ss_ps, ss_ps[:, 0:TB], rstd, 1024)
            for kc in range(8):
                I('dve', 'scalar_tensor_tensor', [xout, fg, rstd], [xout], out=xout[:, kc, :], in0=xout[:, kc, :],
                  scalar=fg[:, kc:kc + 1], in1=rstd[:], op0=ALU.mult, op1=ALU.mult)
            s.dma('sp', ov[:, :, tsl], xout[:], reads=[xout], writes=owr)


def phase_kvq(s, C, XI, gkvT_d, gqT_d, kv_w_d, kv_wp_d, wq_d, wqp_d, ropeC_d, ropeS_d, Q3, K3, V3, after_tb=None, NT=4096, TB=512):
    I = s.I
    modv = C['modv']
    gkv = s.sb('gkv', [128, 8], F32)
    s.dma('sp', gkv[:], gkvT_d, writes=[gkv])
    gq = s.sb('gq', [128, 8], F32)
    s.dma('sp', gq[:], gqT_d, writes=[gq])
    Gkv = s.sb('Gkv', [128, 8], F32)
    Gq = s.sb('Gq', [128, 8], F32)
    I('dve', 'scalar_tensor_tensor', [modv, gkv], [Gkv], out=Gkv[:], in0=modv[:, 104:112], scalar=1.0, in1=gkv[:], op0=ALU.add, op1=ALU.mult)
    I('dve', 'scalar_tensor_tensor', [modv, gq], [Gq], out=Gq[:], in0=modv[:, 56:64], scalar=1.0, in1=gq[:], op0=ALU.add, op1=ALU.mult)
    Wkv = s.sb('Wkv', [128, 8, 2048], BF16)
    Wkp = s.sb('Wkp', [128, 8, 1024], BF16)
    Wq = s.sb('Wq', [128, 8, 1024], BF16)
    Wqp = s.sb('Wqp', [128, 8, 1024], BF16)
    load_cast_w(s, Wkv, kv_w_d)
    load_cast_w(s, Wkp, kv_wp_d)
    load_cast_w(s, Wq, wq_d)
    load_cast_w(s, Wqp, wqp_d)
    xs = [s.sb('xs', [128, 8, TB], F32) for _ in range(2)]
    sq = s.sb('sq', [128, 8, TB], BF16)
    hkv = s.sb('hkv', [128, 8, TB], BF16)
    hq = s.sb('hq', [128, 8, TB], BF16)
    tmp = [s.sb('tmp', [128, TB], F32) for _ in range(2)]
    rstd = s.sb('rstd', [128, TB], F32)
    rc = [s.sb('rc', [128, TB], F32) for _ in range(2)]
    rs = [s.sb('rs', [128, TB], F32) for _ in range(2)]
    ta = [s.sb('ta', [128, TB], F32) for _ in range(3)]
    tb_ = [s.sb('tb', [128, TB], F32) for _ in range(3)]
    outs = [s.sb('ost', [128, TB], BF16) for _ in range(4)]
    vouts = [s.sb('vost', [128, 256], BF16) for _ in range(4)]
    ss_ps = s.ps('ss', [128, 512])
    pps = [s.ps('pp', [128, 512]) for _ in range(6)]
    xv = XI.h[:, :].rearrange("(kc p) t -> p kc t", p=128)
    it = 0
    io = 0
    for tb in range(NT // TB):
        tsl = slice(tb * TB, (tb + 1) * TB)
        x = xs[tb % 2]
        cc = rc[tb % 2]
        sn = rs[tb % 2]
        s.dma('sp', x[:], xv[:, :, tsl], reads=[XI], writes=[x])
        s.dma('sp', cc[:], ropeC_d[:, tsl], writes=[cc])
        s.dma('sp', sn[:], ropeS_d[:, tsl], writes=[sn])
        norm_mod(s, C, x, (Gkv, 0), (modv, 96), hkv, tmp, sq, ss_ps, rstd, TB=TB)
        norm_mod(s, C, x, (Gq, 0), (modv, 48), hq, tmp, sq, ss_ps, rstd, TB=TB)
        for (W, Wp, hh, dst) in ((Wkv, Wkp, hkv, K3), (Wq, Wqp, hq, Q3)):
            for n in range(8):
                pa = pps[it % 6]
                it += 1
                for kc in range(8):
                    I('pe', 'matmul', [W, hh], [pa], pa[:, 0:TB], lhsT=W[:, kc, n * 128:(n + 1) * 128], rhs=hh[:, kc, :],
                      start=(kc == 0), stop=(kc == 7))
                pb = pps[it % 6]
                it += 1
                for kc in range(8):
                    I('pe', 'matmul', [Wp, hh], [pb], pb[0:32, 0:TB], lhsT=Wp[:, kc, n * 128:n * 128 + 32], rhs=hh[:, kc, :],
                      start=(kc == 0), stop=(kc == 7))
                ot = outs[io % 4]
                a = ta[io % 3]
                b = tb_[io % 3]
                I('dve', 'tensor_tensor', [cc], [pa, a], out=a[:], in0=pa[:, 0:TB], in1=cc[:], op=ALU.mult)
                I('dve', 'tensor_tensor', [sn], [pb, b], out=b[0:32, :], in0=pb[0:32, 0:TB], in1=sn[0:32, :], op=ALU.mult)
                I('pool', 'tensor_tensor', [a, b], [a], out=a[0:32, :], in0=a[0:32, :], in1=b[0:32, :], op=ALU.add)
                I('act', 'copy', [a], [ot], out=ot[:], in_=a[:])
                s.dma('sp', dst.h[tb * 1024 + n * 128:tb * 1024 + (n + 1) * 128, :], ot[:], reads=[ot], writes=[dst])
                io += 1
        for tsb in range(TB // 128):
            for hd in range(4):
                pa = pps[it % 6]
                it += 1
                for kc in range(8):
                    I('pe', 'matmul', [Wkv, hkv], [pa], pa[:, 0:256], lhsT=hkv[:, kc, tsb * 128:(tsb + 1) * 128],
                      rhs=Wkv[:, kc, 1024 + hd * 256:1024 + (hd + 1) * 256], start=(kc == 0), stop=(kc == 7))
                vo = vouts[io % 4]
                io += 1
                I('act', 'copy', [], [pa, vo], out=vo[:], in_=pa[:, 0:256])
                r0 = (tb * 4 + hd) * TB + tsb * 128
                s.dma('sp', V3.h[r0:r0 + 128, :], vo[:], reads=[vo], writes=[V3])
        if after_tb is not None:
            after_tb(tb)


def phase_attn(s, C, G3q, G3k, G3v, lamT_d, O4, lam_init, S=16384, IDX_QK=260, IDX_V=324, after_block=None):
    I = s.I
    scale = 128 ** -0.5
    NKB = S // 128
    NQB = S // 512
    ones_bf = C['ones_bf']
    ones_f = s.sb('ones_f', [128, 128], F32)
    I('pool', 'memset', [], [ones_f], ones_f[:], 1.0)
    tri = C['tri']
    idx = C['idx']
    ident = C['ident']
    gq = G3q.h[:, :]
    gk = G3k.h[:, :]
    misc = s.ps('misc', [128, 512])
    lp = s.sb('lp', [128, 4], F32)
    s.dma('sp', lp[:], lamT_d, writes=[lp])
    pr = s.sb('pr', [128, 2], F32)
    I('dve', 'tensor_tensor', [lp], [pr], out=pr[:, 0:1], in0=lp[:, 0:1], in1=lp[:, 1:2], op=ALU.mult)
    I('dve', 'tensor_tensor', [lp], [pr], out=pr[:, 1:2], in0=lp[:, 2:3], in1=lp[:, 3:4], op=ALU.mult)
    I('pe', 'matmul', [ones_f, pr], [misc], misc[:, 0:2], lhsT=ones_f[:], rhs=pr[:], start=True, stop=True)
    ex = s.sb('ex', [128, 2], F32)
    I('act', 'activation', [], [misc, ex], out=ex[:], in_=misc[:, 0:2], func=AF.Exp)
    nlam = s.sb('nlam', [128, 1], F32)
    I('dve', 'tensor_tensor', [ex], [nlam], out=nlam[:], in0=ex[:, 1:2], in1=ex[:, 0:1], op=ALU.subtract)
    I('dve', 'tensor_scalar', [nlam], [nlam], out=nlam[:], in0=nlam[:], scalar1=-lam_init, scalar2=None, op0=ALU.add)
    KbB = [[s.sb(f'Kb{m}_{j}', [128, 512], BF16) for j in range(NQB)] for m in range(2)]
    VbB = [s.sb(f'Vb{j}', [128, 4, 257], BF16) for j in range(NQB)]
    sqs = [s.sb('sqs', [128, 512], BF16) for _ in range(2)]
    kmx = [s.sb(f'kmx{m}', [128, NQB], F32) for m in range(2)]
    krun = [s.sb(f'krun{m}', [128, NQB], F32) for m in range(2)]
    lb_i = [0]

    def load_block(j):
        for m in range(2):
            sq = sqs[lb_i[0] % 2]
            lb_i[0] += 1
            col = IDX_QK + m * 32 + j
            kb_ = KbB[m][j]
            s.gather(kb_[:], gk, idx[:, col:col + 1], [G3k, idx], [kb_])
            I('act', 'activation', [kb_], [sq], out=sq[:], in_=kb_[:], func=AF.Square)
            I('pe', 'matmul', [ones_bf, sq], [misc], misc[:, 0:512], lhsT=ones_bf[:], rhs=sq[:], start=True, stop=True)
            I('dve', 'reduce_max', [], [misc, kmx[m]], out=kmx[m][:, j:j + 1], in_=misc[:, 0:512], axis=AX.X)
            if j == 0:
                I('dve', 'tensor_copy', [kmx[m]], [krun[m]], out=krun[m][:, 0:1], in_=kmx[m][:, 0:1])
            else:
                I('dve', 'tensor_tensor', [kmx[m], krun[m]], [krun[m]], out=krun[m][:, j:j + 1], in0=krun[m][:, j - 1:j],
                  in1=kmx[m][:, j:j + 1], op=ALU.max)
        vb_ = VbB[j]
        I('pool', 'memset', [], [vb_], vb_[:, :, 256:257], 1.0)
        for q4 in range(4):
            kb = 4 * j + q4
            s.gather(vb_[:, q4, 0:256], G3v.h[:, :], idx[:, IDX_V + kb:IDX_V + kb + 1], [G3v, idx], [vb_])

    load_block(0)
    acc = [s.ps(f'acc{j}', [128, 512]) for j in range(4)]
    stp = [s.ps(f'st{j}', [128, 512]) for j in range(3)]
    qbf = [s.sb('qbf', [128, 512], BF16) for _ in range(2)]
    PT = [s.sb('PT', [128, 512], BF16) for _ in range(3)]
    mq = [s.sb('mq', [128, 1], F32) for _ in range(2)]
    bias = [s.sb('bias', [128, 1], F32) for _ in range(2)]
    o1 = s.sb('o1', [128, 4, 256], F32)
    rl = [s.sb('rl', [128, 1], F32) for _ in range(4)]
    ot = [s.sb('ot', [128, 256], F32) for _ in range(4)]
    oT_st = [s.sb('oTs', [128, 256], F32) for _ in range(2)]
    it = 0
    qi = 0
    for qb in range(NQB):
        if qb + 1 < NQB:
            load_block(qb + 1)
        for m in range(2):
            qb_ = qbf[qi % 2]
            sq = sqs[qi % 2]
            mq_ = mq[qi % 2]
            bi = bias[qi % 2]
            qi += 1
            col = IDX_QK + m * 32 + qb
            s.gather(qb_[:], gq, idx[:, col:col + 1], [G3q, idx], [qb_])
            I('act', 'activation', [qb_], [sq], out=sq[:], in_=qb_[:], func=AF.Square)
            I('pe', 'matmul', [ones_bf, sq], [misc], misc[:, 0:512], lhsT=ones_bf[:], rhs=sq[:], start=True, stop=True)
            I('dve', 'reduce_max', [], [misc, mq_], out=mq_[:], in_=misc[:, 0:512], axis=AX.X)
            I('dve', 'tensor_tensor', [mq_, krun[m]], [mq_], out=mq_[:], in0=mq_[:], in1=krun[m][:, qb:qb + 1], op=ALU.mult)
            I('act', 'activation', [mq_], [mq_], out=mq_[:], in_=mq_[:], func=AF.Sqrt)
            I('dve', 'tensor_scalar', [mq_], [bi], out=bi[:], in0=mq_[:], scalar1=-scale, scalar2=None, op0=ALU.mult)
            nkb = 4 * qb + 4
            LOOK = 2
            slots = {}

            def issue_st(kb):
                nonlocal it
                r = kb - 4 * qb
                c0 = 128 * max(r, 0)
                sp_ = stp[it % 3]
                pt = PT[it % 3]
                it += 1
                slots[kb] = (sp_, pt, r, c0)
                kbuf = KbB[m][kb // 4]
                I('pe', 'matmul', [kbuf, qb_], [sp_], sp_[:, c0:512], lhsT=kbuf[:, (kb % 4) * 128:(kb % 4 + 1) * 128], rhs=qb_[:, c0:512],
                  start=True, stop=True)

            for kb in range(min(LOOK, nkb)):
                issue_st(kb)
            for kb in range(nkb):
                sp_, pt, r, c0 = slots.pop(kb)
                I('act', 'activation', [bi], [sp_, pt], out=pt[:, c0:512], in_=sp_[:, c0:512], func=AF.Exp, bias=bi[:, 0:1], scale=scale)
                if r >= 0:
                    I('pool', 'tensor_tensor', [pt, tri], [pt], out=pt[:, c0:c0 + 128], in0=pt[:, c0:c0 + 128], in1=tri[:], op=ALU.mult)
                if kb + LOOK < nkb:
                    issue_st(kb + LOOK)
                for j in range(max(r, 0), 4):
                    vbuf = VbB[kb // 4]
                    I('pe', 'matmul', [pt, vbuf], [acc[j]], acc[j][:, 0:257], lhsT=pt[:, 128 * j:128 * (j + 1)], rhs=vbuf[:, kb % 4, :],
                      start=(kb == 0), stop=(kb == 4 * qb + j))
            for j in range(4):
                a = acc[j]
                I('dve', 'reciprocal', [], [a, rl[j]], out=rl[j][:], in_=a[:, 256:257])
                if m == 0:
                    I('act', 'mul', [rl[j]], [a, o1], out=o1[:, j, :], in_=a[:, 0:256], mul=rl[j][:, 0:1])
                else:
                    I('dve', 'tensor_tensor', [rl[j], nlam], [rl[j]], out=rl[j][:], in0=rl[j][:], in1=nlam[:], op=ALU.mult)
                    I('dve', 'scalar_tensor_tensor', [rl[j], o1], [a, ot[j]], out=ot[j][:], in0=a[:, 0:256], scalar=rl[j][:, 0:1],
                      in1=o1[:, j, :], op0=ALU.mult, op1=ALU.add)
                    for cch in range(2):
                        I('pe', 'transpose', [ot[j], ident], [misc], misc[:, cch * 128:(cch + 1) * 128],
                          ot[j][:, cch * 128:(cch + 1) * 128], ident[:])
                    oTs = oT_st[(qb * 4 + j) % 2]
                    I('act', 'copy', [], [misc, oTs], out=oTs[:], in_=misc[:, 0:256])
                    tq = qb // 8
                    c0 = (qb % 8) * 512 + j * 128
                    for cch in range(2):
                        r0 = tq * 256 + cch * 128
                        s.dma('sp', O4.h[r0:r0 + 128, c0:c0 + 128], oTs[:, cch * 128:(cch + 1) * 128], reads=[oTs], writes=[O4])
                    if after_block is not None and qb % 8 == 7 and j == 3:
                        after_block(qb // 8)

import math
from concourse.bass_utils import run_bass_kernel_spmd

NCORES = 8
SEQ = 16384
TPC = 4096
LAM_INIT1 = 0.8 - 0.6 * math.exp(-0.3 * 1)
ROT_DIM = 32
ROPE_THETA = 500000.0
NIDX = 516
IDX_G1, IDX_AB, IDX_G2, IDX_QK, IDX_V, IDX_G4 = 0, 192, 196, 260, 324, 452
GROUPS = [[0, 1, 2, 3], [4, 5, 6, 7]]

IN_SPECS = {
    "xT": ((1024, TPC), F32), "cT": ((128, 8, 1), F32), "w0": ((1024, 6144), F32), "w1": ((1024, 6144), F32),
    "w2": ((1024, 2048), F32), "mb": ((128, 112), F32), "gT0": ((128, 8), F32), "w_in": ((1024, 4112), F32),
    "cw": ((128, 24), F32), "hp": ((128, 4), F32), "ident": ((128, 128), F32), "negTi": ((128, 128), F32),
    "negTs": ((128, 128), F32), "tri": ((128, 128), F32), "idx": ((128, NIDX), U32), "og_a": ((128, 1), F32),
    "a_w_out": ((1024, 1024), F32), "gm0": ((128, 8), F32), "m1_0": ((1024, 4096), F32), "m2_0": ((4096, 1024), F32),
    "gkv": ((128, 8), F32), "gq": ((128, 8), F32), "kv_w": ((1024, 2048), F32), "kv_wp": ((1024, 1024), F32),
    "wq": ((1024, 1024), F32), "wqp": ((1024, 1024), F32), "ropeC": ((128, TPC), F32), "ropeS": ((128, TPC), F32),
    "lamT": ((128, 4), F32), "og_b": ((128, 2), F32), "b_w_out": ((1024, 1024), F32), "gm1": ((128, 8), F32),
    "m1_1": ((1024, 4096), F32), "m2_1": ((4096, 1024), F32), "fg": ((128, 8), F32),
}


def build_program(stop_after=99):
    nc = bass.Bass("TRN2", target_bir_lowering=False)
    d = {k: nc.dram_tensor(k, list(shp), dt, kind="ExternalInput").ap() for k, (shp, dt) in IN_SPECS.items()}
    outT = nc.dram_tensor("outT", [1024, TPC], F32, kind="ExternalOutput").ap()
    s = Sched(nc)

    def scratch(name, shape, dt=F32):
        return s.dram(nc.dram_tensor(name, list(shape), dt), name)

    P1 = scratch("P1", (8 * 3136, 512))
    Z1 = scratch("Z1", (1024, TPC))
    G1 = scratch("G1", (4 * 8 * 3136, 512))
    O2 = scratch("O2", (1024, TPC))
    G2 = scratch("G2", (4096, TPC))
    XM = scratch("XM", (1024, TPC))
    X1 = scratch("X1", (1024, TPC))
    Q3 = scratch("Q3", (8 * 1024, 512), BF16)
    K3 = scratch("K3", (8 * 1024, 512), BF16)
    V3 = scratch("V3", (4 * TPC, 256), BF16)
    G3q = scratch("G3q", (4 * 8 * 1024, 512), BF16)
    G3k = scratch("G3k", (4 * 8 * 1024, 512), BF16)
    G3v = scratch("G3v", (16 * TPC, 256), BF16)
    O4 = scratch("O4", (1024, TPC))
    G4 = scratch("G4", (4096, TPC))
    XM2 = scratch("XM2", (1024, TPC))

    C = consts(s)
    gdn_consts(s, C, d["ident"], d["negTi"], d["negTs"])
    trif = s.sb('trif', [128, 128], F32)
    s.dma('sp', trif[:], d["tri"], writes=[trif])
    C['tri'] = s.sb('tri', [128, 128], BF16)
    s.I('dve', 'tensor_copy', [trif], [C['tri']], out=C['tri'][:], in_=trif[:])
    C['idx'] = s.sb('idx', [128, NIDX], U32)
    s.dma('sp', C['idx'][:], d["idx"], writes=[C['idx']])
    C['modv'] = s.sb('modvg', [128, 112], F32)

    def ag(src, dst):
        R, Fc = src.h.shape
        esz = 2 if src.h.dtype == BF16 else 4
        RS = (1 << 20) // (Fc * esz)
        assert R % RS == 0
        for k in range(R // RS):
            ag_slice(src, dst, k, RS)

    def ag_slice(src, dst, k, RS):
        s.allgather(src.h[k * RS:(k + 1) * RS, :], dst.h[k * 4 * RS:(k + 1) * 4 * RS, :], src, dst, GROUPS)

    def p1_after_tb(tb):
        lo = ((tb * 3136 + 511) // 512) if tb > 0 else 0
        for k in range(49):
            if (512 * k + 511) // 3136 == tb:
                ag_slice(P1, G1, k, 512)

    def o2_after_block(blk):
        for k in (2 * blk, 2 * blk + 1):
            ag_slice(O2, G2, k, 64)

    def kvq_after_tb(tb):
        ag_slice(Q3, G3q, tb, 1024)
        ag_slice(K3, G3k, tb, 1024)
        ag_slice(V3, G3v, tb, 2048)

    def o4_after_block(tq):
        for k in range(4 * tq, 4 * tq + 4):
            ag_slice(O4, G4, k, 64)

    s.begin_phase()
    phase_mod(s, C, d["cT"], [d["w0"], d["w1"], d["w2"]], d["mb"])
    s.end_phase()
    if stop_after == 1:
        s.begin_phase()
        s.end_phase(final=True)
        return nc
    s.begin_phase()
    phase_l1(s, C, d["xT"], d["gT0"], d["w_in"], P1, Z1, after_tb=p1_after_tb)
    s.end_phase()
    if stop_after == 2:
        s.begin_phase()
        s.end_phase(final=True)
        return nc
    s.begin_phase()
    phase_gdn(s, C, G1, d["cw"], d["hp"], O2, S=SEQ, IDX_G1=IDX_G1, IDX_AB=IDX_AB, after_block=o2_after_block)
    s.end_phase()
    if stop_after == 3:
        s.begin_phase()
        s.end_phase(final=True)
        return nc
    s.begin_phase()
    phase_post(s, C, 'gdn', G2, IDX_G2, Z1, d["xT"], None, 16, d["og_a"], d["a_w_out"], XM)
    s.end_phase()
    if stop_after == 4:
        s.begin_phase()
        s.end_phase(final=True)
        return nc
    s.begin_phase()
    phase_mlp(s, C, XM, 24, d["gm0"], d["m1_0"], d["m2_0"], X1.h[:, :], XO=X1)
    s.end_phase()
    if stop_after == 5:
        s.begin_phase()
        s.end_phase(final=True)
        return nc
    s.begin_phase()
    phase_kvq(s, C, X1, d["gkv"], d["gq"], d["kv_w"], d["kv_wp"], d["wq"], d["wqp"], d["ropeC"], d["ropeS"], Q3, K3, V3, after_tb=kvq_after_tb)
    s.end_phase()
    if stop_after == 6:
        s.begin_phase()
        s.end_phase(final=True)
        return nc
    s.begin_phase()
    phase_attn(s, C, G3q, G3k, G3v, d["lamT"], O4, LAM_INIT1, S=SEQ, IDX_QK=IDX_QK, IDX_V=IDX_V, after_block=o4_after_block)
    s.end_phase()
    if stop_after == 7:
        s.begin_phase()
        s.end_phase(final=True)
        return nc
    s.begin_phase()
    phase_post(s, C, 'attn', G4, IDX_G4, None, X1.h[:, :], X1, 64, d["og_b"], d["b_w_out"], XM2, lam_init=LAM_INIT1)
    s.end_phase()
    if stop_after == 8:
        s.begin_phase()
        s.end_phase(final=True)
        return nc
    s.begin_phase()
    phase_mlp(s, C, XM2, 72, d["gm1"], d["m1_1"], d["m2_1"], outT, final_g_d=d["fg"])
    s.end_phase(final=True)
    return nc


def _fm8(v):
    return np.ascontiguousarray(np.asarray(v, np.float32).reshape(-1, 128).T)


def _perm_cols(w):
    w = np.asarray(w)
    out = w.copy()
    for c in range(w.shape[1] // 128):
        b = c * 128
        out[:, b:b + 16] = w[:, b + 16:b + 32]
        out[:, b + 16:b + 32] = w[:, b:b + 16]
    return out


def _rope_tables(t0, n):
    pos = (t0 + np.arange(n)).astype(np.float32)
    inv = (np.float32(ROPE_THETA) ** (-(np.arange(0, ROT_DIM, 2, dtype=np.float32)) / np.float32(ROT_DIM))).astype(np.float32)
    fr = (pos[:, None] * inv[None, :]).astype(np.float32)
    cos = np.cos(fr).astype(np.float32).T
    sin = np.sin(fr).astype(np.float32).T
    Cc = np.ones((128, n), np.float32)
    Ss = np.zeros((128, n), np.float32)
    Cc[0:16] = cos
    Cc[16:32] = cos
    Ss[0:16] = -sin
    Ss[16:32] = sin
    return Cc, Ss


def _grow(src, row, RS):
    return ((row // RS) * 4 + src) * RS + row % RS


def _idx_table(r):
    p = np.arange(128, dtype=np.int64)
    t = np.zeros((128, NIDX), np.int64)
    for hh in range(2):
        h = 2 * r + hh
        for X in range(3):
            for sb in range(32):
                t[:, IDX_G1 + (hh * 3 + X) * 32 + sb] = _grow(sb // 8, (sb % 8) * 3136 + X * 1024 + h * 128 + p, 512)
        t[:, IDX_AB + 2 * hh] = _grow(p // 32, ((p % 32) // 4) * 3136 + 3072 + h, 512) * 4 + p % 4
        t[:, IDX_AB + 2 * hh + 1] = _grow(p // 32, ((p % 32) // 4) * 3136 + 3080 + h, 512) * 4 + p % 4
    for kc in range(8):
        for tb in range(8):
            t[:, IDX_G2 + kc * 8 + tb] = _grow(kc // 2, ((kc % 2) * 4 + r) * 128 + p, 64) * 8 + tb
            t[:, IDX_G4 + kc * 8 + tb] = _grow(kc // 2, r * 256 + (kc % 2) * 128 + p, 64) * 8 + tb
    for m in range(2):
        for j in range(32):
            t[:, IDX_QK + m * 32 + j] = _grow(j // 8, (j % 8) * 1024 + r * 256 + m * 128 + p, 1024)
    for kb in range(128):
        t[:, IDX_V + kb] = _grow(kb // 32, (((kb % 32) // 4) * 4 + r) * 512 + ((kb % 32) % 4) * 128 + p, 2048)
    return t.astype(np.uint32)


def kernel(x, c, mod_w, mod_b, norm_mix_g, norm_mlp_g, a_w_in, a_conv_w, a_log, a_dt_bias, a_out_norm_g, a_w_out,
           kv_mod_w, kv_mod_b, kv_norm_g, kv_w, b_w_q, b_lambda, b_subln_g, b_w_out, mlp_w1, mlp_w2, final_g):
    f32 = np.float32
    x = np.asarray(x, f32)
    B, S, D = x.shape
    A = lambda v: np.asarray(v, f32)
    jj, ii = np.meshgrid(np.arange(128), np.arange(128), indexing='ij')
    shared = {
        "w0": A(mod_w[0]), "w1": A(mod_w[1]), "w2": A(kv_mod_w),
        "mb": np.ascontiguousarray(np.concatenate([A(mod_b[0]), A(mod_b[1]), A(kv_mod_b)]).reshape(112, 128).T),
        "gT0": _fm8(norm_mix_g[0]), "w_in": A(a_w_in[0]), "ident": np.eye(128, dtype=f32),
        "negTi": np.where(ii >= jj, 0.0, -30000.0).astype(f32), "negTs": np.where(ii > jj, 0.0, -30000.0).astype(f32),
        "tri": (np.arange(128)[:, None] <= np.arange(128)[None, :]).astype(f32),
        "og_a": A(a_out_norm_g[0]).reshape(128, 1), "a_w_out": A(a_w_out[0]), "gm0": _fm8(norm_mlp_g[0]),
        "m1_0": A(mlp_w1[0]), "m2_0": A(mlp_w2[0]), "gkv": _fm8(kv_norm_g), "gq": _fm8(norm_mix_g[1]),
        "kv_w": A(kv_w), "kv_wp": _perm_cols(A(kv_w)[:, :1024]), "wq": A(b_w_q[0]), "wqp": _perm_cols(A(b_w_q[0])),
        "lamT": np.ascontiguousarray(A(b_lambda[0]).T), "og_b": np.ascontiguousarray(A(b_subln_g[0]).reshape(2, 128).T),
        "b_w_out": A(b_w_out[0]), "gm1": _fm8(norm_mlp_g[1]), "m1_1": A(mlp_w1[1]), "m2_1": A(mlp_w2[1]), "fg": _fm8(final_g),
    }
    maps = []
    for cc in range(NCORES):
        b, r = cc // 4, cc % 4
        t0 = r * TPC
        m = dict(shared)
        m["xT"] = np.ascontiguousarray(x[b, t0:t0 + TPC].T)
        m["cT"] = np.ascontiguousarray(A(c)[b].reshape(8, 128).T.reshape(128, 8, 1))
        cw = np.empty((128, 24), f32)
        hp = np.empty((128, 4), f32)
        for hh in range(2):
            h = 2 * r + hh
            for X in range(3):
                r0 = X * 1024 + h * 128
                for j in range(4):
                    cw[:, (hh * 3 + X) * 4 + j] = A(a_conv_w[0])[j, r0:r0 + 128]
            hp[:, 2 * hh] = A(a_log[0])[h]
            hp[:, 2 * hh + 1] = A(a_dt_bias[0])[h]
        m["cw"], m["hp"] = cw, hp
        m["ropeC"], m["ropeS"] = _rope_tables(t0, TPC)
        m["idx"] = _idx_table(r)
        maps.append({k: np.ascontiguousarray(m[k]) for k in IN_SPECS})
    nc = build_program()
    res = run_bass_kernel_spmd(nc, maps, core_ids=list(range(NCORES)))
    out = np.empty((B, S, D), f32)
    for cc in range(NCORES):
        b, r = cc // 4, cc % 4
        out[b, r * TPC:(r + 1) * TPC] = np.asarray(res.results[cc]["outT"]).T
    return out
```

```python
import contextlib
from collections import defaultdict
import numpy as np
import concourse.bass as bass
import concourse.mybir as mybir

F32 = mybir.dt.float32
BF16 = mybir.dt.bfloat16
U32 = mybir.dt.uint32
AF = mybir.ActivationFunctionType
ALU = mybir.AluOpType
AX = mybir.AxisListType

ENGS = ['sp', 'act', 'dve', 'pool', 'pe']


class Buf:
    def __init__(self, h, name, tr=None, multi=False):
        self.h = h
        self.name = name
        self.multi = multi
        self.tr = tr if tr is not None else {'lw': None, 'rd': {}, 'ws': {}}

    def __getitem__(self, k):
        return self.h[k]

    def view(self, pattern, **kw):
        return Buf(self.h[:].rearrange(pattern, **kw), self.name, self.tr, self.multi)


class Sched:
    def __init__(self, nc):
        self.nc = nc
        self.es = contextlib.ExitStack()
        self.ops = {e: [] for e in ENGS}
        self.val = defaultdict(int)
        self.known = {e: defaultdict(int) for e in ENGS}
        self.sem = {}
        for e in ENGS:
            self.sem['c_' + e] = self.es.enter_context(nc.semaphore('c_' + e))
        self.NS = 8
        self.ndma = defaultdict(int)
        for e in ['sp', 'act', 'pool']:
            for k in range(self.NS):
                nm = f'd_{e}_{k}'
                self.sem[nm] = self.es.enter_context(nc.semaphore(nm))
        self.nbuf = 0
        self.pes = None
        self.sem['cc'] = self.es.enter_context(nc.semaphore('cc'))

    def begin_phase(self):
        self.pes = contextlib.ExitStack()
        if hasattr(self, '_stage'):
            del self._stage

    def end_phase(self, final=False):
        allc = [(c, v) for c, v in self.val.items() if v > 0]
        for eng in ENGS:
            waits = []
            for c, v in allc:
                if self.known[eng][c] < v:
                    waits.append((c, v))
                    self.known[eng][c] = v
            self.ops[eng].append((waits, None, None, 0))
        self._emit_block()
        if self.pes is not None:
            self.pes.close()
            self.pes = None
        if final:
            self.es.close()

    def _stack(self):
        return self.pes if self.pes is not None else self.es

    def sb(self, name, shape, dt):
        self.nbuf += 1
        return Buf(self._stack().enter_context(self.nc.sbuf_tensor(f"{name}_{self.nbuf}", list(shape), dt)), name)

    def ps(self, name, shape, dt=F32):
        self.nbuf += 1
        return Buf(self._stack().enter_context(self.nc.psum_tensor(f"{name}_{self.nbuf}", list(shape), dt)), name)

    def dram(self, h, name):
        return Buf(h, name, multi=True)

    def op(self, eng, fn, reads=(), writes=(), dma=False, cc=False):
        deps = {}
        if cc:
            ctr = 'cc'
        elif dma:
            k = self.ndma[eng] % self.NS
            self.ndma[eng] += 1
            ctr = f'd_{eng}_{k}'
            if self.val[ctr] > 0:
                deps[ctr] = self.val[ctr]
        else:
            ctr = 'c_' + eng
        for b in reads:
            lw = b.tr['lw']
            if lw is not None:
                deps[lw[0]] = max(deps.get(lw[0], 0), lw[1])
            if b.multi:
                for c, v in b.tr['ws'].items():
                    deps[c] = max(deps.get(c, 0), v)
        for b in writes:
            if b.multi:
                continue
            lw = b.tr['lw']
            if lw is not None:
                deps[lw[0]] = max(deps.get(lw[0], 0), lw[1])
            for c, v in b.tr['rd'].items():
                deps[c] = max(deps.get(c, 0), v)
        waits = []
        for c, v in deps.items():
            if eng == 'pe' and c == 'c_pe':
                continue
            if self.known[eng][c] < v:
                waits.append((c, v))
                self.known[eng][c] = v
        inc = 1 if cc else (16 if dma else 1)
        self.val[ctr] += inc
        v = self.val[ctr]
        self.ops[eng].append((waits, fn, ctr, inc))
        for b in reads:
            b.tr['rd'][ctr] = max(b.tr['rd'].get(ctr, 0), v)
        for b in writes:
            if b.multi:
                b.tr['ws'][ctr] = max(b.tr['ws'].get(ctr, 0), v)
                continue
            b.tr['lw'] = (ctr, v)
            b.tr['rd'] = {}

    def I(self, eng, method, reads, writes, *a, **kw):
        self.op(eng, lambda e: getattr(e, method)(*a, **kw), reads, writes)

    def dma(self, eng, out_ap, in_ap, reads=(), writes=(), **kw):
        self.op(eng, lambda e: e.dma_start(out=out_ap, in_=in_ap, **kw), reads, writes, dma=True)

    def allgather(self, in_ap, out_ap, in_buf, out_buf, groups):
        self.op('pool', lambda e: e.collective_compute("AllGather", ALU.bypass, replica_groups=groups,
                                                       ins=[in_ap.opt()], outs=[out_ap.opt()]),
                reads=[in_buf], writes=[out_buf], cc=True)

    def gather(self, out_ap, in_view, idx_ap, reads, writes):
        self.op('pool', lambda e: e.indirect_dma_start(out=out_ap, out_offset=None, in_=in_view,
                                                       in_offset=bass.IndirectOffsetOnAxis(ap=idx_ap, axis=0)),
                reads, writes, dma=True)

    def _emit_block(self):
        nc = self.nc
        with nc.Block() as block:
            decos = {'sp': block.sync, 'act': block.scalar, 'dve': block.vector,
                     'pool': block.gpsimd, 'pe': block.tensor}
            for eng in ENGS:
                ops = self.ops[eng]

                def body(e, ops=ops):
                    for waits, fn, ctr, inc in ops:
                        for c, v in waits:
                            e.wait_ge(self.sem[c], v)
                        if fn is not None:
                            ins = fn(e)
                            if inc == 1 and ctr == 'cc':
                                ins.then_inc(self.sem[ctr])
                            else:
                                ins.then_inc(self.sem[ctr], inc)
                decos[eng](body)
        self.nops_total = getattr(self, 'nops_total', 0) + sum(len(v) for v in self.ops.values())
        self.ops = {e: [] for e in ENGS}

    def emit(self):
        self.end_phase(final=True)

    def n_ops(self):
        return {e: len(self.ops[e]) for e in ENGS}


EPS = 1e-6


def consts(s):
    c = {}
    c['ones_bf'] = s.sb('ones_bf', [128, 128], BF16)
    s.op('pool', lambda e: e.memset(c['ones_bf'][:], 1.0), writes=[c['ones_bf']])
    return c


def phase_mod(s, C, cT_d, ws, modbT_d):
    NJ = sum(w.shape[1] for w in ws) // 128
    cs = s.sb('cs', [128, 8, 1], F32)
    s.dma('sp', cs[:], cT_d, writes=[cs])
    s.op('act', lambda e: e.activation(out=cs[:], in_=cs[:], func=AF.Silu), reads=[cs], writes=[cs])
    mb = s.sb('mb', [128, NJ], F32)
    s.dma('sp', mb[:], modbT_d, writes=[mb])
    ps = s.ps('modps', [128, 512])
    st = [s.sb('mst', [128, 8, 512], F32) for _ in range(2)]
    j = 0
    k = 0
    for w in ws:
        wv = w.rearrange("(kc p) n -> p kc n", p=128)
        for n0 in range(0, w.shape[1], 512):
            b = st[k % 2]
            s.dma('sp' if k % 2 == 0 else 'pool', b[:], wv[:, :, n0:n0 + 512], writes=[b])
            k += 1
            for jj in range(4):
                for kc in range(8):
                    s.I('pe', 'matmul', [b, cs], [ps], ps[:, j:j + 1], lhsT=b[:, kc, jj * 128:(jj + 1) * 128], rhs=cs[:, kc, :],
                        start=(kc == 0), stop=(kc == 7))
                j += 1
    modv = C['modv']
    s.I('dve', 'tensor_tensor', [mb], [ps, modv], out=modv[:], in0=ps[:, 0:NJ], in1=mb[:], op=ALU.add)


def get_stage(s):
    if not hasattr(s, '_stage'):
        s._stage = [s.sb('wst', [128, 2048], F32) for _ in range(2)]
        s._stage_i = 0
    return s._stage


def load_cast_w(s, Wb, w_d, piece=512, engs=('dve', 'pool')):
    K, N = w_d.shape
    KC = K // 128
    nk = 2048 // piece
    wv = w_d.rearrange("(kc p) n -> p kc n", p=128)
    st = get_stage(s)
    for k0 in range(0, KC, nk):
        kw = min(nk, KC - k0)
        for n0 in range(0, N, piece):
            nw = min(piece, N - n0)
            i = s._stage_i
            s._stage_i += 1
            b = st[i % 2]
            bv = b[:].rearrange("p (k n) -> p k n", n=piece)
            s.dma('sp' if i % 2 == 0 else 'pool', bv[:, 0:kw, 0:nw], wv[:, k0:k0 + kw, n0:n0 + nw], writes=[b])
            eng = engs[i % len(engs)]
            s.I(eng, 'tensor_copy', [b], [Wb], out=Wb[:, k0:k0 + kw, n0:n0 + nw], in_=bv[:, 0:kw, 0:nw])


def norm_mod(s, C, xs, G, SH, h, tmp, sq, ss_ps, rstd, D=1024, TB=512):
    KC = D // 128
    s.op('act', lambda e: e.activation(out=sq[:], in_=xs[:], func=AF.Square), reads=[xs], writes=[sq])
    for kc in range(KC):
        s.op('pe', lambda e, kc=kc: e.matmul(ss_ps[:, 0:TB], lhsT=C['ones_bf'][:], rhs=sq[:, kc, :],
                                              start=(kc == 0), stop=(kc == KC - 1)),
             reads=[sq, C['ones_bf']], writes=[ss_ps])
    s.op('dve', lambda e: e.tensor_scalar(out=rstd[:], in0=ss_ps[:, 0:TB], scalar1=1.0 / D, scalar2=EPS,
                                          op0=ALU.mult, op1=ALU.add), reads=[ss_ps], writes=[rstd])
    s.op('dve', lambda e: e.reciprocal(out=rstd[:], in_=rstd[:]), reads=[rstd], writes=[rstd])
    s.op('act', lambda e: e.activation(out=rstd[:], in_=rstd[:], func=AF.Sqrt), reads=[rstd], writes=[rstd])
    for kc in range(KC):
        t = tmp[kc % len(tmp)]
        s.op('dve', lambda e, kc=kc, t=t: e.scalar_tensor_tensor(
            out=t[:], in0=xs[:, kc, :], scalar=G[0][:, G[1] + kc:G[1] + kc + 1], in1=rstd[:],
            op0=ALU.mult, op1=ALU.mult), reads=[xs, G[0], rstd], writes=[t])
        s.op('act', lambda e, kc=kc, t=t: e.activation(
            out=h[:, kc, :], in_=t[:], func=AF.Identity, bias=SH[0][:, SH[1] + kc:SH[1] + kc + 1], scale=1.0),
             reads=[t, SH[0]], writes=[h])


def phase_l1(s, C, xT_d, gT_d, w_in_d, P1, Z1, after_tb=None, NT=4096, TB=512):
    NOUT = w_in_d.shape[1]
    modv = C['modv']
    gT = s.sb('gT', [128, 8], F32)
    s.dma('sp', gT[:], gT_d, writes=[gT])
    G = s.sb('G', [128, 8], F32)
    s.op('dve', lambda e: e.scalar_tensor_tensor(out=G[:], in0=modv[:, 8:16], scalar=1.0, in1=gT[:],
                                                 op0=ALU.add, op1=ALU.mult), reads=[modv, gT], writes=[G])
    Wb = s.sb('Wb', [128, 8, NOUT], BF16)
    load_cast_w(s, Wb, w_in_d)
    xs = [s.sb('xs', [128, 8, TB], F32) for _ in range(2)]
    sq = s.sb('sq', [128, 8, TB], BF16)
    hs = [s.sb('h', [128, 8, TB], BF16) for _ in range(2)]
    tmp = [s.sb('tmp', [128, TB], F32) for _ in range(2)]
    rstd = s.sb('rstd', [128, TB], F32)
    ss_ps = s.ps('ss', [128, 512])
    pps = [s.ps('pp', [128, 512]) for _ in range(4)]
    outs = [s.sb('ost', [128, TB], F32) for _ in range(4)]
    xv = xT_d.rearrange("(kc p) t -> p kc t", p=128)
    nchunks = [(n0, min(128, NOUT - n0)) for n0 in range(0, NOUT, 128)]
    it = 0
    for tb in range(NT // TB):
        x = xs[tb % 2]
        h = hs[tb % 2]
        s.dma('sp', x[:], xv[:, :, tb * TB:(tb + 1) * TB], writes=[x])
        norm_mod(s, C, x, (G, 0), (modv, 0), h, tmp, sq, ss_ps, rstd)
        for (n0, nw) in nchunks:
            pp = pps[it % 4]
            o = outs[it % 4]
            for kc in range(8):
                s.op('pe', lambda e, pp=pp, kc=kc, n0=n0, nw=nw, h=h: e.matmul(
                    pp[0:nw, 0:TB], lhsT=Wb[:, kc, n0:n0 + nw], rhs=h[:, kc, :], start=(kc == 0), stop=(kc == 7)),
                     reads=[Wb, h], writes=[pp])
            if it % 2 == 0:
                s.op('act', lambda e, pp=pp, o=o, nw=nw: e.copy(out=o[0:nw, :], in_=pp[0:nw, 0:TB]),
                     reads=[pp], writes=[o])
            else:
                s.op('dve', lambda e, pp=pp, o=o, nw=nw: e.tensor_copy(out=o[0:nw, :], in_=pp[0:nw, 0:TB]),
                     reads=[pp], writes=[o])
            if n0 < 3072:
                dst, db = P1.h[tb * 3136 + n0:tb * 3136 + n0 + nw, :], P1
            elif n0 < 4096:
                dst, db = Z1.h[n0 - 3072:n0 - 3072 + nw, tb * TB:(tb + 1) * TB], Z1
            else:
                dst, db = P1.h[tb * 3136 + 3072:tb * 3136 + 3072 + nw, :], P1
            s.dma('sp', dst, o[0:nw, :], reads=[o], writes=[db])
            it += 1
        if after_tb is not None:
            after_tb(tb)


L2EPS = 1e-6


def gdn_consts(s, C, ident_d, negTi_d, negTs_d):
    for nm, d in (('ident', ident_d), ('negTi', negTi_d), ('negTs', negTs_d)):
        C[nm] = s.sb(nm, [128, 128], F32)
        s.dma('sp', C[nm][:], d, writes=[C[nm]])
        C[nm + '_bf'] = s.sb(nm + '_bf', [128, 128], BF16)
        s.I('dve', 'tensor_copy', [C[nm]], [C[nm + '_bf']], out=C[nm + '_bf'][:], in_=C[nm][:])


class PsumPool:
    def __init__(self, s, nbanks, name):
        self.banks = [s.ps(f'{name}{i}', [128, 512]) for i in range(nbanks)]
        self.i = 0

    def get(self):
        b = self.banks[(self.i // 4) % len(self.banks)]
        sl = self.i % 4
        self.i += 1
        return b, b[:, sl * 128:(sl + 1) * 128]

    def get2(self):
        if self.i % 2:
            self.i += 1
        b = self.banks[(self.i // 4) % len(self.banks)]
        sl = self.i % 4
        self.i += 2
        return b, b[:, sl * 128:(sl + 2) * 128]

    def get_bank(self):
        self.i = ((self.i + 3) // 4) * 4
        b = self.banks[(self.i // 4) % len(self.banks)]
        self.i += 4
        return b


def phase_gdn(s, C, G1, cw_d, hp_d, O2, S=16384, NH=2, IDX_G1=0, IDX_AB=192, after_block=None):
    SB = 512
    NSB = S // SB
    scale = 128 ** -0.5
    ident = C['ident']
    idx = C['idx']
    g1v512 = G1.h[:, :]
    I = s.I
    cw = s.sb('cw', [128, NH * 3 * 4], F32)
    s.dma('sp', cw[:], cw_d, writes=[cw])
    hp = s.sb('hp', [128, NH * 2], F32)
    s.dma('sp', hp[:], hp_d, writes=[hp])
    pp = PsumPool(s, 6, 'lv')
    seqbank = [s.ps(f'seq{h}', [128, 512]) for h in range(NH)]
    bc127 = ident[:, 127:128].to_broadcast([128, 128])
    import math
    lnsc = s.sb('lnsc', [128, 1], F32)
    I('pool', 'memset', [], [lnsc], lnsc[:], math.log(scale))
    lnscale_c = lnsc[:, 0:1]

    tabs = []
    for h in range(NH):
        T = {}
        for nm in ('a', 'b', 'x', 't', 'gcum', 'rtab', 'gcumT', 'betaT', 'negg', 'be', 'kdf', 'glc', 'ones'):
            T[nm] = s.sb(f'{nm}{h}', [128, 128], F32)
        T['nA'] = s.sb(f'nA{h}', [128, 1], F32)
        a, b, x, t, nA = T['a'], T['b'], T['x'], T['t'], T['nA']
        g1v128 = G1.h[:, :].rearrange("r (b w) -> (r b) w", w=128)
        s.gather(a[:], g1v128, idx[:, IDX_AB + 2 * h:IDX_AB + 2 * h + 1], [G1, idx], [a])
        s.gather(b[:], g1v128, idx[:, IDX_AB + 2 * h + 1:IDX_AB + 2 * h + 2], [G1, idx], [b])
        I('pool', 'memset', [], [T['ones']], T['ones'][:], 1.0)
        I('dve', 'tensor_scalar', [a, hp], [x], out=x[:], in0=a[:], scalar1=hp[:, 2 * h + 1:2 * h + 2], scalar2=None, op0=ALU.add)
        I('act', 'activation', [x], [t], out=t[:], in_=x[:], func=AF.Abs)
        I('act', 'activation', [t], [t], out=t[:], in_=t[:], func=AF.Exp, scale=-1.0)
        I('act', 'activation', [t], [t], out=t[:], in_=t[:], func=AF.Ln, bias=1.0)
        I('dve', 'tensor_scalar_max', [x], [x], out=x[:], in0=x[:], scalar1=0.0)
        I('dve', 'tensor_tensor', [x, t], [x], out=x[:], in0=x[:], in1=t[:], op=ALU.add)
        I('act', 'activation', [hp], [nA], out=nA[:], in_=hp[:, 2 * h:2 * h + 1], func=AF.Exp)
        I('dve', 'tensor_scalar', [nA], [nA], out=nA[:], in0=nA[:], scalar1=-1.0, scalar2=None, op0=ALU.mult)
        I('dve', 'tensor_scalar', [x, nA], [x], out=x[:], in0=x[:], scalar1=nA[:, 0:1], scalar2=None, op0=ALU.mult)
        I('dve', 'tensor_tensor_scan', [T['ones'], x], [T['gcum']], out=T['gcum'][:], data0=T['ones'][:], data1=x[:],
          initial=0.0, op0=ALU.mult, op1=ALU.add)
        I('act', 'activation', [b], [b], out=b[:], in_=b[:], func=AF.Sigmoid)
        I('act', 'activation', [b], [t], out=t[:], in_=b[:], func=AF.Ln)
        I('dve', 'tensor_tensor', [T['gcum'], t], [T['rtab']], out=T['rtab'][:], in0=T['gcum'][:], in1=t[:], op=ALU.add)
        bk, p1 = pp.get()
        I('pe', 'transpose', [T['gcum'], ident], [bk], p1, T['gcum'][:], ident[:])
        I('act', 'copy', [], [bk, T['gcumT']], out=T['gcumT'][:], in_=p1)
        bk2, p2 = pp.get()
        I('pe', 'transpose', [b, ident], [bk2], p2, b[:], ident[:])
        I('act', 'copy', [], [bk2, T['betaT']], out=T['betaT'][:], in_=p2)
        bk3, p3 = pp.get()
        I('pe', 'matmul', [ident, T['gcumT']], [bk3], p3, lhsT=bc127, rhs=T['gcumT'][:], start=True, stop=True)
        I('act', 'activation', [], [bk3, T['glc']], out=T['glc'][:], in_=p3, func=AF.Exp)
        I('dve', 'tensor_tensor', [T['gcumT']], [bk3, T['kdf']], out=T['kdf'][:], in0=p3, in1=T['gcumT'][:], op=ALU.subtract)
        I('act', 'activation', [T['kdf']], [T['kdf']], out=T['kdf'][:], in_=T['kdf'][:], func=AF.Exp)
        I('dve', 'tensor_scalar', [T['gcumT']], [T['negg']], out=T['negg'][:], in0=T['gcumT'][:], scalar1=-1.0, scalar2=None, op0=ALU.mult)
        I('act', 'activation', [T['gcumT']], [T['be']], out=T['be'][:], in_=T['gcumT'][:], func=AF.Exp)
        I('dve', 'tensor_tensor', [T['be'], T['betaT']], [T['be']], out=T['be'][:], in0=T['be'][:], in1=T['betaT'][:], op=ALU.mult)
        T['cat'] = s.sb(f'cat{h}', [128, 384], F32)
        I('pool', 'tensor_copy', [T['gcum']], [T['cat']], out=T['cat'][:, 0:128], in_=T['gcum'][:])
        I('pool', 'tensor_copy', [T['rtab']], [T['cat']], out=T['cat'][:, 128:256], in_=T['rtab'][:])
        I('pool', 'tensor_copy', [T['gcum']], [T['cat']], out=T['cat'][:, 256:384], in_=T['gcum'][:])
        tabs.append(T)

    S32 = [s.sb(f'S32_{h}', [128, 128], F32) for h in range(NH)]
    Sbf = [s.sb(f'Sbf_{h}', [128, 128], BF16) for h in range(NH)]
    for h in range(NH):
        I('pool', 'memset', [], [S32[h]], S32[h][:], 0.0)
        I('pool', 'memset', [], [Sbf[h]], Sbf[h][:], 0.0)

    NSET = 3
    sets = []
    for u in range(NSET):
        U = {}
        U['xin'] = [s.sb(f'xin{u}{X}', [128, 3 + SB], F32) for X in range(3)]
        U['acc'] = [s.sb(f'acc{u}{X}', [128, SB], F32) for X in range(3)]
        U['cv'] = [s.sb(f'cv{u}{X}', [128, SB], F32) for X in range(3)]
        U['sq'] = s.sb(f'sq{u}', [128, SB], BF16)
        U['rn'] = s.sb(f'rn{u}', [128, SB], F32)
        U['xn'] = [s.sb(f'xn{u}{X}', [128, SB], F32) for X in range(2)]
        U['ost'] = s.sb(f'ost{u}', [128, 4, 128], F32)
        U['ch'] = []
        for c in range(4):
            D = {}
            for nm in ('kbg', 'vb', 'CT', 'DT', 'EGB', 'Q0', 'Q1', 'u'):
                D[nm] = s.sb(f'{nm}{u}{c}', [128, 128], F32)
            for nm in ('YP0', 'YP1'):
                D[nm] = s.sb(f'{nm}{u}{c}', [128, 256], F32)
            for nm in ('kdec', 'qkT', 'qg', 'wT', 'vnew'):
                D[nm] = s.sb(f'{nm}{u}{c}', [128, 128], BF16)
            U['ch'].append(D)
        sets.append(U)

    def stageA_gen(h, sb, U, Uprev):
        T = tabs[h]
        t0 = sb * SB
        for X in range(3):
            xin = U['xin'][X]
            if sb == 0:
                I('pool', 'memset', [], [xin], xin[:, 0:3], 0.0)
            else:
                pxin = Uprev['xin'][X]
                I('pool', 'tensor_copy', [pxin], [xin], out=xin[:, 0:3], in_=pxin[:, SB:SB + 3])
            col = IDX_G1 + (h * 3 + X) * 32 + sb
            s.gather(xin[:, 3:3 + SB], g1v512, idx[:, col:col + 1], [G1, idx], [xin])
        yield
        for X in range(3):
            xin = U['xin'][X]
            eng = 'dve'
            acc = U['acc'][X]
            wc = (h * 3 + X) * 4
            I(eng, 'tensor_scalar', [xin, cw], [acc], out=acc[:], in0=xin[:, 0:SB], scalar1=cw[:, wc:wc + 1], scalar2=None, op0=ALU.mult)
            for j in range(1, 4):
                if eng == 'dve':
                    I(eng, 'scalar_tensor_tensor', [xin, cw, acc], [acc], out=acc[:], in0=xin[:, j:j + SB],
                      scalar=cw[:, wc + j:wc + j + 1], in1=acc[:], op0=ALU.mult, op1=ALU.add)
                else:
                    tq = U['rn']
                    I(eng, 'tensor_scalar', [xin, cw], [tq], out=tq[:], in0=xin[:, j:j + SB], scalar1=cw[:, wc + j:wc + j + 1],
                      scalar2=None, op0=ALU.mult)
                    I(eng, 'tensor_tensor', [tq, acc], [acc], out=acc[:], in0=acc[:], in1=tq[:], op=ALU.add)
            cv = U['cv'][X]
            I('act', 'activation', [acc], [cv], out=cv[:], in_=acc[:], func=AF.Silu)
            yield
        for X in range(2):
            cv = U['cv'][X]
            sq, rn, xn = U['sq'], U['rn'], U['xn'][X]
            I('act', 'activation', [cv], [sq], out=sq[:], in_=cv[:], func=AF.Square)
            bk = pp.get_bank()
            I('pe', 'matmul', [C['ones_bf'], sq], [bk], bk[:, 0:SB], lhsT=C['ones_bf'][:], rhs=sq[:], start=True, stop=True)
            I('dve', 'tensor_scalar', [], [bk, rn], out=rn[:], in0=bk[:, 0:SB], scalar1=L2EPS, scalar2=None, op0=ALU.add)
            I('dve', 'reciprocal', [rn], [rn], out=rn[:], in_=rn[:])
            I('act', 'activation', [rn], [rn], out=rn[:], in_=rn[:], func=AF.Sqrt)
            I('pool', 'tensor_tensor', [cv, rn], [xn], out=xn[:], in0=cv[:], in1=rn[:], op=ALU.mult)
            yield

    def prep_gen(h, sb, U):
        T = tabs[h]
        qn, kn, vv = U['xn'][0], U['xn'][1], U['cv'][2]
        for c in range(4):
            D = U['ch'][c]
            n = sb * 4 + c
            cs = slice(c * 128, (c + 1) * 128)
            icol = ident[:, n:n + 1].to_broadcast([128, 128])
            be_c, kdf_c, bet_c, ng_c = (T[k][:, n:n + 1] for k in ('be', 'kdf', 'betaT', 'negg'))
            b_trk, p_trk = pp.get()
            I('pe', 'transpose', [kn, ident], [b_trk], p_trk, kn[:, cs], ident[:])
            b_trv, p_trv = pp.get()
            I('pe', 'transpose', [vv, ident], [b_trv], p_trv, vv[:, cs], ident[:])
            b_kk, p_kk = pp.get()
            I('pe', 'matmul', [kn], [b_kk], p_kk, lhsT=kn[:, cs], rhs=kn[:, cs], start=True, stop=True)
            b_bc = pp.get_bank()
            p_dt, p_ct, p_gb = b_bc[:, 0:128], b_bc[:, 128:256], b_bc[:, 256:384]
            I('pe', 'matmul', [ident, T['cat']], [b_bc], b_bc[:, 0:384], lhsT=icol, rhs=T['cat'][:], start=True, stop=False, skip_group_check=True)
            I('pe', 'matmul', [C['ident_bf'], C['negTi_bf']], [b_bc], p_dt, lhsT=C['ident_bf'][:], rhs=C['negTi_bf'][:],
              start=False, stop=False, skip_group_check=True)
            I('pe', 'matmul', [C['ident_bf'], C['negTs_bf']], [b_bc], p_ct, lhsT=C['ident_bf'][:], rhs=C['negTs_bf'][:],
              start=False, stop=True, skip_group_check=True)
            I('dve', 'tensor_scalar', [T['be']], [b_trk, D['kbg']], out=D['kbg'][:], in0=p_trk, scalar1=be_c, scalar2=None, op0=ALU.mult)
            I('dve', 'tensor_scalar', [T['kdf']], [b_trk, D['kdec']], out=D['kdec'][:], in0=p_trk, scalar1=kdf_c, scalar2=None, op0=ALU.mult)
            I('dve', 'tensor_scalar', [T['betaT']], [b_trv, D['vb']], out=D['vb'][:], in0=p_trv, scalar1=bet_c, scalar2=None, op0=ALU.mult)
            I('act', 'activation', [T['negg']], [b_bc, D['CT']], out=D['CT'][:], in_=p_ct, func=AF.Exp, bias=ng_c, scale=1.0)
            YP0 = D['YP0']
            I('dve', 'scalar_tensor_tensor', [D['CT']], [b_kk, YP0], out=YP0[:, 128:256], in0=p_kk, scalar=-1.0, in1=D['CT'][:],
              op0=ALU.mult, op1=ALU.mult)
            I('pool', 'tensor_copy', [ident], [YP0], out=YP0[:, 0:128], in_=ident[:])
            b_qk, p_qk = pp.get()
            I('pe', 'matmul', [kn, qn], [b_qk], p_qk, lhsT=kn[:, cs], rhs=qn[:, cs], start=True, stop=True)
            I('act', 'activation', [T['negg']], [b_bc, D['DT']], out=D['DT'][:], in_=p_dt, func=AF.Exp, bias=ng_c, scale=1.0)
            I('dve', 'scalar_tensor_tensor', [D['DT']], [b_qk, D['qkT']], out=D['qkT'][:], in0=p_qk, scalar=scale, in1=D['DT'][:],
              op0=ALU.mult, op1=ALU.mult)
            I('act', 'activation', [lnsc], [b_bc, D['EGB']], out=D['EGB'][:], in_=p_gb, func=AF.Exp, bias=lnscale_c, scale=1.0)
            I('pool', 'tensor_tensor', [qn, D['EGB']], [D['qg']], out=D['qg'][:], in0=qn[:, cs], in1=D['EGB'][:], op=ALU.mult)
            yield
        for c in range(4):
            D = U['ch'][c]
            b_q, p_q = pp.get()
            I('pe', 'transpose', [D['YP0'], ident], [b_q], p_q, D['YP0'][:, 128:256], ident[:])
            I('dve', 'tensor_copy', [], [b_q, D['Q0']], out=D['Q0'][:], in_=p_q)
        yield
        m = 1
        cur = 0
        while m <= 64:
            nxt = 1 - cur
            YPc, YPn, Qc, Qn_ = f'YP{cur}', f'YP{nxt}', f'Q{cur}', f'Q{nxt}'
            for c in range(4):
                D = U['ch'][c]
                wa = 256 if m < 32 else 128
                if wa == 256:
                    ba, pa_ = pp.get2()
                else:
                    ba, pa_ = pp.get()
                I('pe', 'matmul', [D[Qc], D[YPc]], [ba], pa_, lhsT=D[Qc][:], rhs=D[YPc][:, 0:wa], start=True, stop=True)
                if m < 64:
                    b2, p2 = pp.get()
                    I('pe', 'matmul', [D[Qc], D[YPc]], [b2], p2, lhsT=D[YPc][:, 128:256], rhs=D[Qc][:], start=True, stop=True)
                I('dve', 'tensor_tensor', [D[YPc]], [ba, D[YPn]], out=D[YPn][:, 0:128], in0=pa_[:, 0:128], in1=D[YPc][:, 0:128], op=ALU.add)
                if wa == 256:
                    I('dve', 'tensor_copy', [], [ba, D[YPn]], out=D[YPn][:, 128:256], in_=pa_[:, 128:256])
                if m < 64:
                    I('act', 'copy', [], [b2, D[Qn_]], out=D[Qn_][:], in_=p2)
            cur = nxt
            m *= 2
            yield
        Yf = f'YP{cur}'
        for c in range(4):
            D = U['ch'][c]
            bu, pu = pp.get()
            I('pe', 'matmul', [D[Yf], D['vb']], [bu], pu, lhsT=D[Yf][:, 0:128], rhs=D['vb'][:], start=True, stop=True)
            bw, pw = pp.get()
            I('pe', 'matmul', [D[Yf], D['kbg']], [bw], pw, lhsT=D['kbg'][:], rhs=D[Yf][:, 0:128], start=True, stop=True)
            I('dve', 'tensor_copy', [], [bu, D['u']], out=D['u'][:], in_=pu)
            I('dve', 'tensor_copy', [], [bw, D['wT']], out=D['wT'][:], in_=pw)
        yield

    def seq_gen(h, sb, U):
        T = tabs[h]
        bk = seqbank[h]
        pvn, po, psu = bk[:, 0:128], bk[:, 128:256], bk[:, 256:384]
        Sb, Sf = Sbf[h], S32[h]
        for c in range(4):
            D = U['ch'][c]
            n = sb * 4 + c
            I('pe', 'matmul', [D['wT'], Sb], [bk], pvn, lhsT=D['wT'][:], rhs=Sb[:], start=True, stop=True)
            I('dve', 'tensor_tensor', [D['u']], [bk, D['vnew']], out=D['vnew'][:], in0=D['u'][:], in1=pvn, op=ALU.subtract)
            I('pe', 'matmul', [D['qg'], Sb], [bk], po, lhsT=Sb[:], rhs=D['qg'][:], start=True, stop=False)
            I('pe', 'matmul', [D['qkT'], D['vnew']], [bk], po, lhsT=D['vnew'][:], rhs=D['qkT'][:], start=False, stop=True)
            I('pe', 'matmul', [D['kdec'], D['vnew']], [bk], psu, lhsT=D['kdec'][:], rhs=D['vnew'][:], start=True, stop=True)
            I('dve', 'scalar_tensor_tensor', [Sf, T['glc']], [bk, Sf], out=Sf[:], in0=Sf[:], scalar=T['glc'][:, n:n + 1], in1=psu,
              op0=ALU.mult, op1=ALU.add)
            I('pool', 'tensor_copy', [Sf], [Sb], out=Sb[:], in_=Sf[:])
            I('dve', 'tensor_copy', [], [bk, U['ost']], out=U['ost'][:, c, :], in_=po)
            yield
        t0 = sb * SB
        r0 = (h * 4 + sb // 8) * 128
        c0 = (sb % 8) * SB
        s.dma('sp', O2.h[r0:r0 + 128, c0:c0 + SB], U['ost'][:].rearrange("p c e -> p (c e)"), reads=[U['ost']], writes=[O2])
        if after_block is not None and sb % 8 == 7:
            after_block(h * 4 + sb // 8)
        yield

    def run_rr(gens):
        gens = list(gens)
        while gens:
            for g in list(gens):
                try:
                    next(g)
                except StopIteration:
                    gens.remove(g)

    units = [(h, sb) for sb in range(NSB) for h in range(NH)]
    NU = len(units)

    def uset(u):
        return sets[u % NSET]

    run_rr([stageA_gen(units[0][0], units[0][1], uset(0), uset(-2))])
    for u in range(NU):
        h, sb = units[u]
        gens = [prep_gen(h, sb, uset(u))]
        if u + 1 < NU:
            h2, sb2 = units[u + 1]
            gens.append(stageA_gen(h2, sb2, uset(u + 1), uset(u - 1)))
        if u >= 1:
            hp_, sbp = units[u - 1]
            gens.append(seq_gen(hp_, sbp, uset(u - 1)))
        run_rr(gens)
    hl, sbl = units[NU - 1]
    run_rr([seq_gen(hl, sbl, uset(NU - 1))])


def rstd_from_ps(s, ss_ps_bank, ss_ap, rstd, D):
    I = s.I
    I('dve', 'tensor_scalar', [], [ss_ps_bank, rstd], out=rstd[:], in0=ss_ap, scalar1=1.0 / D, scalar2=EPS, op0=ALU.mult, op1=ALU.add)
    I('dve', 'reciprocal', [rstd], [rstd], out=rstd[:], in_=rstd[:])
    I('act', 'activation', [rstd], [rstd], out=rstd[:], in_=rstd[:], func=AF.Sqrt)


def phase_post(s, C, mode, Gin, idx_base, Z1, xsrc, xsrc_buf, gt_off, og_d, w_out_d, XO, lam_init=0.0, NT=4096, TB=512):
    I = s.I
    modv = C['modv']
    idx = C['idx']
    gv = Gin.h[:, :].rearrange("r (b w) -> (r b) w", w=TB)
    ncol = 1 if mode == 'gdn' else 2
    og = s.sb('og', [128, ncol], F32)
    s.dma('sp', og[:], og_d, writes=[og])
    if mode == 'attn':
        I('dve', 'tensor_scalar', [og], [og], out=og[:], in0=og[:], scalar1=1.0 - lam_init, scalar2=None, op0=ALU.mult)
    Wb = s.sb('Wb', [128, 8, 1024], BF16)
    load_cast_w(s, Wb, w_out_d)
    os_ = [s.sb('o', [128, 8, TB], F32) for _ in range(2)]
    zs = [s.sb('z', [128, 8, TB], F32) for _ in range(2)] if mode == 'gdn' else None
    xs = [s.sb('x', [128, 8, TB], F32) for _ in range(2)]
    sq = s.sb('sq', [128, 8, TB], BF16)
    hb = [s.sb('hb', [128, 8, TB], BF16) for _ in range(2)]
    rstd = [s.sb('rstd', [128, TB], F32) for _ in range(2)]
    t1 = [s.sb('t1', [128, TB], F32) for _ in range(2)]
    t2 = [s.sb('t2', [128, TB], F32) for _ in range(2)]
    ssb = [s.ps('ss', [128, 512]) for _ in range(2)]
    pps = [s.ps('pp', [128, 512]) for _ in range(4)]
    outs = [s.sb('ost', [128, TB], F32) for _ in range(4)]
    xv = xsrc.rearrange("(kc p) t -> p kc t", p=128)
    zv = Z1.h[:, :].rearrange("(kc p) t -> p kc t", p=128) if mode == 'gdn' else None
    it = 0
    k2 = 0
    for tb in range(NT // TB):
        tsl = slice(tb * TB, (tb + 1) * TB)
        o = os_[tb % 2]
        x = xs[tb % 2]
        h = hb[tb % 2]
        for kc in range(8):
            col = idx_base + kc * 8 + tb
            s.gather(o[:, kc, :], gv, idx[:, col:col + 1], [Gin, idx], [o])
        s.dma('sp', x[:], xv[:, :, tsl], reads=[xsrc_buf] if xsrc_buf is not None else [], writes=[x])
        if mode == 'gdn':
            z = zs[tb % 2]
            s.dma('sp', z[:], zv[:, :, tsl], reads=[Z1], writes=[z])
        I('act', 'activation', [o], [sq], out=sq[:], in_=o[:], func=AF.Square)
        if mode == 'gdn':
            for kc in range(8):
                sb_ = ssb[k2 % 2]
                r = rstd[k2 % 2]
                a1 = t1[k2 % 2]
                a2 = t2[k2 % 2]
                k2 += 1
                I('pe', 'matmul', [C['ones_bf'], sq], [sb_], sb_[:, 0:TB], lhsT=C['ones_bf'][:], rhs=sq[:, kc, :], start=True, stop=True)
                rstd_from_ps(s, sb_, sb_[:, 0:TB], r, 128)
                I('dve', 'tensor_tensor', [o, r], [a1], out=a1[:], in0=o[:, kc, :], in1=r[:], op=ALU.mult)
                I('act', 'activation', [z], [a2], out=a2[:], in_=z[:, kc, :], func=AF.Silu)
                I('pool', 'tensor_tensor', [a1, a2], [a1], out=a1[:], in0=a1[:], in1=a2[:], op=ALU.mult)
                I('act', 'mul', [a1, og], [h], out=h[:, kc, :], in_=a1[:], mul=og[:, 0:1])
        else:
            for hd in range(4):
                sb_ = ssb[k2 % 2]
                r = rstd[k2 % 2]
                k2 += 1
                for j in range(2):
                    I('pe', 'matmul', [C['ones_bf'], sq], [sb_], sb_[:, 0:TB], lhsT=C['ones_bf'][:], rhs=sq[:, 2 * hd + j, :],
                      start=(j == 0), stop=(j == 1))
                rstd_from_ps(s, sb_, sb_[:, 0:TB], r, 256)
                for j in range(2):
                    I('dve', 'scalar_tensor_tensor', [o, og, r], [h], out=h[:, 2 * hd + j, :], in0=o[:, 2 * hd + j, :],
                      scalar=og[:, j:j + 1], in1=r[:], op0=ALU.mult, op1=ALU.mult)
        for n in range(8):
            pp = pps[it % 4]
            ot = outs[it % 4]
            for kc in range(8):
                I('pe', 'matmul', [Wb, h], [pp], pp[:, 0:TB], lhsT=Wb[:, kc, n * 128:(n + 1) * 128], rhs=h[:, kc, :],
                  start=(kc == 0), stop=(kc == 7))
            I('dve', 'scalar_tensor_tensor', [modv, x], [pp, ot], out=ot[:], in0=pp[:, 0:TB],
              scalar=modv[:, gt_off + n:gt_off + n + 1], in1=x[:, n, :], op0=ALU.mult, op1=ALU.add)
            s.dma('sp', XO.h[n * 128:(n + 1) * 128, tsl], ot[:], reads=[ot], writes=[XO])
            it += 1


def phase_mlp(s, C, XI, moff, gT_d, w1_d, w2_d, xoT_d, XO=None, final_g_d=None, NT=4096, TB=256):
    I = s.I
    modv = C['modv']
    gT = s.sb('gT', [128, 8], F32)
    s.dma('sp', gT[:], gT_d, writes=[gT])
    G = s.sb('G', [128, 8], F32)
    I('dve', 'scalar_tensor_tensor', [modv, gT], [G], out=G[:], in0=modv[:, moff + 8:moff + 16], scalar=1.0, in1=gT[:],
      op0=ALU.add, op1=ALU.mult)
    if final_g_d is not None:
        fg = s.sb('fg', [128, 8], F32)
        s.dma('sp', fg[:], final_g_d, writes=[fg])
    W1 = s.sb('W1', [128, 8, 4096], BF16)
    W2 = s.sb('W2', [128, 32, 1024], BF16)
    load_cast_w(s, W1, w1_d)
    load_cast_w(s, W2, w2_d)
    xs = [b.view("p (k t) -> p k t", t=TB) for b in get_stage(s)]
    assert TB == 256
    sq = s.sb('sq', [128, 8, TB], BF16)
    hs = [s.sb('h', [128, 8, TB], BF16) for _ in range(2)]
    tmp = [s.sb('tmp', [128, TB], F32) for _ in range(2)]
    rstd = s.sb('rstd', [128, TB], F32)
    hid = s.sb('hid', [128, 32, TB], BF16)
    rl = [s.sb('rl', [128, TB], F32) for _ in range(2)]
    xo = [s.sb('xo', [128, 8, TB], F32) for _ in range(1)]
    ss_ps = s.ps('ss', [128, 512])
    pps = [s.ps('pp', [128, 512]) for _ in range(6)]
    xv = XI.h[:, :].rearrange("(kc p) t -> p kc t", p=128)
    ov = xoT_d.rearrange("(kc p) t -> p kc t", p=128)
    owr = [XO] if XO is not None else []
    it = 0
    for tb in range(NT // TB):
        tsl = slice(tb * TB, (tb + 1) * TB)
        x = xs[tb % 2]
        h = hs[tb % 2]
        xout = xo[0]
        s.dma('sp', x[:], xv[:, :, tsl], reads=[XI], writes=[x])
        norm_mod(s, C, x, (G, 0), (modv, moff), h, tmp, sq, ss_ps, rstd, TB=TB)
        for f in range(32):
            pp = pps[it % 6]
            r = rl[it % 2]
            it += 1
            for kc in range(8):
                I('pe', 'matmul', [W1, h], [pp], pp[:, 0:TB], lhsT=W1[:, kc, f * 128:(f + 1) * 128], rhs=h[:, kc, :],
                  start=(kc == 0), stop=(kc == 7))
            I('act', 'activation', [], [pp, r], out=r[:], in_=pp[:, 0:TB], func=AF.Relu)
            I('pool', 'tensor_tensor', [r], [hid], out=hid[:, f, :], in0=r[:], in1=r[:], op=ALU.mult)
        for n in range(8):
            pp = pps[it % 6]
            it += 1
            for f in range(32):
                I('pe', 'matmul', [W2, hid], [pp], pp[:, 0:TB], lhsT=W2[:, f, n * 128:(n + 1) * 128], rhs=hid[:, f, :],
                  start=(f == 0), stop=(f == 31))
            I('dve', 'scalar_tensor_tensor', [modv, x], [pp, xout], out=xout[:, n, :], in0=pp[:, 0:TB],
              scalar=modv[:, moff + 16 + n:moff + 17 + n], in1=x[:, n, :], op0=ALU.mult, op1=ALU.add)
        if final_g_d is None:
            s.dma('sp', ov[:, :, tsl], xout[:], reads=[xout], writes=owr)
        else:
            I('act', 'activation', [xout], [sq], out=sq[:], in_=xout[:], func=AF.Square)
            for kc in range(8):
                I('pe', 'matmul', [C['ones_bf'], sq], [ss_ps], ss_ps[:, 0:TB], lhsT=C['ones_bf'][:], rhs=sq[:, kc, :],
                  start=(kc == 0), stop=(kc == 7))
            rstd_from_ps(s, ss_ps, ss_ps[:, 0:TB], rstd, 1024)
            for kc in range(8):
                I('dve', 'scalar_tensor_tensor', [xout, fg, rstd], [xout], out=xout[:, kc, :], in0=xout[:, kc, :],
                  scalar=fg[:, kc:kc + 1], in1=rstd[:], op0=ALU.mult, op1=ALU.mult)
            s.dma('sp', ov[:, :, tsl], xout[:], reads=[xout], writes=owr)


def phase_kvq(s, C, XI, gkvT_d, gqT_d, kv_w_d, kv_wp_d, wq_d, wqp_d, ropeC_d, ropeS_d, Q3, K3, V3, after_tb=None, NT=4096, TB=512):
    I = s.I
    modv = C['modv']
    gkv = s.sb('gkv', [128, 8], F32)
    s.dma('sp', gkv[:], gkvT_d, writes=[gkv])
    gq = s.sb('gq', [128, 8], F32)
    s.dma('sp', gq[:], gqT_d, writes=[gq])
    Gkv = s.sb('Gkv', [128, 8], F32)
    Gq = s.sb('Gq', [128, 8], F32)
    I('dve', 'scalar_tensor_tensor', [modv, gkv], [Gkv], out=Gkv[:], in0=modv[:, 104:112], scalar=1.0, in1=gkv[:], op0=ALU.add, op1=ALU.mult)
    I('dve', 'scalar_tensor_tensor', [modv, gq], [Gq], out=Gq[:], in0=modv[:, 56:64], scalar=1.0, in1=gq[:], op0=ALU.add, op1=ALU.mult)
    Wkv = s.sb('Wkv', [128, 8, 2048], BF16)
    Wkp = s.sb('Wkp', [128, 8, 1024], BF16)
    Wq = s.sb('Wq', [128, 8, 1024], BF16)
    Wqp = s.sb('Wqp', [128, 8, 1024], BF16)
    load_cast_w(s, Wkv, kv_w_d)
    load_cast_w(s, Wkp, kv_wp_d)
    load_cast_w(s, Wq, wq_d)
    load_cast_w(s, Wqp, wqp_d)
    xs = [s.sb('xs', [128, 8, TB], F32) for _ in range(2)]
    sq = s.sb('sq', [128, 8, TB], BF16)
    hkv = s.sb('hkv', [128, 8, TB], BF16)
    hq = s.sb('hq', [128, 8, TB], BF16)
    tmp = [s.sb('tmp', [128, TB], F32) for _ in range(2)]
    rstd = s.sb('rstd', [128, TB], F32)
    rc = [s.sb('rc', [128, TB], F32) for _ in range(2)]
    rs = [s.sb('rs', [128, TB], F32) for _ in range(2)]
    ta = [s.sb('ta', [128, TB], F32) for _ in range(3)]
    tb_ = [s.sb('tb', [128, TB], F32) for _ in range(3)]
    outs = [s.sb('ost', [128, TB], BF16) for _ in range(4)]
    vouts = [s.sb('vost', [128, 256], BF16) for _ in range(4)]
    ss_ps = s.ps('ss', [128, 512])
    pps = [s.ps('pp', [128, 512]) for _ in range(6)]
    xv = XI.h[:, :].rearrange("(kc p) t -> p kc t", p=128)
    it = 0
    io = 0
    for tb in range(NT // TB):
        tsl = slice(tb * TB, (tb + 1) * TB)
        x = xs[tb % 2]
        cc = rc[tb % 2]
        sn = rs[tb % 2]
        s.dma('sp', x[:], xv[:, :, tsl], reads=[XI], writes=[x])
        s.dma('sp', cc[:], ropeC_d[:, tsl], writes=[cc])
        s.dma('sp', sn[:], ropeS_d[:, tsl], writes=[sn])
        norm_mod(s, C, x, (Gkv, 0), (modv, 96), hkv, tmp, sq, ss_ps, rstd, TB=TB)
        norm_mod(s, C, x, (Gq, 0), (modv, 48), hq, tmp, sq, ss_ps, rstd, TB=TB)
        for (W, Wp, hh, dst) in ((Wkv, Wkp, hkv, K3), (Wq, Wqp, hq, Q3)):
            for n in range(8):
                pa = pps[it % 6]
                it += 1
                for kc in range(8):
                    I('pe', 'matmul', [W, hh], [pa], pa[:, 0:TB], lhsT=W[:, kc, n * 128:(n + 1) * 128], rhs=hh[:, kc, :],
                      start=(kc == 0), stop=(kc == 7))
                pb = pps[it % 6]
                it += 1
                for kc in range(8):
                    I('pe', 'matmul', [Wp, hh], [pb], pb[0:32, 0:TB], lhsT=Wp[:, kc, n * 128:n * 128 + 32], rhs=hh[:, kc, :],
                      start=(kc == 0), stop=(kc == 7))
                ot = outs[io % 4]
                a = ta[io % 3]
                b = tb_[io % 3]
                I('dve', 'tensor_tensor', [cc], [pa, a], out=a[:], in0=pa[:, 0:TB], in1=cc[:], op=ALU.mult)
                I('dve', 'tensor_tensor', [sn], [pb, b], out=b[0:32, :], in0=pb[0:32, 0:TB], in1=sn[0:32, :], op=ALU.mult)
                I('pool', 'tensor_tensor', [a, b], [a], out=a[0:32, :], in0=a[0:32, :], in1=b[0:32, :], op=ALU.add)
                I('act', 'copy', [a], [ot], out=ot[:], in_=a[:])
                s.dma('sp', dst.h[tb * 1024 + n * 128:tb * 1024 + (n + 1) * 128, :], ot[:], reads=[ot], writes=[dst])
                io += 1
        for tsb in range(TB // 128):
            for hd in range(4):
                pa = pps[it % 6]
                it += 1
                for kc in range(8):
                    I('pe', 'matmul', [Wkv, hkv], [pa], pa[:, 0:256], lhsT=hkv[:, kc, tsb * 128:(tsb + 1) * 128],
                      rhs=Wkv[:, kc, 1024 + hd * 256:1024 + (hd + 1) * 256], start=(kc == 0), stop=(kc == 7))
                vo = vouts[io % 4]
                io += 1
                I('act', 'copy', [], [pa, vo], out=vo[:], in_=pa[:, 0:256])
                r0 = (tb * 4 + hd) * TB + tsb * 128
                s.dma('sp', V3.h[r0:r0 + 128, :], vo[:], reads=[vo], writes=[V3])
        if after_tb is not None:
            after_tb(tb)


def phase_attn(s, C, G3q, G3k, G3v, lamT_d, O4, lam_init, S=16384, IDX_QK=260, IDX_V=324, after_block=None):
    I = s.I
    scale = 128 ** -0.5
    NKB = S // 128
    NQB = S // 512
    ones_bf = C['ones_bf']
    ones_f = s.sb('ones_f', [128, 128], F32)
    I('pool', 'memset', [], [ones_f], ones_f[:], 1.0)
    tri = C['tri']
    idx = C['idx']
    ident = C['ident']
    gq = G3q.h[:, :]
    gk = G3k.h[:, :]
    misc = s.ps('misc', [128, 512])
    lp = s.sb('lp', [128, 4], F32)
    s.dma('sp', lp[:], lamT_d, writes=[lp])
    pr = s.sb('pr', [128, 2], F32)
    I('dve', 'tensor_tensor', [lp], [pr], out=pr[:, 0:1], in0=lp[:, 0:1], in1=lp[:, 1:2], op=ALU.mult)
    I('dve', 'tensor_tensor', [lp], [pr], out=pr[:, 1:2], in0=lp[:, 2:3], in1=lp[:, 3:4], op=ALU.mult)
    I('pe', 'matmul', [ones_f, pr], [misc], misc[:, 0:2], lhsT=ones_f[:], rhs=pr[:], start=True, stop=True)
    ex = s.sb('ex', [128, 2], F32)
    I('act', 'activation', [], [misc, ex], out=ex[:], in_=misc[:, 0:2], func=AF.Exp)
    nlam = s.sb('nlam', [128, 1], F32)
    I('dve', 'tensor_tensor', [ex], [nlam], out=nlam[:], in0=ex[:, 1:2], in1=ex[:, 0:1], op=ALU.subtract)
    I('dve', 'tensor_scalar', [nlam], [nlam], out=nlam[:], in0=nlam[:], scalar1=-lam_init, scalar2=None, op0=ALU.add)
    KbB = [[s.sb(f'Kb{m}_{j}', [128, 512], BF16) for j in range(NQB)] for m in range(2)]
    VbB = [s.sb(f'Vb{j}', [128, 4, 257], BF16) for j in range(NQB)]
    sqs = [s.sb('sqs', [128, 512], BF16) for _ in range(2)]
    kmx = [s.sb(f'kmx{m}', [128, NQB], F32) for m in range(2)]
    krun = [s.sb(f'krun{m}', [128, NQB], F32) for m in range(2)]
    lb_i = [0]

    def load_block(j):
        for m in range(2):
            sq = sqs[lb_i[0] % 2]
            lb_i[0] += 1
            col = IDX_QK + m * 32 + j
            kb_ = KbB[m][j]
            s.gather(kb_[:], gk, idx[:, col:col + 1], [G3k, idx], [kb_])
            I('act', 'activation', [kb_], [sq], out=sq[:], in_=kb_[:], func=AF.Square)
            I('pe', 'matmul', [ones_bf, sq], [misc], misc[:, 0:512], lhsT=ones_bf[:], rhs=sq[:], start=True, stop=True)
            I('dve', 'reduce_max', [], [misc, kmx[m]], out=kmx[m][:, j:j + 1], in_=misc[:, 0:512], axis=AX.X)
            if j == 0:
                I('dve', 'tensor_copy', [kmx[m]], [krun[m]], out=krun[m][:, 0:1], in_=kmx[m][:, 0:1])
            else:
                I('dve', 'tensor_tensor', [kmx[m], krun[m]], [krun[m]], out=krun[m][:, j:j + 1], in0=krun[m][:, j - 1:j],
                  in1=kmx[m][:, j:j + 1], op=ALU.max)
        vb_ = VbB[j]
        I('pool', 'memset', [], [vb_], vb_[:, :, 256:257], 1.0)
        for q4 in range(4):
            kb = 4 * j + q4
            s.gather(vb_[:, q4, 0:256], G3v.h[:, :], idx[:, IDX_V + kb:IDX_V + kb + 1], [G3v, idx], [vb_])

    load_block(0)
    acc = [s.ps(f'acc{j}', [128, 512]) for j in range(4)]
    stp = [s.ps(f'st{j}', [128, 512]) for j in range(3)]
    qbf = [s.sb('qbf', [128, 512], BF16) for _ in range(2)]
    PT = [s.sb('PT', [128, 512], BF16) for _ in range(3)]
    mq = [s.sb('mq', [128, 1], F32) for _ in range(2)]
    bias = [s.sb('bias', [128, 1], F32) for _ in range(2)]
    o1 = s.sb('o1', [128, 4, 256], F32)
    rl = [s.sb('rl', [128, 1], F32) for _ in range(4)]
    ot = [s.sb('ot', [128, 256], F32) for _ in range(4)]
    oT_st = [s.sb('oTs', [128, 256], F32) for _ in range(2)]
    it = 0
    qi = 0
    for qb in range(NQB):
        if qb + 1 < NQB:
            load_block(qb + 1)
        for m in range(2):
            qb_ = qbf[qi % 2]
            sq = sqs[qi % 2]
            mq_ = mq[qi % 2]
            bi = bias[qi % 2]
            qi += 1
            col = IDX_QK + m * 32 + qb
            s.gather(qb_[:], gq, idx[:, col:col + 1], [G3q, idx], [qb_])
            I('act', 'activation', [qb_], [sq], out=sq[:], in_=qb_[:], func=AF.Square)
            I('pe', 'matmul', [ones_bf, sq], [misc], misc[:, 0:512], lhsT=ones_bf[:], rhs=sq[:], start=True, stop=True)
            I('dve', 'reduce_max', [], [misc, mq_], out=mq_[:], in_=misc[:, 0:512], axis=AX.X)
            I('dve', 'tensor_tensor', [mq_, krun[m]], [mq_], out=mq_[:], in0=mq_[:], in1=krun[m][:, qb:qb + 1], op=ALU.mult)
            I('act', 'activation', [mq_], [mq_], out=mq_[:], in_=mq_[:], func=AF.Sqrt)
            I('dve', 'tensor_scalar', [mq_], [bi], out=bi[:], in0=mq_[:], scalar1=-scale, scalar2=None, op0=ALU.mult)
            nkb = 4 * qb + 4
            LOOK = 2
            slots = {}

            def issue_st(kb):
                nonlocal it
                r = kb - 4 * qb
                c0 = 128 * max(r, 0)
                sp_ = stp[it % 3]
                pt = PT[it % 3]
                it += 1
                slots[kb] = (sp_, pt, r, c0)
                kbuf = KbB[m][kb // 4]
                I('pe', 'matmul', [kbuf, qb_], [sp_], sp_[:, c0:512], lhsT=kbuf[:, (kb % 4) * 128:(kb % 4 + 1) * 128], rhs=qb_[:, c0:512],
                  start=True, stop=True)

            for kb in range(min(LOOK, nkb)):
                issue_st(kb)
            for kb in range(nkb):
                sp_, pt, r, c0 = slots.pop(kb)
                I('act', 'activation', [bi], [sp_, pt], out=pt[:, c0:512], in_=sp_[:, c0:512], func=AF.Exp, bias=bi[:, 0:1], scale=scale)
                if r >= 0:
                    I('pool', 'tensor_tensor', [pt, tri], [pt], out=pt[:, c0:c0 + 128], in0=pt[:, c0:c0 + 128], in1=tri[:], op=ALU.mult)
                if kb + LOOK < nkb:
                    issue_st(kb + LOOK)
                for j in range(max(r, 0), 4):
                    vbuf = VbB[kb // 4]
                    I('pe', 'matmul', [pt, vbuf], [acc[j]], acc[j][:, 0:257], lhsT=pt[:, 128 * j:128 * (j + 1)], rhs=vbuf[:, kb % 4, :],
                      start=(kb == 0), stop=(kb == 4 * qb + j))
            for j in range(4):
                a = acc[j]
                I('dve', 'reciprocal', [], [a, rl[j]], out=rl[j][:], in_=a[:, 256:257])
                if m == 0:
                    I('act', 'mul', [rl[j]], [a, o1], out=o1[:, j, :], in_=a[:, 0:256], mul=rl[j][:, 0:1])
                else:
                    I('dve', 'tensor_tensor', [rl[j], nlam], [rl[j]], out=rl[j][:], in0=rl[j][:], in1=nlam[:], op=ALU.mult)
                    I('dve', 'scalar_tensor_tensor', [rl[j], o1], [a, ot[j]], out=ot[j][:], in0=a[:, 0:256], scalar=rl[j][:, 0:1],
                      in1=o1[:, j, :], op0=ALU.mult, op1=ALU.add)
                    for cch in range(2):
                        I('pe', 'transpose', [ot[j], ident], [misc], misc[:, cch * 128:(cch + 1) * 128],
                          ot[j][:, cch * 128:(cch + 1) * 128], ident[:])
                    oTs = oT_st[(qb * 4 + j) % 2]
                    I('act', 'copy', [], [misc, oTs], out=oTs[:], in_=misc[:, 0:256])
                    tq = qb // 8
                    c0 = (qb % 8) * 512 + j * 128
                    for cch in range(2):
                        r0 = tq * 256 + cch * 128
                        s.dma('sp', O4.h[r0:r0 + 128, c0:c0 + 128], oTs[:, cch * 128:(cch + 1) * 128], reads=[oTs], writes=[O4])
                    if after_block is not None and qb % 8 == 7 and j == 3:
                        after_block(qb // 8)

import math
from concourse.bass_utils import run_bass_kernel_spmd

NCORES = 8
SEQ = 16384
TPC = 4096
LAM_INIT1 = 0.8 - 0.6 * math.exp(-0.3 * 1)
ROT_DIM = 32
ROPE_THETA = 500000.0
NIDX = 516
IDX_G1, IDX_AB, IDX_G2, IDX_QK, IDX_V, IDX_G4 = 0, 192, 196, 260, 324, 452
GROUPS = [[0, 1, 2, 3], [4, 5, 6, 7]]

IN_SPECS = {
    "xT": ((1024, TPC), F32), "cT": ((128, 8, 1), F32), "w0": ((1024, 6144), F32), "w1": ((1024, 6144), F32),
    "w2": ((1024, 2048), F32), "mb": ((128, 112), F32), "gT0": ((128, 8), F32), "w_in": ((1024, 4112), F32),
    "cw": ((128, 24), F32), "hp": ((128, 4), F32), "ident": ((128, 128), F32), "negTi": ((128, 128), F32),
    "negTs": ((128, 128), F32), "tri": ((128, 128), F32), "idx": ((128, NIDX), U32), "og_a": ((128, 1), F32),
    "a_w_out": ((1024, 1024), F32), "gm0": ((128, 8), F32), "m1_0": ((1024, 4096), F32), "m2_0": ((4096, 1024), F32),
    "gkv": ((128, 8), F32), "gq": ((128, 8), F32), "kv_w": ((1024, 2048), F32), "kv_wp": ((1024, 1024), F32),
    "wq": ((1024, 1024), F32), "wqp": ((1024, 1024), F32), "ropeC": ((128, TPC), F32), "ropeS": ((128, TPC), F32),
    "lamT": ((128, 4), F32), "og_b": ((128, 2), F32), "b_w_out": ((1024, 1024), F32), "gm1": ((128, 8), F32),
    "m1_1": ((1024, 4096), F32), "m2_1": ((4096, 1024), F32), "fg": ((128, 8), F32),
}


def build_program(stop_after=99):
    nc = bass.Bass("TRN2", target_bir_lowering=False)
    d = {k: nc.dram_tensor(k, list(shp), dt, kind="ExternalInput").ap() for k, (shp, dt) in IN_SPECS.items()}
    outT = nc.dram_tensor("outT", [1024, TPC], F32, kind="ExternalOutput").ap()
    s = Sched(nc)

    def scratch(name, shape, dt=F32):
        return s.dram(nc.dram_tensor(name, list(shape), dt), name)

    P1 = scratch("P1", (8 * 3136, 512))
    Z1 = scratch("Z1", (1024, TPC))
    G1 = scratch("G1", (4 * 8 * 3136, 512))
    O2 = scratch("O2", (1024, TPC))
    G2 = scratch("G2", (4096, TPC))
    XM = scratch("XM", (1024, TPC))
    X1 = scratch("X1", (1024, TPC))
    Q3 = scratch("Q3", (8 * 1024, 512), BF16)
    K3 = scratch("K3", (8 * 1024, 512), BF16)
    V3 = scratch("V3", (4 * TPC, 256), BF16)
    G3q = scratch("G3q", (4 * 8 * 1024, 512), BF16)
    G3k = scratch("G3k", (4 * 8 * 1024, 512), BF16)
    G3v = scratch("G3v", (16 * TPC, 256), BF16)
    O4 = scratch("O4", (1024, TPC))
    G4 = scratch("G4", (4096, TPC))
    XM2 = scratch("XM2", (1024, TPC))

    C = consts(s)
    gdn_consts(s, C, d["ident"], d["negTi"], d["negTs"])
    trif = s.sb('trif', [128, 128], F32)
    s.dma('sp', trif[:], d["tri"], writes=[trif])
    C['tri'] = s.sb('tri', [128, 128], BF16)
    s.I('dve', 'tensor_copy', [trif], [C['tri']], out=C['tri'][:], in_=trif[:])
    C['idx'] = s.sb('idx', [128, NIDX], U32)
    s.dma('sp', C['idx'][:], d["idx"], writes=[C['idx']])
    C['modv'] = s.sb('modvg', [128, 112], F32)

    def ag(src, dst):
        R, Fc = src.h.shape
        esz = 2 if src.h.dtype == BF16 else 4
        RS = (1 << 20) // (Fc * esz)
        assert R % RS == 0
        for k in range(R // RS):
            ag_slice(src, dst, k, RS)

    def ag_slice(src, dst, k, RS):
        s.allgather(src.h[k * RS:(k + 1) * RS, :], dst.h[k * 4 * RS:(k + 1) * 4 * RS, :], src, dst, GROUPS)

    def p1_after_tb(tb):
        lo = ((tb * 3136 + 511) // 512) if tb > 0 else 0
        for k in range(49):
            if (512 * k + 511) // 3136 == tb:
                ag_slice(P1, G1, k, 512)

    def o2_after_block(blk):
        for k in (2 * blk, 2 * blk + 1):
            ag_slice(O2, G2, k, 64)

    def kvq_after_tb(tb):
        ag_slice(Q3, G3q, tb, 1024)
        ag_slice(K3, G3k, tb, 1024)
        ag_slice(V3, G3v, tb, 2048)

    def o4_after_block(tq):
        for k in range(4 * tq, 4 * tq + 4):
            ag_slice(O4, G4, k, 64)

    s.begin_phase()
    phase_mod(s, C, d["cT"], [d["w0"], d["w1"], d["w2"]], d["mb"])
    s.end_phase()
    if stop_after == 1:
        s.begin_phase()
        s.end_phase(final=True)
        return nc
    s.begin_phase()
    phase_l1(s, C, d["xT"], d["gT0"], d["w_in"], P1, Z1, after_tb=p1_after_tb)
    s.end_phase()
    if stop_after == 2:
        s.begin_phase()
        s.end_phase(final=True)
        return nc
    s.begin_phase()
    phase_gdn(s, C, G1, d["cw"], d["hp"], O2, S=SEQ, IDX_G1=IDX_G1, IDX_AB=IDX_AB, after_block=o2_after_block)
    s.end_phase()
    if stop_after == 3:
        s.begin_phase()
        s.end_phase(final=True)
        return nc
    s.begin_phase()
    phase_post(s, C, 'gdn', G2, IDX_G2, Z1, d["xT"], None, 16, d["og_a"], d["a_w_out"], XM)
    s.end_phase()
    if stop_after == 4:
        s.begin_phase()
        s.end_phase(final=True)
        return nc
    s.begin_phase()
    phase_mlp(s, C, XM, 24, d["gm0"], d["m1_0"], d["m2_0"], X1.h[:, :], XO=X1)
    s.end_phase()
    if stop_after == 5:
        s.begin_phase()
        s.end_phase(final=True)
        return nc
    s.begin_phase()
    phase_kvq(s, C, X1, d["gkv"], d["gq"], d["kv_w"], d["kv_wp"], d["wq"], d["wqp"], d["ropeC"], d["ropeS"], Q3, K3, V3, after_tb=kvq_after_tb)
    s.end_phase()
    if stop_after == 6:
        s.begin_phase()
        s.end_phase(final=True)
        return nc
    s.begin_phase()
    phase_attn(s, C, G3q, G3k, G3v, d["lamT"], O4, LAM_INIT1, S=SEQ, IDX_QK=IDX_QK, IDX_V=IDX_V, after_block=o4_after_block)
    s.end_phase()
    if stop_after == 7:
        s.begin_phase()
        s.end_phase(final=True)
        return nc
    s.begin_phase()
    phase_post(s, C, 'attn', G4, IDX_G4, None, X1.h[:, :], X1, 64, d["og_b"], d["b_w_out"], XM2, lam_init=LAM_INIT1)
    s.end_phase()
    if stop_after == 8:
        s.begin_phase()
        s.end_phase(final=True)
        return nc
    s.begin_phase()
    phase_mlp(s, C, XM2, 72, d["gm1"], d["m1_1"], d["m2_1"], outT, final_g_d=d["fg"])
    s.end_phase(final=True)
    return nc


def _fm8(v):
    return np.ascontiguousarray(np.asarray(v, np.float32).reshape(-1, 128).T)


def _perm_cols(w):
    w = np.asarray(w)
    out = w.copy()
    for c in range(w.shape[1] // 128):
        b = c * 128
        out[:, b:b + 16] = w[:, b + 16:b + 32]
        out[:, b + 16:b + 32] = w[:, b:b + 16]
    return out


def _rope_tables(t0, n):
    pos = (t0 + np.arange(n)).astype(np.float32)
    inv = (np.float32(ROPE_THETA) ** (-(np.arange(0, ROT_DIM, 2, dtype=np.float32)) / np.float32(ROT_DIM))).astype(np.float32)
    fr = (pos[:, None] * inv[None, :]).astype(np.float32)
    cos = np.cos(fr).astype(np.float32).T
    sin = np.sin(fr).astype(np.float32).T
    Cc = np.ones((128, n), np.float32)
    Ss = np.zeros((128, n), np.float32)
    Cc[0:16] = cos
    Cc[16:32] = cos
    Ss[0:16] = -sin
    Ss[16:32] = sin
    return Cc, Ss


def _grow(src, row, RS):
    return ((row // RS) * 4 + src) * RS + row % RS


def _idx_table(r):
    p = np.arange(128, dtype=np.int64)
    t = np.zeros((128, NIDX), np.int64)
    for hh in range(2):
        h = 2 * r + hh
        for X in range(3):
            for sb in range(32):
                t[:, IDX_G1 + (hh * 3 + X) * 32 + sb] = _grow(sb // 8, (sb % 8) * 3136 + X * 1024 + h * 128 + p, 512)
        t[:, IDX_AB + 2 * hh] = _grow(p // 32, ((p % 32) // 4) * 3136 + 3072 + h, 512) * 4 + p % 4
        t[:, IDX_AB + 2 * hh + 1] = _grow(p // 32, ((p % 32) // 4) * 3136 + 3080 + h, 512) * 4 + p % 4
    for kc in range(8):
        for tb in range(8):
            t[:, IDX_G2 + kc * 8 + tb] = _grow(kc // 2, ((kc % 2) * 4 + r) * 128 + p, 64) * 8 + tb
            t[:, IDX_G4 + kc * 8 + tb] = _grow(kc // 2, r * 256 + (kc % 2) * 128 + p, 64) * 8 + tb
    for m in range(2):
        for j in range(32):
            t[:, IDX_QK + m * 32 + j] = _grow(j // 8, (j % 8) * 1024 + r * 256 + m * 128 + p, 1024)
    for kb in range(128):
        t[:, IDX_V + kb] = _grow(kb // 32, (((kb % 32) // 4) * 4 + r) * 512 + ((kb % 32) % 4) * 128 + p, 2048)
    return t.astype(np.uint32)


def kernel(x, c, mod_w, mod_b, norm_mix_g, norm_mlp_g, a_w_in, a_conv_w, a_log, a_dt_bias, a_out_norm_g, a_w_out,
           kv_mod_w, kv_mod_b, kv_norm_g, kv_w, b_w_q, b_lambda, b_subln_g, b_w_out, mlp_w1, mlp_w2, final_g):
    f32 = np.float32
    x = np.asarray(x, f32)
    B, S, D = x.shape
    A = lambda v: np.asarray(v, f32)
    jj, ii = np.meshgrid(np.arange(128), np.arange(128), indexing='ij')
    shared = {
        "w0": A(mod_w[0]), "w1": A(mod_w[1]), "w2": A(kv_mod_w),
        "mb": np.ascontiguousarray(np.concatenate([A(mod_b[0]), A(mod_b[1]), A(kv_mod_b)]).reshape(112, 128).T),
        "gT0": _fm8(norm_mix_g[0]), "w_in": A(a_w_in[0]), "ident": np.eye(128, dtype=f32),
        "negTi": np.where(ii >= jj, 0.0, -30000.0).astype(f32), "negTs": np.where(ii > jj, 0.0, -30000.0).astype(f32),
        "tri": (np.arange(128)[:, None] <= np.arange(128)[None, :]).astype(f32),
        "og_a": A(a_out_norm_g[0]).reshape(128, 1), "a_w_out": A(a_w_out[0]), "gm0": _fm8(norm_mlp_g[0]),
        "m1_0": A(mlp_w1[0]), "m2_0": A(mlp_w2[0]), "gkv": _fm8(kv_norm_g), "gq": _fm8(norm_mix_g[1]),
        "kv_w": A(kv_w), "kv_wp": _perm_cols(A(kv_w)[:, :1024]), "wq": A(b_w_q[0]), "wqp": _perm_cols(A(b_w_q[0])),
        "lamT": np.ascontiguousarray(A(b_lambda[0]).T), "og_b": np.ascontiguousarray(A(b_subln_g[0]).reshape(2, 128).T),
        "b_w_out": A(b_w_out[0]), "gm1": _fm8(norm_mlp_g[1]), "m1_1": A(mlp_w1[1]), "m2_1": A(mlp_w2[1]), "fg": _fm8(final_g),
    }
    maps = []
    for cc in range(NCORES):
        b, r = cc // 4, cc % 4
        t0 = r * TPC
        m = dict(shared)
        m["xT"] = np.ascontiguousarray(x[b, t0:t0 + TPC].T)
        m["cT"] = np.ascontiguousarray(A(c)[b].reshape(8, 128).T.reshape(128, 8, 1))
        cw = np.empty((128, 24), f32)
        hp = np.empty((128, 4), f32)
        for hh in range(2):
            h = 2 * r + hh
            for X in range(3):
                r0 = X * 1024 + h * 128
                for j in range(4):
                    cw[:, (hh * 3 + X) * 4 + j] = A(a_conv_w[0])[j, r0:r0 + 128]
            hp[:, 2 * hh] = A(a_log[0])[h]
            hp[:, 2 * hh + 1] = A(a_dt_bias[0])[h]
        m["cw"], m["hp"] = cw, hp
        m["ropeC"], m["ropeS"] = _rope_tables(t0, TPC)
        m["idx"] = _idx_table(r)
        maps.append({k: np.ascontiguousarray(m[k]) for k in IN_SPECS})
    nc = build_program()
    res = run_bass_kernel_spmd(nc, maps, core_ids=list(range(NCORES)))
    out = np.empty((B, S, D), f32)
    for cc in range(NCORES):
        b, r = cc // 4, cc % 4
        out[b, r * TPC:(r + 1) * TPC] = np.asarray(res.results[cc]["outT"]).T
    return out
```

```python
import contextlib
from collections import defaultdict
import numpy as np
import concourse.bass as bass
import concourse.mybir as mybir

F32 = mybir.dt.float32
BF16 = mybir.dt.bfloat16
U32 = mybir.dt.uint32
AF = mybir.ActivationFunctionType
ALU = mybir.AluOpType
AX = mybir.AxisListType

ENGS = ['sp', 'act', 'dve', 'pool', 'pe']


class Buf:
    def __init__(self, h, name, tr=None, multi=False):
        self.h = h
        self.name = name
        self.multi = multi
        self.tr = tr if tr is not None else {'lw': None, 'rd': {}, 'ws': {}}

    def __getitem__(self, k):
        return self.h[k]

    def view(self, pattern, **kw):
        return Buf(self.h[:].rearrange(pattern, **kw), self.name, self.tr, self.multi)


class Sched:
    def __init__(self, nc):
        self.nc = nc
        self.es = contextlib.ExitStack()
        self.ops = {e: [] for e in ENGS}
        self.val = defaultdict(int)
        self.known = {e: defaultdict(int) for e in ENGS}
        self.sem = {}
        for e in ENGS:
            self.sem['c_' + e] = self.es.enter_context(nc.semaphore('c_' + e))
        self.NS = 8
        self.ndma = defaultdict(int)
        for e in ['sp', 'act', 'pool']:
            for k in range(self.NS):
                nm = f'd_{e}_{k}'
                self.sem[nm] = self.es.enter_context(nc.semaphore(nm))
        self.nbuf = 0
        self.pes = None
        self.sem['cc'] = self.es.enter_context(nc.semaphore('cc'))

    def begin_phase(self):
        self.pes = contextlib.ExitStack()
        if hasattr(self, '_stage'):
            del self._stage

    def end_phase(self, final=False):
        allc = [(c, v) for c, v in self.val.items() if v > 0]
        for eng in ENGS:
            waits = []
            for c, v in allc:
                if self.known[eng][c] < v:
                    waits.append((c, v))
                    self.known[eng][c] = v
            self.ops[eng].append((waits, None, None, 0))
        self._emit_block()
        if self.pes is not None:
            self.pes.close()
            self.pes = None
        if final:
            self.es.close()

    def _stack(self):
        return self.pes if self.pes is not None else self.es

    def sb(self, name, shape, dt):
        self.nbuf += 1
        return Buf(self._stack().enter_context(self.nc.sbuf_tensor(f"{name}_{self.nbuf}", list(shape), dt)), name)

    def ps(self, name, shape, dt=F32):
        self.nbuf += 1
        return Buf(self._stack().enter_context(self.nc.psum_tensor(f"{name}_{self.nbuf}", list(shape), dt)), name)

    def dram(self, h, name):
        return Buf(h, name, multi=True)

    def op(self, eng, fn, reads=(), writes=(), dma=False, cc=False):
        deps = {}
        if cc:
            ctr = 'cc'
        elif dma:
            k = self.ndma[eng] % self.NS
            self.ndma[eng] += 1
            ctr = f'd_{eng}_{k}'
            if self.val[ctr] > 0:
                deps[ctr] = self.val[ctr]
        else:
            ctr = 'c_' + eng
        for b in reads:
            lw = b.tr['lw']
            if lw is not None:
                deps[lw[0]] = max(deps.get(lw[0], 0), lw[1])
            if b.multi:
                for c, v in b.tr['ws'].items():
                    deps[c] = max(deps.get(c, 0), v)
        for b in writes:
            if b.multi:
                continue
            lw = b.tr['lw']
            if lw is not None:
                deps[lw[0]] = max(deps.get(lw[0], 0), lw[1])
            for c, v in b.tr['rd'].items():
                deps[c] = max(deps.get(c, 0), v)
        waits = []
        for c, v in deps.items():
            if eng == 'pe' and c == 'c_pe':
                continue
            if self.known[eng][c] < v:
                waits.append((c, v))
                self.known[eng][c] = v
        inc = 1 if cc else (16 if dma else 1)
        self.val[ctr] += inc
        v = self.val[ctr]
        self.ops[eng].append((waits, fn, ctr, inc))
        for b in reads:
            b.tr['rd'][ctr] = max(b.tr['rd'].get(ctr, 0), v)
        for b in writes:
            if b.multi:
                b.tr['ws'][ctr] = max(b.tr['ws'].get(ctr, 0), v)
                continue
            b.tr['lw'] = (ctr, v)
            b.tr['rd'] = {}

    def I(self, eng, method, reads, writes, *a, **kw):
        self.op(eng, lambda e: getattr(e, method)(*a, **kw), reads, writes)

    def dma(self, eng, out_ap, in_ap, reads=(), writes=(), **kw):
        self.op(eng, lambda e: e.dma_start(out=out_ap, in_=in_ap, **kw), reads, writes, dma=True)

    def allgather(self, in_ap, out_ap, in_buf, out_buf, groups):
        self.op('pool', lambda e: e.collective_compute("AllGather", ALU.bypass, replica_groups=groups,
                                                       ins=[in_ap.opt()], outs=[out_ap.opt()]),
                reads=[in_buf], writes=[out_buf], cc=True)

    def gather(self, out_ap, in_view, idx_ap, reads, writes):
        self.op('pool', lambda e: e.indirect_dma_start(out=out_ap, out_offset=None, in_=in_view,
                                                       in_offset=bass.IndirectOffsetOnAxis(ap=idx_ap, axis=0)),
                reads, writes, dma=True)

    def _emit_block(self):
        nc = self.nc
        with nc.Block() as block:
            decos = {'sp': block.sync, 'act': block.scalar, 'dve': block.vector,
                     'pool': block.gpsimd, 'pe': block.tensor}
            for eng in ENGS:
                ops = self.ops[eng]

                def body(e, ops=ops):
                    for waits, fn, ctr, inc in ops:
                        for c, v in waits:
                            e.wait_ge(self.sem[c], v)
                        if fn is not None:
                            ins = fn(e)
                            if inc == 1 and ctr == 'cc':
                                ins.then_inc(self.sem[ctr])
                            else:
                                ins.then_inc(self.sem[ctr], inc)
                decos[eng](body)
        self.nops_total = getattr(self, 'nops_total', 0) + sum(len(v) for v in self.ops.values())
        self.ops = {e: [] for e in ENGS}

    def emit(self):
        self.end_phase(final=True)

    def n_ops(self):
        return {e: len(self.ops[e]) for e in ENGS}


EPS = 1e-6


def consts(s):
    c = {}
    c['ones_bf'] = s.sb('ones_bf', [128, 128], BF16)
    s.op('pool', lambda e: e.memset(c['ones_bf'][:], 1.0), writes=[c['ones_bf']])
    return c


def phase_mod(s, C, cT_d, ws, modbT_d):
    NJ = sum(w.shape[1] for w in ws) // 128
    cs = s.sb('cs', [128, 8, 1], F32)
    s.dma('sp', cs[:], cT_d, writes=[cs])
    s.op('act', lambda e: e.activation(out=cs[:], in_=cs[:], func=AF.Silu), reads=[cs], writes=[cs])
    mb = s.sb('mb', [128, NJ], F32)
    s.dma('sp', mb[:], modbT_d, writes=[mb])
    ps = s.ps('modps', [128, 512])
    st = [s.sb('mst', [128, 8, 512], F32) for _ in range(2)]
    j = 0
    k = 0
    for w in ws:
        wv = w.rearrange("(kc p) n -> p kc n", p=128)
        for n0 in range(0, w.shape[1], 512):
            b = st[k % 2]
            s.dma('sp' if k % 2 == 0 else 'pool', b[:], wv[:, :, n0:n0 + 512], writes=[b])
            k += 1
            for jj in range(4):
                for kc in range(8):
                    s.I('pe', 'matmul', [b, cs], [ps], ps[:, j:j + 1], lhsT=b[:, kc, jj * 128:(jj + 1) * 128], rhs=cs[:, kc, :],
                        start=(kc == 0), stop=(kc == 7))
                j += 1
    modv = C['modv']
    s.I('dve', 'tensor_tensor', [mb], [ps, modv], out=modv[:], in0=ps[:, 0:NJ], in1=mb[:], op=ALU.add)


def get_stage(s):
    if not hasattr(s, '_stage'):
        s._stage = [s.sb('wst', [128, 2048], F32) for _ in range(2)]
        s._stage_i = 0
    return s._stage


def load_cast_w(s, Wb, w_d, piece=512, engs=('dve', 'pool')):
    K, N = w_d.shape
    KC = K // 128
    nk = 2048 // piece
    wv = w_d.rearrange("(kc p) n -> p kc n", p=128)
    st = get_stage(s)
    for k0 in range(0, KC, nk):
        kw = min(nk, KC - k0)
        for n0 in range(0, N, piece):
            nw = min(piece, N - n0)
            i = s._stage_i
            s._stage_i += 1
            b = st[i % 2]
            bv = b[:].rearrange("p (k n) -> p k n", n=piece)
            s.dma('sp' if i % 2 == 0 else 'pool', bv[:, 0:kw, 0:nw], wv[:, k0:k0 + kw, n0:n0 + nw], writes=[b])
            eng = engs[i % len(engs)]
            s.I(eng, 'tensor_copy', [b], [Wb], out=Wb[:, k0:k0 + kw, n0:n0 + nw], in_=bv[:, 0:kw, 0:nw])


def norm_mod(s, C, xs, G, SH, h, tmp, sq, ss_ps, rstd, D=1024, TB=512):
    KC = D // 128
    s.op('act', lambda e: e.activation(out=sq[:], in_=xs[:], func=AF.Square), reads=[xs], writes=[sq])
    for kc in range(KC):
        s.op('pe', lambda e, kc=kc: e.matmul(ss_ps[:, 0:TB], lhsT=C['ones_bf'][:], rhs=sq[:, kc, :],
                                              start=(kc == 0), stop=(kc == KC - 1)),
             reads=[sq, C['ones_bf']], writes=[ss_ps])
    s.op('dve', lambda e: e.tensor_scalar(out=rstd[:], in0=ss_ps[:, 0:TB], scalar1=1.0 / D, scalar2=EPS,
                                          op0=ALU.mult, op1=ALU.add), reads=[ss_ps], writes=[rstd])
    s.op('dve', lambda e: e.reciprocal(out=rstd[:], in_=rstd[:]), reads=[rstd], writes=[rstd])
    s.op('act', lambda e: e.activation(out=rstd[:], in_=rstd[:], func=AF.Sqrt), reads=[rstd], writes=[rstd])
    for kc in range(KC):
        t = tmp[kc % len(tmp)]
        s.op('dve', lambda e, kc=kc, t=t: e.scalar_tensor_tensor(
            out=t[:], in0=xs[:, kc, :], scalar=G[0][:, G[1] + kc:G[1] + kc + 1], in1=rstd[:],
            op0=ALU.mult, op1=ALU.mult), reads=[xs, G[0], rstd], writes=[t])
        s.op('act', lambda e, kc=kc, t=t: e.activation(
            out=h[:, kc, :], in_=t[:], func=AF.Identity, bias=SH[0][:, SH[1] + kc:SH[1] + kc + 1], scale=1.0),
             reads=[t, SH[0]], writes=[h])


def phase_l1(s, C, xT_d, gT_d, w_in_d, P1, Z1, after_tb=None, NT=4096, TB=512):
    NOUT = w_in_d.shape[1]
    modv = C['modv']
    gT = s.sb('gT', [128, 8], F32)
    s.dma('sp', gT[:], gT_d, writes=[gT])
    G = s.sb('G', [128, 8], F32)
    s.op('dve', lambda e: e.scalar_tensor_tensor(out=G[:], in0=modv[:, 8:16], scalar=1.0, in1=gT[:],
                                                 op0=ALU.add, op1=ALU.mult), reads=[modv, gT], writes=[G])
    Wb = s.sb('Wb', [128, 8, NOUT], BF16)
    load_cast_w(s, Wb, w_in_d)
    xs = [s.sb('xs', [128, 8, TB], F32) for _ in range(2)]
    sq = s.sb('sq', [128, 8, TB], BF16)
    hs = [s.sb('h', [128, 8, TB], BF16) for _ in range(2)]
    tmp = [s.sb('tmp', [128, TB], F32) for _ in range(2)]
    rstd = s.sb('rstd', [128, TB], F32)
    ss_ps = s.ps('ss', [128, 512])
    pps = [s.ps('pp', [128, 512]) for _ in range(4)]
    outs = [s.sb('ost', [128, TB], F32) for _ in range(4)]
    xv = xT_d.rearrange("(kc p) t -> p kc t", p=128)
    nchunks = [(n0, min(128, NOUT - n0)) for n0 in range(0, NOUT, 128)]
    it = 0
    for tb in range(NT // TB):
        x = xs[tb % 2]
        h = hs[tb % 2]
        s.dma('sp', x[:], xv[:, :, tb * TB:(tb + 1) * TB], writes=[x])
        norm_mod(s, C, x, (G, 0), (modv, 0), h, tmp, sq, ss_ps, rstd)
        for (n0, nw) in nchunks:
            pp = pps[it % 4]
            o = outs[it % 4]
            for kc in range(8):
                s.op('pe', lambda e, pp=pp, kc=kc, n0=n0, nw=nw, h=h: e.matmul(
                    pp[0:nw, 0:TB], lhsT=Wb[:, kc, n0:n0 + nw], rhs=h[:, kc, :], start=(kc == 0), stop=(kc == 7)),
                     reads=[Wb, h], writes=[pp])
            if it % 2 == 0:
                s.op('act', lambda e, pp=pp, o=o, nw=nw: e.copy(out=o[0:nw, :], in_=pp[0:nw, 0:TB]),
                     reads=[pp], writes=[o])
            else:
                s.op('dve', lambda e, pp=pp, o=o, nw=nw: e.tensor_copy(out=o[0:nw, :], in_=pp[0:nw, 0:TB]),
                     reads=[pp], writes=[o])
            if n0 < 3072:
                dst, db = P1.h[tb * 3136 + n0:tb * 3136 + n0 + nw, :], P1
            elif n0 < 4096:
                dst, db = Z1.h[n0 - 3072:n0 - 3072 + nw, tb * TB:(tb + 1) * TB], Z1
            else:
                dst, db = P1.h[tb * 3136 + 3072:tb * 3136 + 3072 + nw, :], P1
            s.dma('sp', dst, o[0:nw, :], reads=[o], writes=[db])
            it += 1
        if after_tb is not None:
            after_tb(tb)


L2EPS = 1e-6


def gdn_consts(s, C, ident_d, negTi_d, negTs_d):
    for nm, d in (('ident', ident_d), ('negTi', negTi_d), ('negTs', negTs_d)):
        C[nm] = s.sb(nm, [128, 128], F32)
        s.dma('sp', C[nm][:], d, writes=[C[nm]])
        C[nm + '_bf'] = s.sb(nm + '_bf', [128, 128], BF16)
        s.I('dve', 'tensor_copy', [C[nm]], [C[nm + '_bf']], out=C[nm + '_bf'][:], in_=C[nm][:])


class PsumPool:
    def __init__(self, s, nbanks, name):
        self.banks = [s.ps(f'{name}{i}', [128, 512]) for i in range(nbanks)]
        self.i = 0

    def get(self):
        b = self.banks[(self.i // 4) % len(self.banks)]
        sl = self.i % 4
        self.i += 1
        return b, b[:, sl * 128:(sl + 1) * 128]

    def get2(self):
        if self.i % 2:
            self.i += 1
        b = self.banks[(self.i // 4) % len(self.banks)]
        sl = self.i % 4
        self.i += 2
        return b, b[:, sl * 128:(sl + 2) * 128]

    def get_bank(self):
        self.i = ((self.i + 3) // 4) * 4
        b = self.banks[(self.i // 4) % len(self.banks)]
        self.i += 4
        return b


def phase_gdn(s, C, G1, cw_d, hp_d, O2, S=16384, NH=2, IDX_G1=0, IDX_AB=192, after_block=None):
    SB = 512
    NSB = S // SB
    scale = 128 ** -0.5
    ident = C['ident']
    idx = C['idx']
    g1v512 = G1.h[:, :]
    I = s.I
    cw = s.sb('cw', [128, NH * 3 * 4], F32)
    s.dma('sp', cw[:], cw_d, writes=[cw])
    hp = s.sb('hp', [128, NH * 2], F32)
    s.dma('sp', hp[:], hp_d, writes=[hp])
    pp = PsumPool(s, 6, 'lv')
    seqbank = [s.ps(f'seq{h}', [128, 512]) for h in range(NH)]
    bc127 = ident[:, 127:128].to_broadcast([128, 128])
    import math
    lnsc = s.sb('lnsc', [128, 1], F32)
    I('pool', 'memset', [], [lnsc], lnsc[:], math.log(scale))
    lnscale_c = lnsc[:, 0:1]

    tabs = []
    for h in range(NH):
        T = {}
        for nm in ('a', 'b', 'x', 't', 'gcum', 'rtab', 'gcumT', 'betaT', 'negg', 'be', 'kdf', 'glc', 'ones'):
            T[nm] = s.sb(f'{nm}{h}', [128, 128], F32)
        T['nA'] = s.sb(f'nA{h}', [128, 1], F32)
        a, b, x, t, nA = T['a'], T['b'], T['x'], T['t'], T['nA']
        g1v128 = G1.h[:, :].rearrange("r (b w) -> (r b) w", w=128)
        s.gather(a[:], g1v128, idx[:, IDX_AB + 2 * h:IDX_AB + 2 * h + 1], [G1, idx], [a])
        s.gather(b[:], g1v128, idx[:, IDX_AB + 2 * h + 1:IDX_AB + 2 * h + 2], [G1, idx], [b])
        I('pool', 'memset', [], [T['ones']], T['ones'][:], 1.0)
        I('dve', 'tensor_scalar', [a, hp], [x], out=x[:], in0=a[:], scalar1=hp[:, 2 * h + 1:2 * h + 2], scalar2=None, op0=ALU.add)
        I('act', 'activation', [x], [t], out=t[:], in_=x[:], func=AF.Abs)
        I('act', 'activation', [t], [t], out=t[:], in_=t[:], func=AF.Exp, scale=-1.0)
        I('act', 'activation', [t], [t], out=t[:], in_=t[:], func=AF.Ln, bias=1.0)
        I('dve', 'tensor_scalar_max', [x], [x], out=x[:], in0=x[:], scalar1=0.0)
        I('dve', 'tensor_tensor', [x, t], [x], out=x[:], in0=x[:], in1=t[:], op=ALU.add)
        I('act', 'activation', [hp], [nA], out=nA[:], in_=hp[:, 2 * h:2 * h + 1], func=AF.Exp)
        I('dve', 'tensor_scalar', [nA], [nA], out=nA[:], in0=nA[:], scalar1=-1.0, scalar2=None, op0=ALU.mult)
        I('dve', 'tensor_scalar', [x, nA], [x], out=x[:], in0=x[:], scalar1=nA[:, 0:1], scalar2=None, op0=ALU.mult)
        I('dve', 'tensor_tensor_scan', [T['ones'], x], [T['gcum']], out=T['gcum'][:], data0=T['ones'][:], data1=x[:],
          initial=0.0, op0=ALU.mult, op1=ALU.add)
        I('act', 'activation', [b], [b], out=b[:], in_=b[:], func=AF.Sigmoid)
        I('act', 'activation', [b], [t], out=t[:], in_=b[:], func=AF.Ln)
        I('dve', 'tensor_tensor', [T['gcum'], t], [T['rtab']], out=T['rtab'][:], in0=T['gcum'][:], in1=t[:], op=ALU.add)
        bk, p1 = pp.get()
        I('pe', 'transpose', [T['gcum'], ident], [bk], p1, T['gcum'][:], ident[:])
        I('act', 'copy', [], [bk, T['gcumT']], out=T['gcumT'][:], in_=p1)
        bk2, p2 = pp.get()
        I('pe', 'transpose', [b, ident], [bk2], p2, b[:], ident[:])
        I('act', 'copy', [], [bk2, T['betaT']], out=T['betaT'][:], in_=p2)
        bk3, p3 = pp.get()
        I('pe', 'matmul', [ident, T['gcumT']], [bk3], p3, lhsT=bc127, rhs=T['gcumT'][:], start=True, stop=True)
        I('act', 'activation', [], [bk3, T['glc']], out=T['glc'][:], in_=p3, func=AF.Exp)
        I('dve', 'tensor_tensor', [T['gcumT']], [bk3, T['kdf']], out=T['kdf'][:], in0=p3, in1=T['gcumT'][:], op=ALU.subtract)
        I('act', 'activation', [T['kdf']], [T['kdf']], out=T['kdf'][:], in_=T['kdf'][:], func=AF.Exp)
        I('dve', 'tensor_scalar', [T['gcumT']], [T['negg']], out=T['negg'][:], in0=T['gcumT'][:], scalar1=-1.0, scalar2=None, op0=ALU.mult)
        I('act', 'activation', [T['gcumT']], [T['be']], out=T['be'][:], in_=T['gcumT'][:], func=AF.Exp)
        I('dve', 'tensor_tensor', [T['be'], T['betaT']], [T['be']], out=T['be'][:], in0=T['be'][:], in1=T['betaT'][:], op=ALU.mult)
        T['cat'] = s.sb(f'cat{h}', [128, 384], F32)
        I('pool', 'tensor_copy', [T['gcum']], [T['cat']], out=T['cat'][:, 0:128], in_=T['gcum'][:])
        I('pool', 'tensor_copy', [T['rtab']], [T['cat']], out=T['cat'][:, 128:256], in_=T['rtab'][:])
        I('pool', 'tensor_copy', [T['gcum']], [T['cat']], out=T['cat'][:, 256:384], in_=T['gcum'][:])
        tabs.append(T)

    S32 = [s.sb(f'S32_{h}', [128, 128], F32) for h in range(NH)]
    Sbf = [s.sb(f'Sbf_{h}', [128, 128], BF16) for h in range(NH)]
    for h in range(NH):
        I('pool', 'memset', [], [S32[h]], S32[h][:], 0.0)
        I('pool', 'memset', [], [Sbf[h]], Sbf[h][:], 0.0)

    NSET = 3
    sets = []
    for u in range(NSET):
        U = {}
        U['xin'] = [s.sb(f'xin{u}{X}', [128, 3 + SB], F32) for X in range(3)]
        U['acc'] = [s.sb(f'acc{u}{X}', [128, SB], F32) for X in range(3)]
        U['cv'] = [s.sb(f'cv{u}{X}', [128, SB], F32) for X in range(3)]
        U['sq'] = s.sb(f'sq{u}', [128, SB], BF16)
        U['rn'] = s.sb(f'rn{u}', [128, SB], F32)
        U['xn'] = [s.sb(f'xn{u}{X}', [128, SB], F32) for X in range(2)]
        U['ost'] = s.sb(f'ost{u}', [128, 4, 128], F32)
        U['ch'] = []
        for c in range(4):
            D = {}
            for nm in ('kbg', 'vb', 'CT', 'DT', 'EGB', 'Q0', 'Q1', 'u'):
                D[nm] = s.sb(f'{nm}{u}{c}', [128, 128], F32)
            for nm in ('YP0', 'YP1'):
                D[nm] = s.sb(f'{nm}{u}{c}', [128, 256], F32)
            for nm in ('kdec', 'qkT', 'qg', 'wT', 'vnew'):
                D[nm] = s.sb(f'{nm}{u}{c}', [128, 128], BF16)
            U['ch'].append(D)
        sets.append(U)

    def stageA_gen(h, sb, U, Uprev):
        T = tabs[h]
        t0 = sb * SB
        for X in range(3):
            xin = U['xin'][X]
            if sb == 0:
                I('pool', 'memset', [], [xin], xin[:, 0:3], 0.0)
            else:
                pxin = Uprev['xin'][X]
                I('pool', 'tensor_copy', [pxin], [xin], out=xin[:, 0:3], in_=pxin[:, SB:SB + 3])
            col = IDX_G1 + (h * 3 + X) * 32 + sb
            s.gather(xin[:, 3:3 + SB], g1v512, idx[:, col:col + 1], [G1, idx], [xin])
        yield
        for X in range(3):
            xin = U['xin'][X]
            eng = 'dve'
            acc = U['acc'][X]
            wc = (h * 3 + X) * 4
            I(eng, 'tensor_scalar', [xin, cw], [acc], out=acc[:], in0=xin[:, 0:SB], scalar1=cw[:, wc:wc + 1], scalar2=None, op0=ALU.mult)
            for j in range(1, 4):
                if eng == 'dve':
                    I(eng, 'scalar_tensor_tensor', [xin, cw, acc], [acc], out=acc[:], in0=xin[:, j:j + SB],
                      scalar=cw[:, wc + j:wc + j + 1], in1=acc[:], op0=ALU.mult, op1=ALU.add)
                else:
                    tq = U['rn']
                    I(eng, 'tensor_scalar', [xin, cw], [tq], out=tq[:], in0=xin[:, j:j + SB], scalar1=cw[:, wc + j:wc + j + 1],
                      scalar2=None, op0=ALU.mult)
                    I(eng, 'tensor_tensor', [tq, acc], [acc], out=acc[:], in0=acc[:], in1=tq[:], op=ALU.add)
            cv = U['cv'][X]
            I('act', 'activation', [acc], [cv], out=cv[:], in_=acc[:], func=AF.Silu)
            yield
        for X in range(2):
            cv = U['cv'][X]
            sq, rn, xn = U['sq'], U['rn'], U['xn'][X]
            I('act', 'activation', [cv], [sq], out=sq[:], in_=cv[:], func=AF.Square)
            bk = pp.get_bank()
            I('pe', 'matmul', [C['ones_bf'], sq], [bk], bk[:, 0:SB], lhsT=C['ones_bf'][:], rhs=sq[:], start=True, stop=True)
            I('dve', 'tensor_scalar', [], [bk, rn], out=rn[:], in0=bk[:, 0:SB], scalar1=L2EPS, scalar2=None, op0=ALU.add)
            I('dve', 'reciprocal', [rn], [rn], out=rn[:], in_=rn[:])
            I('act', 'activation', [rn], [rn], out=rn[:], in_=rn[:], func=AF.Sqrt)
            I('pool', 'tensor_tensor', [cv, rn], [xn], out=xn[:], in0=cv[:], in1=rn[:], op=ALU.mult)
            yield

    def prep_gen(h, sb, U):
        T = tabs[h]
        qn, kn, vv = U['xn'][0], U['xn'][1], U['cv'][2]
        for c in range(4):
            D = U['ch'][c]
            n = sb * 4 + c
            cs = slice(c * 128, (c + 1) * 128)
            icol = ident[:, n:n + 1].to_broadcast([128, 128])
            be_c, kdf_c, bet_c, ng_c = (T[k][:, n:n + 1] for k in ('be', 'kdf', 'betaT', 'negg'))
            b_trk, p_trk = pp.get()
            I('pe', 'transpose', [kn, ident], [b_trk], p_trk, kn[:, cs], ident[:])
            b_trv, p_trv = pp.get()
            I('pe', 'transpose', [vv, ident], [b_trv], p_trv, vv[:, cs], ident[:])
            b_kk, p_kk = pp.get()
            I('pe', 'matmul', [kn], [b_kk], p_kk, lhsT=kn[:, cs], rhs=kn[:, cs], start=True, stop=True)
            b_bc = pp.get_bank()
            p_dt, p_ct, p_gb = b_bc[:, 0:128], b_bc[:, 128:256], b_bc[:, 256:384]
            I('pe', 'matmul', [ident, T['cat']], [b_bc], b_bc[:, 0:384], lhsT=icol, rhs=T['cat'][:], start=True, stop=False, skip_group_check=True)
            I('pe', 'matmul', [C['ident_bf'], C['negTi_bf']], [b_bc], p_dt, lhsT=C['ident_bf'][:], rhs=C['negTi_bf'][:],
              start=False, stop=False, skip_group_check=True)
            I('pe', 'matmul', [C['ident_bf'], C['negTs_bf']], [b_bc], p_ct, lhsT=C['ident_bf'][:], rhs=C['negTs_bf'][:],
              start=False, stop=True, skip_group_check=True)
            I('dve', 'tensor_scalar', [T['be']], [b_trk, D['kbg']], out=D['kbg'][:], in0=p_trk, scalar1=be_c, scalar2=None, op0=ALU.mult)
            I('dve', 'tensor_scalar', [T['kdf']], [b_trk, D['kdec']], out=D['kdec'][:], in0=p_trk, scalar1=kdf_c, scalar2=None, op0=ALU.mult)
            I('dve', 'tensor_scalar', [T['betaT']], [b_trv, D['vb']], out=D['vb'][:], in0=p_trv, scalar1=bet_c, scalar2=None, op0=ALU.mult)
            I('act', 'activation', [T['negg']], [b_bc, D['CT']], out=D['CT'][:], in_=p_ct, func=AF.Exp, bias=ng_c, scale=1.0)
            YP0 = D['YP0']
            I('dve', 'scalar_tensor_tensor', [D['CT']], [b_kk, YP0], out=YP0[:, 128:256], in0=p_kk, scalar=-1.0, in1=D['CT'][:],
              op0=ALU.mult, op1=ALU.mult)
            I('pool', 'tensor_copy', [ident], [YP0], out=YP0[:, 0:128], in_=ident[:])
            b_qk, p_qk = pp.get()
            I('pe', 'matmul', [kn, qn], [b_qk], p_qk, lhsT=kn[:, cs], rhs=qn[:, cs], start=True, stop=True)
            I('act', 'activation', [T['negg']], [b_bc, D['DT']], out=D['DT'][:], in_=p_dt, func=AF.Exp, bias=ng_c, scale=1.0)
            I('dve', 'scalar_tensor_tensor', [D['DT']], [b_qk, D['qkT']], out=D['qkT'][:], in0=p_qk, scalar=scale, in1=D['DT'][:],
              op0=ALU.mult, op1=ALU.mult)
            I('act', 'activation', [lnsc], [b_bc, D['EGB']], out=D['EGB'][:], in_=p_gb, func=AF.Exp, bias=lnscale_c, scale=1.0)
            I('pool', 'tensor_tensor', [qn, D['EGB']], [D['qg']], out=D['qg'][:], in0=qn[:, cs], in1=D['EGB'][:], op=ALU.mult)
            yield
        for c in range(4):
            D = U['ch'][c]
            b_q, p_q = pp.get()
            I('pe', 'transpose', [D['YP0'], ident], [b_q], p_q, D['YP0'][:, 128:256], ident[:])
            I('dve', 'tensor_copy', [], [b_q, D['Q0']], out=D['Q0'][:], in_=p_q)
        yield
        m = 1
        cur = 0
        while m <= 64:
            nxt = 1 - cur
            YPc, YPn, Qc, Qn_ = f'YP{cur}', f'YP{nxt}', f'Q{cur}', f'Q{nxt}'
            for c in range(4):
                D = U['ch'][c]
                wa = 256 if m < 32 else 128
                if wa == 256:
                    ba, pa_ = pp.get2()
                else:
                    ba, pa_ = pp.get()
                I('pe', 'matmul', [D[Qc], D[YPc]], [ba], pa_, lhsT=D[Qc][:], rhs=D[YPc][:, 0:wa], start=True, stop=True)
                if m < 64:
                    b2, p2 = pp.get()
                    I('pe', 'matmul', [D[Qc], D[YPc]], [b2], p2, lhsT=D[YPc][:, 128:256], rhs=D[Qc][:], start=True, stop=True)
                I('dve', 'tensor_tensor', [D[YPc]], [ba, D[YPn]], out=D[YPn][:, 0:128], in0=pa_[:, 0:128], in1=D[YPc][:, 0:128], op=ALU.add)
                if wa == 256:
                    I('act', 'copy', [], [ba, D[YPn]], out=D[YPn][:, 128:256], in_=pa_[:, 128:256])
                if m < 64:
                    I('act', 'copy', [], [b2, D[Qn_]], out=D[Qn_][:], in_=p2)
            cur = nxt
            m *= 2
            yield
        Yf = f'YP{cur}'
        for c in range(4):
            D = U['ch'][c]
            bu, pu = pp.get()
            I('pe', 'matmul', [D[Yf], D['vb']], [bu], pu, lhsT=D[Yf][:, 0:128], rhs=D['vb'][:], start=True, stop=True)
            bw, pw = pp.get()
            I('pe', 'matmul', [D[Yf], D['kbg']], [bw], pw, lhsT=D['kbg'][:], rhs=D[Yf][:, 0:128], start=True, stop=True)
            I('act', 'copy', [], [bu, D['u']], out=D['u'][:], in_=pu)
            I('dve', 'tensor_copy', [], [bw, D['wT']], out=D['wT'][:], in_=pw)
        yield

    def seq_gen(h, sb, U):
        T = tabs[h]
        bk = seqbank[h]
        pvn, po, psu = bk[:, 0:128], bk[:, 128:256], bk[:, 256:384]
        Sb, Sf = Sbf[h], S32[h]
        for c in range(4):
            D = U['ch'][c]
            n = sb * 4 + c
            I('pe', 'matmul', [D['wT'], Sb], [bk], pvn, lhsT=D['wT'][:], rhs=Sb[:], start=True, stop=True)
            I('dve', 'tensor_tensor', [D['u']], [bk, D['vnew']], out=D['vnew'][:], in0=D['u'][:], in1=pvn, op=ALU.subtract)
            I('pe', 'matmul', [D['qg'], Sb], [bk], po, lhsT=Sb[:], rhs=D['qg'][:], start=True, stop=False)
            I('pe', 'matmul', [D['qkT'], D['vnew']], [bk], po, lhsT=D['vnew'][:], rhs=D['qkT'][:], start=False, stop=True)
            I('pe', 'matmul', [D['kdec'], D['vnew']], [bk], psu, lhsT=D['kdec'][:], rhs=D['vnew'][:], start=True, stop=True)
            I('dve', 'scalar_tensor_tensor', [Sf, T['glc']], [bk, Sf], out=Sf[:], in0=Sf[:], scalar=T['glc'][:, n:n + 1], in1=psu,
              op0=ALU.mult, op1=ALU.add)
            I('pool', 'tensor_copy', [Sf], [Sb], out=Sb[:], in_=Sf[:])
            I('dve', 'tensor_copy', [], [bk, U['ost']], out=U['ost'][:, c, :], in_=po)
            yield
        t0 = sb * SB
        r0 = (h * 4 + sb // 8) * 128
        c0 = (sb % 8) * SB
        s.dma('sp', O2.h[r0:r0 + 128, c0:c0 + SB], U['ost'][:].rearrange("p c e -> p (c e)"), reads=[U['ost']], writes=[O2])
        if after_block is not None and sb % 8 == 7:
            after_block(h * 4 + sb // 8)
        yield

    def run_rr(gens):
        gens = list(gens)
        while gens:
            for g in list(gens):
                try:
                    next(g)
                except StopIteration:
                    gens.remove(g)

    units = [(h, sb) for sb in range(NSB) for h in range(NH)]
    NU = len(units)

    def uset(u):
        return sets[u % NSET]

    run_rr([stageA_gen(units[0][0], units[0][1], uset(0), uset(-2))])
    for u in range(NU):
        h, sb = units[u]
        gens = [prep_gen(h, sb, uset(u))]
        if u + 1 < NU:
            h2, sb2 = units[u + 1]
            gens.append(stageA_gen(h2, sb2, uset(u + 1), uset(u - 1)))
        if u >= 1:
            hp_, sbp = units[u - 1]
            gens.append(seq_gen(hp_, sbp, uset(u - 1)))
        run_rr(gens)
    hl, sbl = units[NU - 1]
    run_rr([seq_gen(hl, sbl, uset(NU - 1))])


def rstd_from_ps(s, ss_ps_bank, ss_ap, rstd, D):
    I = s.I
    I('dve', 'tensor_scalar', [], [ss_ps_bank, rstd], out=rstd[:], in0=ss_ap, scalar1=1.0 / D, scalar2=EPS, op0=ALU.mult, op1=ALU.add)
    I('dve', 'reciprocal', [rstd], [rstd], out=rstd[:], in_=rstd[:])
    I('act', 'activation', [rstd], [rstd], out=rstd[:], in_=rstd[:], func=AF.Sqrt)


def phase_post(s, C, mode, Gin, idx_base, Z1, xsrc, xsrc_buf, gt_off, og_d, w_out_d, XO, lam_init=0.0, NT=4096, TB=512):
    I = s.I
    modv = C['modv']
    idx = C['idx']
    gv = Gin.h[:, :].rearrange("r (b w) -> (r b) w", w=TB)
    ncol = 1 if mode == 'gdn' else 2
    og = s.sb('og', [128, ncol], F32)
    s.dma('sp', og[:], og_d, writes=[og])
    if mode == 'attn':
        I('dve', 'tensor_scalar', [og], [og], out=og[:], in0=og[:], scalar1=1.0 - lam_init, scalar2=None, op0=ALU.mult)
    Wb = s.sb('Wb', [128, 8, 1024], BF16)
    load_cast_w(s, Wb, w_out_d)
    os_ = [s.sb('o', [128, 8, TB], F32) for _ in range(2)]
    zs = [s.sb('z', [128, 8, TB], F32) for _ in range(2)] if mode == 'gdn' else None
    xs = [s.sb('x', [128, 8, TB], F32) for _ in range(2)]
    sq = s.sb('sq', [128, 8, TB], BF16)
    hb = [s.sb('hb', [128, 8, TB], BF16) for _ in range(2)]
    rstd = [s.sb('rstd', [128, TB], F32) for _ in range(2)]
    t1 = [s.sb('t1', [128, TB], F32) for _ in range(2)]
    t2 = [s.sb('t2', [128, TB], F32) for _ in range(2)]
    ssb = [s.ps('ss', [128, 512]) for _ in range(2)]
    pps = [s.ps('pp', [128, 512]) for _ in range(4)]
    outs = [s.sb('ost', [128, TB], F32) for _ in range(4)]
    xv = xsrc.rearrange("(kc p) t -> p kc t", p=128)
    zv = Z1.h[:, :].rearrange("(kc p) t -> p kc t", p=128) if mode == 'gdn' else None
    it = 0
    k2 = 0
    for tb in range(NT // TB):
        tsl = slice(tb * TB, (tb + 1) * TB)
        o = os_[tb % 2]
        x = xs[tb % 2]
        h = hb[tb % 2]
        for kc in range(8):
            col = idx_base + kc * 8 + tb
            s.gather(o[:, kc, :], gv, idx[:, col:col + 1], [Gin, idx], [o])
        s.dma('sp', x[:], xv[:, :, tsl], reads=[xsrc_buf] if xsrc_buf is not None else [], writes=[x])
        if mode == 'gdn':
            z = zs[tb % 2]
            s.dma('sp', z[:], zv[:, :, tsl], reads=[Z1], writes=[z])
        I('act', 'activation', [o], [sq], out=sq[:], in_=o[:], func=AF.Square)
        if mode == 'gdn':
            for kc in range(8):
                sb_ = ssb[k2 % 2]
                r = rstd[k2 % 2]
                a1 = t1[k2 % 2]
                a2 = t2[k2 % 2]
                k2 += 1
                I('pe', 'matmul', [C['ones_bf'], sq], [sb_], sb_[:, 0:TB], lhsT=C['ones_bf'][:], rhs=sq[:, kc, :], start=True, stop=True)
                rstd_from_ps(s, sb_, sb_[:, 0:TB], r, 128)
                I('dve', 'tensor_tensor', [o, r], [a1], out=a1[:], in0=o[:, kc, :], in1=r[:], op=ALU.mult)
                I('act', 'activation', [z], [a2], out=a2[:], in_=z[:, kc, :], func=AF.Silu)
                I('pool', 'tensor_tensor', [a1, a2], [a1], out=a1[:], in0=a1[:], in1=a2[:], op=ALU.mult)
                I('act', 'mul', [a1, og], [h], out=h[:, kc, :], in_=a1[:], mul=og[:, 0:1])
        else:
            for hd in range(4):
                sb_ = ssb[k2 % 2]
                r = rstd[k2 % 2]
                k2 += 1
                for j in range(2):
                    I('pe', 'matmul', [C['ones_bf'], sq], [sb_], sb_[:, 0:TB], lhsT=C['ones_bf'][:], rhs=sq[:, 2 * hd + j, :],
                      start=(j == 0), stop=(j == 1))
                rstd_from_ps(s, sb_, sb_[:, 0:TB], r, 256)
                for j in range(2):
                    I('dve', 'scalar_tensor_tensor', [o, og, r], [h], out=h[:, 2 * hd + j, :], in0=o[:, 2 * hd + j, :],
                      scalar=og[:, j:j + 1], in1=r[:], op0=ALU.mult, op1=ALU.mult)
        for n in range(8):
            pp = pps[it % 4]
            ot = outs[it % 4]
            for kc in range(8):
                I('pe', 'matmul', [Wb, h], [pp], pp[:, 0:TB], lhsT=Wb[:, kc, n * 128:(n + 1) * 128], rhs=h[:, kc, :],
                  start=(kc == 0), stop=(kc == 7))
            I('dve', 'scalar_tensor_tensor', [modv, x], [pp, ot], out=ot[:], in0=pp[:, 0:TB],
              scalar=modv[:, gt_off + n:gt_off + n + 1], in1=x[:, n, :], op0=ALU.mult, op1=ALU.add)
            s.dma('sp', XO.h[n * 128:(n + 1) * 128, tsl], ot[:], reads=[ot], writes=[XO])
            it += 1


def phase_mlp(s, C, XI, moff, gT_d, w1_d, w2_d, xoT_d, XO=None, final_g_d=None, NT=4096, TB=256):
    I = s.I
    modv = C['modv']
    gT = s.sb('gT', [128, 8], F32)
    s.dma('sp', gT[:], gT_d, writes=[gT])
    G = s.sb('G', [128, 8], F32)
    I('dve', 'scalar_tensor_tensor', [modv, gT], [G], out=G[:], in0=modv[:, moff + 8:moff + 16], scalar=1.0, in1=gT[:],
      op0=ALU.add, op1=ALU.mult)
    if final_g_d is not None:
        fg = s.sb('fg', [128, 8], F32)
        s.dma('sp', fg[:], final_g_d, writes=[fg])
    W1 = s.sb('W1', [128, 8, 4096], BF16)
    W2 = s.sb('W2', [128, 32, 1024], BF16)
    load_cast_w(s, W1, w1_d)
    load_cast_w(s, W2, w2_d)
    xs = [b.view("p (k t) -> p k t", t=TB) for b in get_stage(s)]
    assert TB == 256
    sq = s.sb('sq', [128, 8, TB], BF16)
    hs = [s.sb('h', [128, 8, TB], BF16) for _ in range(2)]
    tmp = [s.sb('tmp', [128, TB], F32) for _ in range(2)]
    rstd = s.sb('rstd', [128, TB], F32)
    hid = s.sb('hid', [128, 32, TB], BF16)
    rl = [s.sb('rl', [128, TB], F32) for _ in range(2)]
    xo = [s.sb('xo', [128, 8, TB], F32) for _ in range(1)]
    ss_ps = s.ps('ss', [128, 512])
    pps = [s.ps('pp', [128, 512]) for _ in range(6)]
    xv = XI.h[:, :].rearrange("(kc p) t -> p kc t", p=128)
    ov = xoT_d.rearrange("(kc p) t -> p kc t", p=128)
    owr = [XO] if XO is not None else []
    it = 0
    for tb in range(NT // TB):
        tsl = slice(tb * TB, (tb + 1) * TB)
        x = xs[tb % 2]
        h = hs[tb % 2]
        xout = xo[0]
        s.dma('sp', x[:], xv[:, :, tsl], reads=[XI], writes=[x])
        norm_mod(s, C, x, (G, 0), (modv, moff), h, tmp, sq, ss_ps, rstd, TB=TB)
        for f in range(32):
            pp = pps[it % 6]
            r = rl[it % 2]
            it += 1
            for kc in range(8):
                I('pe', 'matmul', [W1, h], [pp], pp[:, 0:TB], lhsT=W1[:, kc, f * 128:(f + 1) * 128], rhs=h[:, kc, :],
                  start=(kc == 0), stop=(kc == 7))
            I('act', 'activation', [], [pp, r], out=r[:], in_=pp[:, 0:TB], func=AF.Relu)
            I('pool', 'tensor_tensor', [r], [hid], out=hid[:, f, :], in0=r[:], in1=r[:], op=ALU.mult)
        for n in range(8):
            pp = pps[it % 6]
            it += 1
            for f in range(32):
                I('pe', 'matmul', [W2, hid], [pp], pp[:, 0:TB], lhsT=W2[:, f, n * 128:(n + 1) * 128], rhs=hid[:, f, :],
                  start=(f == 0), stop=(f == 31))
            I('dve', 'scalar_tensor_tensor', [modv, x], [pp, xout], out=xout[:, n, :], in0=pp[:, 0:TB],
              scalar=modv[:, moff + 16 + n:moff + 17 + n], in1=x[:, n, :], op0=ALU.mult, op1=ALU.add)
        if final_g_d is None:
            s.dma('sp', ov[:, :, tsl], xout[:], reads=[xout], writes=owr)
        else:
            I('act', 'activation', [xout], [sq], out=sq[:], in_=xout[:], func=AF.Square)
            for kc in range(8):
                I('pe', 'matmul', [C['ones_bf'], sq], [ss_ps], ss_ps[:, 0:TB], lhsT=C['ones_bf'][:], rhs=sq[:, kc, :],
                  start=(kc == 0), stop=(kc == 7))
            rstd_from_ps(s, ss_ps, ss_ps[:, 0:TB], rstd, 1024)
            for kc in range(8):
                I('dve', 'scalar_tensor_tensor', [xout, fg, rstd], [xout], out=xout[:, kc, :], in0=xout[:, kc, :],
                  scalar=fg[:, kc:kc + 1], in1=rstd[:], op0=ALU.mult, op1=ALU.mult)
            s.dma('sp', ov[:, :, tsl], xout[:], reads=[xout], writes=owr)


def phase_kvq(s, C, XI, gkvT_d, gqT_d, kv_w_d, kv_wp_d, wq_d, wqp_d, ropeC_d, ropeS_d, Q3, K3, V3, after_tb=None, NT=4096, TB=512):
    I = s.I
    modv = C['modv']
    gkv = s.sb('gkv', [128, 8], F32)
    s.dma('sp', gkv[:], gkvT_d, writes=[gkv])
    gq = s.sb('gq', [128, 8], F32)
    s.dma('sp', gq[:], gqT_d, writes=[gq])
    Gkv = s.sb('Gkv', [128, 8], F32)
    Gq = s.sb('Gq', [128, 8], F32)
    I('dve', 'scalar_tensor_tensor', [modv, gkv], [Gkv], out=Gkv[:], in0=modv[:, 104:112], scalar=1.0, in1=gkv[:], op0=ALU.add, op1=ALU.mult)
    I('dve', 'scalar_tensor_tensor', [modv, gq], [Gq], out=Gq[:], in0=modv[:, 56:64], scalar=1.0, in1=gq[:], op0=ALU.add, op1=ALU.mult)
    Wkv = s.sb('Wkv', [128, 8, 2048], BF16)
    Wkp = s.sb('Wkp', [128, 8, 1024], BF16)
    Wq = s.sb('Wq', [128, 8, 1024], BF16)
    Wqp = s.sb('Wqp', [128, 8, 1024], BF16)
    load_cast_w(s, Wkv, kv_w_d)
    load_cast_w(s, Wkp, kv_wp_d)
    load_cast_w(s, Wq, wq_d)
    load_cast_w(s, Wqp, wqp_d)
    xs = [s.sb('xs', [128, 8, TB], F32) for _ in range(2)]
    sq = s.sb('sq', [128, 8, TB], BF16)
    hkv = s.sb('hkv', [128, 8, TB], BF16)
    hq = s.sb('hq', [128, 8, TB], BF16)
    tmp = [s.sb('tmp', [128, TB], F32) for _ in range(2)]
    rstd = s.sb('rstd', [128, TB], F32)
    rc = [s.sb('rc', [128, TB], F32) for _ in range(2)]
    rs = [s.sb('rs', [128, TB], F32) for _ in range(2)]
    ta = [s.sb('ta', [128, TB], F32) for _ in range(3)]
    tb_ = [s.sb('tb', [128, TB], F32) for _ in range(3)]
    outs = [s.sb('ost', [128, TB], BF16) for _ in range(4)]
    vouts = [s.sb('vost', [128, 256], BF16) for _ in range(4)]
    ss_ps = s.ps('ss', [128, 512])
    pps = [s.ps('pp', [128, 512]) for _ in range(6)]
    xv = XI.h[:, :].rearrange("(kc p) t -> p kc t", p=128)
    it = 0
    io = 0
    for tb in range(NT // TB):
        tsl = slice(tb * TB, (tb + 1) * TB)
        x = xs[tb % 2]
        cc = rc[tb % 2]
        sn = rs[tb % 2]
        s.dma('sp', x[:], xv[:, :, tsl], reads=[XI], writes=[x])
        s.dma('sp', cc[:], ropeC_d[:, tsl], writes=[cc])
        s.dma('sp', sn[:], ropeS_d[:, tsl], writes=[sn])
        norm_mod(s, C, x, (Gkv, 0), (modv, 96), hkv, tmp, sq, ss_ps, rstd, TB=TB)
        norm_mod(s, C, x, (Gq, 0), (modv, 48), hq, tmp, sq, ss_ps, rstd, TB=TB)
        for (W, Wp, hh, dst) in ((Wkv, Wkp, hkv, K3), (Wq, Wqp, hq, Q3)):
            for n in range(8):
                pa = pps[it % 6]
                it += 1
                for kc in range(8):
                    I('pe', 'matmul', [W, hh], [pa], pa[:, 0:TB], lhsT=W[:, kc, n * 128:(n + 1) * 128], rhs=hh[:, kc, :],
                      start=(kc == 0), stop=(kc == 7))
                pb = pps[it % 6]
                it += 1
                for kc in range(8):
                    I('pe', 'matmul', [Wp, hh], [pb], pb[0:32, 0:TB], lhsT=Wp[:, kc, n * 128:n * 128 + 32], rhs=hh[:, kc, :],
                      start=(kc == 0), stop=(kc == 7))
                ot = outs[io % 4]
                a = ta[io % 3]
                b = tb_[io % 3]
                I('dve', 'tensor_tensor', [cc], [pa, a], out=a[:], in0=pa[:, 0:TB], in1=cc[:], op=ALU.mult)
                I('dve', 'tensor_tensor', [sn], [pb, b], out=b[0:32, :], in0=pb[0:32, 0:TB], in1=sn[0:32, :], op=ALU.mult)
                I('pool', 'tensor_tensor', [a, b], [a], out=a[0:32, :], in0=a[0:32, :], in1=b[0:32, :], op=ALU.add)
                I('act', 'copy', [a], [ot], out=ot[:], in_=a[:])
                s.dma('sp', dst.h[tb * 1024 + n * 128:tb * 1024 + (n + 1) * 128, :], ot[:], reads=[ot], writes=[dst])
                io += 1
        for tsb in range(TB // 128):
            for hd in range(4):
                pa = pps[it % 6]
                it += 1
                for kc in range(8):
                    I('pe', 'matmul', [Wkv, hkv], [pa], pa[:, 0:256], lhsT=hkv[:, kc, tsb * 128:(tsb + 1) * 128],
                      rhs=Wkv[:, kc, 1024 + hd * 256:1024 + (hd + 1) * 256], start=(kc == 0), stop=(kc == 7))
                vo = vouts[io % 4]
                io += 1
                I('act', 'copy', [], [pa, vo], out=vo[:], in_=pa[:, 0:256])
                r0 = (tb * 4 + hd) * TB + tsb * 128
                s.dma('sp', V3.h[r0:r0 + 128, :], vo[:], reads=[vo], writes=[V3])
        if after_tb is not None:
            after_tb(tb)


def phase_attn(s, C, G3q, G3k, G3v, lamT_d, O4, lam_init, S=16384, IDX_QK=260, IDX_V=324, after_block=None):
    I = s.I
    scale = 128 ** -0.5
    NKB = S // 128
    NQB = S // 512
    ones_bf = C['ones_bf']
    ones_f = s.sb('ones_f', [128, 128], F32)
    I('pool', 'memset', [], [ones_f], ones_f[:], 1.0)
    tri = C['tri']
    idx = C['idx']
    ident = C['ident']
    gq = G3q.h[:, :]
    gk = G3k.h[:, :]
    misc = s.ps('misc', [128, 512])
    lp = s.sb('lp', [128, 4], F32)
    s.dma('sp', lp[:], lamT_d, writes=[lp])
    pr = s.sb('pr', [128, 2], F32)
    I('dve', 'tensor_tensor', [lp], [pr], out=pr[:, 0:1], in0=lp[:, 0:1], in1=lp[:, 1:2], op=ALU.mult)
    I('dve', 'tensor_tensor', [lp], [pr], out=pr[:, 1:2], in0=lp[:, 2:3], in1=lp[:, 3:4], op=ALU.mult)
    I('pe', 'matmul', [ones_f, pr], [misc], misc[:, 0:2], lhsT=ones_f[:], rhs=pr[:], start=True, stop=True)
    ex = s.sb('ex', [128, 2], F32)
    I('act', 'activation', [], [misc, ex], out=ex[:], in_=misc[:, 0:2], func=AF.Exp)
    nlam = s.sb('nlam', [128, 1], F32)
    I('dve', 'tensor_tensor', [ex], [nlam], out=nlam[:], in0=ex[:, 1:2], in1=ex[:, 0:1], op=ALU.subtract)
    I('dve', 'tensor_scalar', [nlam], [nlam], out=nlam[:], in0=nlam[:], scalar1=-lam_init, scalar2=None, op0=ALU.add)
    KbB = [[s.sb(f'Kb{m}_{j}', [128, 512], BF16) for j in range(NQB)] for m in range(2)]
    VbB = [s.sb(f'Vb{j}', [128, 4, 257], BF16) for j in range(NQB)]
    sqs = [s.sb('sqs', [128, 512], BF16) for _ in range(2)]
    kmx = [s.sb(f'kmx{m}', [128, NQB], F32) for m in range(2)]
    krun = [s.sb(f'krun{m}', [128, NQB], F32) for m in range(2)]
    lb_i = [0]

    def load_block(j):
        for m in range(2):
            sq = sqs[lb_i[0] % 2]
            lb_i[0] += 1
            col = IDX_QK + m * 32 + j
            kb_ = KbB[m][j]
            s.gather(kb_[:], gk, idx[:, col:col + 1], [G3k, idx], [kb_])
            I('act', 'activation', [kb_], [sq], out=sq[:], in_=kb_[:], func=AF.Square)
            I('pe', 'matmul', [ones_bf, sq], [misc], misc[:, 0:512], lhsT=ones_bf[:], rhs=sq[:], start=True, stop=True)
            I('dve', 'reduce_max', [], [misc, kmx[m]], out=kmx[m][:, j:j + 1], in_=misc[:, 0:512], axis=AX.X)
            if j == 0:
                I('dve', 'tensor_copy', [kmx[m]], [krun[m]], out=krun[m][:, 0:1], in_=kmx[m][:, 0:1])
            else:
                I('dve', 'tensor_tensor', [kmx[m], krun[m]], [krun[m]], out=krun[m][:, j:j + 1], in0=krun[m][:, j - 1:j],
                  in1=kmx[m][:, j:j + 1], op=ALU.max)
        vb_ = VbB[j]
        I('pool', 'memset', [], [vb_], vb_[:, :, 256:257], 1.0)
        for q4 in range(4):
            kb = 4 * j + q4
            s.gather(vb_[:, q4, 0:256], G3v.h[:, :], idx[:, IDX_V + kb:IDX_V + kb + 1], [G3v, idx], [vb_])

    load_block(0)
    acc = [s.ps(f'acc{j}', [128, 512]) for j in range(4)]
    stp = [s.ps(f'st{j}', [128, 512]) for j in range(3)]
    qbf = [s.sb('qbf', [128, 512], BF16) for _ in range(2)]
    PT = [s.sb('PT', [128, 512], BF16) for _ in range(3)]
    mq = [s.sb('mq', [128, 1], F32) for _ in range(2)]
    bias = [s.sb('bias', [128, 1], F32) for _ in range(2)]
    o1 = s.sb('o1', [128, 4, 256], F32)
    rl = [s.sb('rl', [128, 1], F32) for _ in range(4)]
    ot = [s.sb('ot', [128, 256], F32) for _ in range(4)]
    oT_st = [s.sb('oTs', [128, 256], F32) for _ in range(2)]
    it = 0
    qi = 0
    for qb in range(NQB):
        if qb + 1 < NQB:
            load_block(qb + 1)
        for m in range(2):
            qb_ = qbf[qi % 2]
            sq = sqs[qi % 2]
            mq_ = mq[qi % 2]
            bi = bias[qi % 2]
            qi += 1
            col = IDX_QK + m * 32 + qb
            s.gather(qb_[:], gq, idx[:, col:col + 1], [G3q, idx], [qb_])
            I('act', 'activation', [qb_], [sq], out=sq[:], in_=qb_[:], func=AF.Square)
            I('pe', 'matmul', [ones_bf, sq], [misc], misc[:, 0:512], lhsT=ones_bf[:], rhs=sq[:], start=True, stop=True)
            I('dve', 'reduce_max', [], [misc, mq_], out=mq_[:], in_=misc[:, 0:512], axis=AX.X)
            I('dve', 'tensor_tensor', [mq_, krun[m]], [mq_], out=mq_[:], in0=mq_[:], in1=krun[m][:, qb:qb + 1], op=ALU.mult)
            I('act', 'activation', [mq_], [mq_], out=mq_[:], in_=mq_[:], func=AF.Sqrt)
            I('dve', 'tensor_scalar', [mq_], [bi], out=bi[:], in0=mq_[:], scalar1=-scale, scalar2=None, op0=ALU.mult)
            nkb = 4 * qb + 4
            LOOK = 2
            slots = {}

            def issue_st(kb):
                nonlocal it
                r = kb - 4 * qb
                c0 = 128 * max(r, 0)
                sp_ = stp[it % 3]
                pt = PT[it % 3]
                it += 1
                slots[kb] = (sp_, pt, r, c0)
                kbuf = KbB[m][kb // 4]
                I('pe', 'matmul', [kbuf, qb_], [sp_], sp_[:, c0:512], lhsT=kbuf[:, (kb % 4) * 128:(kb % 4 + 1) * 128], rhs=qb_[:, c0:512],
                  start=True, stop=True)

            for kb in range(min(LOOK, nkb)):
                issue_st(kb)
            for kb in range(nkb):
                sp_, pt, r, c0 = slots.pop(kb)
                I('act', 'activation', [bi], [sp_, pt], out=pt[:, c0:512], in_=sp_[:, c0:512], func=AF.Exp, bias=bi[:, 0:1], scale=scale)
                if r >= 0:
                    I('pool', 'tensor_tensor', [pt, tri], [pt], out=pt[:, c0:c0 + 128], in0=pt[:, c0:c0 + 128], in1=tri[:], op=ALU.mult)
                if kb + LOOK < nkb:
                    issue_st(kb + LOOK)
                for j in range(max(r, 0), 4):
                    vbuf = VbB[kb // 4]
                    I('pe', 'matmul', [pt, vbuf], [acc[j]], acc[j][:, 0:257], lhsT=pt[:, 128 * j:128 * (j + 1)], rhs=vbuf[:, kb % 4, :],
                      start=(kb == 0), stop=(kb == 4 * qb + j))
            for j in range(4):
                a = acc[j]
                I('dve', 'reciprocal', [], [a, rl[j]], out=rl[j][:], in_=a[:, 256:257])
                if m == 0:
                    I('act', 'mul', [rl[j]], [a, o1], out=o1[:, j, :], in_=a[:, 0:256], mul=rl[j][:, 0:1])
                else:
                    I('dve', 'tensor_tensor', [rl[j], nlam], [rl[j]], out=rl[j][:], in0=rl[j][:], in1=nlam[:], op=ALU.mult)
                    I('dve', 'scalar_tensor_tensor', [rl[j], o1], [a, ot[j]], out=ot[j][:], in0=a[:, 0:256], scalar=rl[j][:, 0:1],
                      in1=o1[:, j, :], op0=ALU.mult, op1=ALU.add)
                    for cch in range(2):
                        I('pe', 'transpose', [ot[j], ident], [misc], misc[:, cch * 128:(cch + 1) * 128],
                          ot[j][:, cch * 128:(cch + 1) * 128], ident[:])
                    oTs = oT_st[(qb * 4 + j) % 2]
                    I('act', 'copy', [], [misc, oTs], out=oTs[:], in_=misc[:, 0:256])
                    tq = qb // 8
                    c0 = (qb % 8) * 512 + j * 128
                    for cch in range(2):
                        r0 = tq * 256 + cch * 128
                        s.dma('sp', O4.h[r0:r0 + 128, c0:c0 + 128], oTs[:, cch * 128:(cch + 1) * 128], reads=[oTs], writes=[O4])
                    if after_block is not None and qb % 8 == 7 and j == 3:
                        after_block(qb // 8)

import math
from concourse.bass_utils import run_bass_kernel_spmd

NCORES = 8
SEQ = 16384
TPC = 4096
LAM_INIT1 = 0.8 - 0.6 * math.exp(-0.3 * 1)
ROT_DIM = 32
ROPE_THETA = 500000.0
NIDX = 516
IDX_G1, IDX_AB, IDX_G2, IDX_QK, IDX_V, IDX_G4 = 0, 192, 196, 260, 324, 452
GROUPS = [[0, 1, 2, 3], [4, 5, 6, 7]]

IN_SPECS = {
    "xT": ((1024, TPC), F32), "cT": ((128, 8, 1), F32), "w0": ((1024, 6144), F32), "w1": ((1024, 6144), F32),
    "w2": ((1024, 2048), F32), "mb": ((128, 112), F32), "gT0": ((128, 8), F32), "w_in": ((1024, 4112), F32),
    "cw": ((128, 24), F32), "hp": ((128, 4), F32), "ident": ((128, 128), F32), "negTi": ((128, 128), F32),
    "negTs": ((128, 128), F32), "tri": ((128, 128), F32), "idx": ((128, NIDX), U32), "og_a": ((128, 1), F32),
    "a_w_out": ((1024, 1024), F32), "gm0": ((128, 8), F32), "m1_0": ((1024, 4096), F32), "m2_0": ((4096, 1024), F32),
    "gkv": ((128, 8), F32), "gq": ((128, 8), F32), "kv_w": ((1024, 2048), F32), "kv_wp": ((1024, 1024), F32),
    "wq": ((1024, 1024), F32), "wqp": ((1024, 1024), F32), "ropeC": ((128, TPC), F32), "ropeS": ((128, TPC), F32),
    "lamT": ((128, 4), F32), "og_b": ((128, 2), F32), "b_w_out": ((1024, 1024), F32), "gm1": ((128, 8), F32),
    "m1_1": ((1024, 4096), F32), "m2_1": ((4096, 1024), F32), "fg": ((128, 8), F32),
}


def build_program(stop_after=99):
    nc = bass.Bass("TRN2", target_bir_lowering=False)
    d = {k: nc.dram_tensor(k, list(shp), dt, kind="ExternalInput").ap() for k, (shp, dt) in IN_SPECS.items()}
    outT = nc.dram_tensor("outT", [1024, TPC], F32, kind="ExternalOutput").ap()
    s = Sched(nc)

    def scratch(name, shape, dt=F32):
        return s.dram(nc.dram_tensor(name, list(shape), dt), name)

    P1 = scratch("P1", (8 * 3136, 512))
    Z1 = scratch("Z1", (1024, TPC))
    G1 = scratch("G1", (4 * 8 * 3136, 512))
    O2 = scratch("O2", (1024, TPC))
    G2 = scratch("G2", (4096, TPC))
    XM = scratch("XM", (1024, TPC))
    X1 = scratch("X1", (1024, TPC))
    Q3 = scratch("Q3", (8 * 1024, 512), BF16)
    K3 = scratch("K3", (8 * 1024, 512), BF16)
    V3 = scratch("V3", (4 * TPC, 256), BF16)
    G3q = scratch("G3q", (4 * 8 * 1024, 512), BF16)
    G3k = scratch("G3k", (4 * 8 * 1024, 512), BF16)
    G3v = scratch("G3v", (16 * TPC, 256), BF16)
    O4 = scratch("O4", (1024, TPC))
    G4 = scratch("G4", (4096, TPC))
    XM2 = scratch("XM2", (1024, TPC))

    C = consts(s)
    gdn_consts(s, C, d["ident"], d["negTi"], d["negTs"])
    trif = s.sb('trif', [128, 128], F32)
    s.dma('sp', trif[:], d["tri"], writes=[trif])
    C['tri'] = s.sb('tri', [128, 128], BF16)
    s.I('dve', 'tensor_copy', [trif], [C['tri']], out=C['tri'][:], in_=trif[:])
    C['idx'] = s.sb('idx', [128, NIDX], U32)
    s.dma('sp', C['idx'][:], d["idx"], writes=[C['idx']])
    C['modv'] = s.sb('modvg', [128, 112], F32)

    def ag(src, dst):
        R, Fc = src.h.shape
        esz = 2 if src.h.dtype == BF16 else 4
        RS = (1 << 20) // (Fc * esz)
        assert R % RS == 0
        for k in range(R // RS):
            ag_slice(src, dst, k, RS)

    def ag_slice(src, dst, k, RS):
        s.allgather(src.h[k * RS:(k + 1) * RS, :], dst.h[k * 4 * RS:(k + 1) * 4 * RS, :], src, dst, GROUPS)

    def p1_after_tb(tb):
        lo = ((tb * 3136 + 511) // 512) if tb > 0 else 0
        for k in range(49):
            if (512 * k + 511) // 3136 == tb:
                ag_slice(P1, G1, k, 512)

    def o2_after_block(blk):
        for k in (2 * blk, 2 * blk + 1):
            ag_slice(O2, G2, k, 64)

    def kvq_after_tb(tb):
        ag_slice(Q3, G3q, tb, 1024)
        ag_slice(K3, G3k, tb, 1024)
        ag_slice(V3, G3v, tb, 2048)

    def o4_after_block(tq):
        for k in range(4 * tq, 4 * tq + 4):
            ag_slice(O4, G4, k, 64)

    s.begin_phase()
    phase_mod(s, C, d["cT"], [d["w0"], d["w1"], d["w2"]], d["mb"])
    phase_l1(s, C, d["xT"], d["gT0"], d["w_in"], P1, Z1, after_tb=p1_after_tb)
    s.end_phase()
    if stop_after == 2:
        s.begin_phase()
        s.end_phase(final=True)
        return nc
    s.begin_phase()
    phase_gdn(s, C, G1, d["cw"], d["hp"], O2, S=SEQ, IDX_G1=IDX_G1, IDX_AB=IDX_AB, after_block=o2_after_block)
    s.end_phase()
    if stop_after == 3:
        s.begin_phase()
        s.end_phase(final=True)
        return nc
    s.begin_phase()
    phase_post(s, C, 'gdn', G2, IDX_G2, Z1, d["xT"], None, 16, d["og_a"], d["a_w_out"], XM)
    s.end_phase()
    if stop_after == 4:
        s.begin_phase()
        s.end_phase(final=True)
        return nc
    s.begin_phase()
    phase_mlp(s, C, XM, 24, d["gm0"], d["m1_0"], d["m2_0"], X1.h[:, :], XO=X1)
    s.end_phase()
    if stop_after == 5:
        s.begin_phase()
        s.end_phase(final=True)
        return nc
    s.begin_phase()
    phase_kvq(s, C, X1, d["gkv"], d["gq"], d["kv_w"], d["kv_wp"], d["wq"], d["wqp"], d["ropeC"], d["ropeS"], Q3, K3, V3, after_tb=kvq_after_tb)
    s.end_phase()
    if stop_after == 6:
        s.begin_phase()
        s.end_phase(final=True)
        return nc
    s.begin_phase()
    phase_attn(s, C, G3q, G3k, G3v, d["lamT"], O4, LAM_INIT1, S=SEQ, IDX_QK=IDX_QK, IDX_V=IDX_V, after_block=o4_after_block)
    s.end_phase()
    if stop_after == 7:
        s.begin_phase()
        s.end_phase(final=True)
        return nc
    s.begin_phase()
    phase_post(s, C, 'attn', G4, IDX_G4, None, X1.h[:, :], X1, 64, d["og_b"], d["b_w_out"], XM2, lam_init=LAM_INIT1)
    s.end_phase()
    if stop_after == 8:
        s.begin_phase()
        s.end_phase(final=True)
        return nc
    s.begin_phase()
    phase_mlp(s, C, XM2, 72, d["gm1"], d["m1_1"], d["m2_1"], outT, final_g_d=d["fg"])
    s.end_phase(final=True)
    return nc


def _fm8(v):
    return np.ascontiguousarray(np.asarray(v, np.float32).reshape(-1, 128).T)


def _perm_cols(w):
    w = np.asarray(w)
    out = w.copy()
    for c in range(w.shape[1] // 128):
        b = c * 128
        out[:, b:b + 16] = w[:, b + 16:b + 32]
        out[:, b + 16:b + 32] = w[:, b:b + 16]
    return out


def _rope_tables(t0, n):
    pos = (t0 + np.arange(n)).astype(np.float32)
    inv = (np.float32(ROPE_THETA) ** (-(np.arange(0, ROT_DIM, 2, dtype=np.float32)) / np.float32(ROT_DIM))).astype(np.float32)
    fr = (pos[:, None] * inv[None, :]).astype(np.float32)
    cos = np.cos(fr).astype(np.float32).T
    sin = np.sin(fr).astype(np.float32).T
    Cc = np.ones((128, n), np.float32)
    Ss = np.zeros((128, n), np.float32)
    Cc[0:16] = cos
    Cc[16:32] = cos
    Ss[0:16] = -sin
    Ss[16:32] = sin
    return Cc, Ss


def _grow(src, row, RS):
    return ((row // RS) * 4 + src) * RS + row % RS


def _idx_table(r):
    p = np.arange(128, dtype=np.int64)
    t = np.zeros((128, NIDX), np.int64)
    for hh in range(2):
        h = 2 * r + hh
        for X in range(3):
            for sb in range(32):
                t[:, IDX_G1 + (hh * 3 + X) * 32 + sb] = _grow(sb // 8, (sb % 8) * 3136 + X * 1024 + h * 128 + p, 512)
        t[:, IDX_AB + 2 * hh] = _grow(p // 32, ((p % 32) // 4) * 3136 + 3072 + h, 512) * 4 + p % 4
        t[:, IDX_AB + 2 * hh + 1] = _grow(p // 32, ((p % 32) // 4) * 3136 + 3080 + h, 512) * 4 + p % 4
    for kc in range(8):
        for tb in range(8):
            t[:, IDX_G2 + kc * 8 + tb] = _grow(kc // 2, ((kc % 2) * 4 + r) * 128 + p, 64) * 8 + tb
            t[:, IDX_G4 + kc * 8 + tb] = _grow(kc // 2, r * 256 + (kc % 2) * 128 + p, 64) * 8 + tb
    for m in range(2):
        for j in range(32):
            t[:, IDX_QK + m * 32 + j] = _grow(j // 8, (j % 8) * 1024 + r * 256 + m * 128 + p, 1024)
    for kb in range(128):
        t[:, IDX_V + kb] = _grow(kb // 32, (((kb % 32) // 4) * 4 + r) * 512 + ((kb % 32) % 4) * 128 + p, 2048)
    return t.astype(np.uint32)


def kernel(x, c, mod_w, mod_b, norm_mix_g, norm_mlp_g, a_w_in, a_conv_w, a_log, a_dt_bias, a_out_norm_g, a_w_out,
           kv_mod_w, kv_mod_b, kv_norm_g, kv_w, b_w_q, b_lambda, b_subln_g, b_w_out, mlp_w1, mlp_w2, final_g):
    f32 = np.float32
    x = np.asarray(x, f32)
    B, S, D = x.shape
    A = lambda v: np.asarray(v, f32)
    jj, ii = np.meshgrid(np.arange(128), np.arange(128), indexing='ij')
    shared = {
        "w0": A(mod_w[0]), "w1": A(mod_w[1]), "w2": A(kv_mod_w),
        "mb": np.ascontiguousarray(np.concatenate([A(mod_b[0]), A(mod_b[1]), A(kv_mod_b)]).reshape(112, 128).T),
        "gT0": _fm8(norm_mix_g[0]), "w_in": A(a_w_in[0]), "ident": np.eye(128, dtype=f32),
        "negTi": np.where(ii >= jj, 0.0, -30000.0).astype(f32), "negTs": np.where(ii > jj, 0.0, -30000.0).astype(f32),
        "tri": (np.arange(128)[:, None] <= np.arange(128)[None, :]).astype(f32),
        "og_a": A(a_out_norm_g[0]).reshape(128, 1), "a_w_out": A(a_w_out[0]), "gm0": _fm8(norm_mlp_g[0]),
        "m1_0": A(mlp_w1[0]), "m2_0": A(mlp_w2[0]), "gkv": _fm8(kv_norm_g), "gq": _fm8(norm_mix_g[1]),
        "kv_w": A(kv_w), "kv_wp": _perm_cols(A(kv_w)[:, :1024]), "wq": A(b_w_q[0]), "wqp": _perm_cols(A(b_w_q[0])),
        "lamT": np.ascontiguousarray(A(b_lambda[0]).T), "og_b": np.ascontiguousarray(A(b_subln_g[0]).reshape(2, 128).T),
        "b_w_out": A(b_w_out[0]), "gm1": _fm8(norm_mlp_g[1]), "m1_1": A(mlp_w1[1]), "m2_1": A(mlp_w2[1]), "fg": _fm8(final_g),
    }
    maps = []
    for cc in range(NCORES):
        b, r = cc // 4, cc % 4
        t0 = r * TPC
        m = dict(shared)
        m["xT"] = np.ascontiguousarray(x[b, t0:t0 + TPC].T)
        m["cT"] = np.ascontiguousarray(A(c)[b].reshape(8, 128).T.reshape(128, 8, 1))
        cw = np.empty((128, 24), f32)
        hp = np.empty((128, 4), f32)
        for hh in range(2):
            h = 2 * r + hh
            for X in range(3):
                r0 = X * 1024 + h * 128
                for j in range(4):
                    cw[:, (hh * 3 + X) * 4 + j] = A(a_conv_w[0])[j, r0:r0 + 128]
            hp[:, 2 * hh] = A(a_log[0])[h]
            hp[:, 2 * hh + 1] = A(a_dt_bias[0])[h]
        m["cw"], m["hp"] = cw, hp
        m["ropeC"], m["ropeS"] = _rope_tables(t0, TPC)
        m["idx"] = _idx_table(r)
        maps.append({k: np.ascontiguousarray(m[k]) for k in IN_SPECS})
    nc = build_program()
    res = run_bass_kernel_spmd(nc, maps, core_ids=list(range(NCORES)))
    out = np.empty((B, S, D), f32)
    for cc in range(NCORES):
        b, r = cc // 4, cc % 4
        out[b, r * TPC:(r + 1) * TPC] = np.asarray(res.results[cc]["outT"]).T
    return out
```
